# Optimizing a Trainium2 kernel written in Bass

```python
import math
import jax, jax.numpy as jnp
from jax import lax
import numpy as np

D_MODEL = 1024
BATCH = 8
SEQ = 2048
DEPTH = 1

SSM_WIDTH = D_MODEL // 2
SSM_GROUP = 16
SSM_GROUPS = SSM_WIDTH // SSM_GROUP
SSM_STATE = 64
ATTN_HEADS = 8
ATTN_HEAD_DIM = 64
ATTN_WIDTH = ATTN_HEADS * ATTN_HEAD_DIM
Q_BLOCK = 128
LN_EPS = 1e-5
DEEPNORM_ALPHA = (2.0 * DEPTH) ** 0.25
DEEPNORM_BETA = (8.0 * DEPTH) ** -0.25
IN_SPLITS = (SSM_WIDTH, SSM_WIDTH, ATTN_WIDTH, ATTN_WIDTH, ATTN_WIDTH, ATTN_HEADS,
             ATTN_WIDTH, D_MODEL, D_MODEL)
IN_COLS = sum(IN_SPLITS)

kernel_name = "hybrid_s5_fox_gated_deepnorm"


def _layer_norm(x, g=None, b=None):
    xf = x.astype(jnp.float32)
    mu = jnp.mean(xf, axis=-1, keepdims=True)
    var = jnp.mean(jnp.square(xf - mu), axis=-1, keepdims=True)
    y = (xf - mu) * lax.rsqrt(var + LN_EPS)
    if g is not None:
        y = y * g.astype(jnp.float32) + b.astype(jnp.float32)
    return y


def _complex_linear_op(e1, e2):
    a1r, a1i, b1r, b1i = e1
    a2r, a2i, b2r, b2i = e2
    return (a2r * a1r - a2i * a1i,
            a2r * a1i + a2i * a1r,
            a2r * b1r - a2i * b1i + b2r,
            a2r * b1i + a2i * b1r + b2i)


def _s5_mixer(u, lam_re, lam_im, log_dt, b_re, b_im, c_re, c_im, d_skip, w_glu):
    bsz, s_len, _ = u.shape
    uf = u.astype(jnp.float32).reshape(bsz, s_len, SSM_GROUPS, SSM_GROUP)
    lr = lam_re.astype(jnp.float32)
    li = lam_im.astype(jnp.float32)
    dt = jnp.exp(log_dt.astype(jnp.float32))[:, None]
    mag = jnp.exp(lr * dt)
    abr = mag * jnp.cos(li * dt)
    abi = mag * jnp.sin(li * dt)
    den = lr * lr + li * li
    cr = ((abr - 1.0) * lr + abi * li) / den
    ci = (abi * lr - (abr - 1.0) * li) / den
    br = b_re.astype(jnp.float32)
    bi = b_im.astype(jnp.float32)
    bbr = cr[..., None] * br - ci[..., None] * bi
    bbi = cr[..., None] * bi + ci[..., None] * br
    bu_r = jnp.einsum('bsgh,gph->bsgp', uf, bbr)
    bu_i = jnp.einsum('bsgh,gph->bsgp', uf, bbi)
    a_r = jnp.broadcast_to(abr[None, None], (1, s_len, SSM_GROUPS, SSM_STATE))
    a_i = jnp.broadcast_to(abi[None, None], (1, s_len, SSM_GROUPS, SSM_STATE))
    _, _, h_r, h_i = lax.associative_scan(_complex_linear_op, (a_r, a_i, bu_r, bu_i), axis=1)
    y = (jnp.einsum('bsgp,ghp->bsgh', h_r, c_re.astype(jnp.float32))
         - jnp.einsum('bsgp,ghp->bsgh', h_i, c_im.astype(jnp.float32))
         + d_skip.astype(jnp.float32).reshape(SSM_GROUPS, SSM_GROUP) * uf)
    y = jax.nn.gelu(y.reshape(bsz, s_len, SSM_WIDTH))
    glu_a, glu_b = jnp.split(y @ w_glu.astype(jnp.float32), 2, axis=-1)
    return glu_a * jax.nn.sigmoid(glu_b)


def _forgetting_attention(q, k, v, log_f):
    s_len = q.shape[2]
    cum_f = jnp.cumsum(log_f, axis=-1)
    scale = ATTN_HEAD_DIM ** -0.5
    outs = []
    for blk in range(s_len // Q_BLOCK):
        q0 = blk * Q_BLOCK
        end = q0 + Q_BLOCK
        qb = q[:, :, q0:end]
        kb = k[:, :, :end]
        vb = v[:, :, :end]
        logits = (jnp.einsum('bhqd,bhkd->bhqk', qb, kb) * scale
                  + cum_f[:, :, q0:end, None] - cum_f[:, :, None, :end])
        causal = jnp.arange(end)[None, :] <= jnp.arange(q0, end)[:, None]
        logits = jnp.where(causal, logits, -jnp.inf)
        p = jax.nn.softmax(logits, axis=-1)
        outs.append(jnp.einsum('bhqk,bhkd->bhqd', p, vb))
    return jnp.concatenate(outs, axis=2)


def setup_inputs(seed: int = 0) -> dict:
    key = jax.random.key(seed)
    ks = jax.random.split(key, 20)
    f32 = jnp.float32
    L = DEPTH
    x = jax.random.normal(ks[0], (BATCH, SEQ, D_MODEL), f32)
    c = jax.random.normal(ks[1], (BATCH, D_MODEL), f32)
    w_ada = jax.random.normal(ks[2], (L, D_MODEL, 3 * D_MODEL), f32) * (0.3 * D_MODEL ** -0.5)
    b_ada = jax.random.normal(ks[3], (L, 3 * D_MODEL), f32) * 0.01
    w_in = jax.random.normal(ks[4], (L, D_MODEL, IN_COLS), f32) * D_MODEL ** -0.5
    b_f = jax.random.uniform(ks[5], (L, ATTN_HEADS), f32, 1.0, 4.0)
    n_idx = jnp.arange(SSM_STATE, dtype=f32)
    lam_re = -0.5 + 0.01 * jax.random.normal(ks[6], (L, SSM_GROUPS, SSM_STATE), f32)
    lam_im = math.pi * n_idx + 0.01 * jax.random.normal(ks[7], (L, SSM_GROUPS, SSM_STATE), f32)
    log_dt = jax.random.uniform(ks[8], (L, SSM_GROUPS), f32, math.log(1e-3), math.log(1e-1))
    b_scale = (2.0 * SSM_GROUP) ** -0.5
    ssm_b_re = jax.random.normal(ks[9], (L, SSM_GROUPS, SSM_STATE, SSM_GROUP), f32) * b_scale
    ssm_b_im = jax.random.normal(ks[10], (L, SSM_GROUPS, SSM_STATE, SSM_GROUP), f32) * b_scale
    c_scale = SSM_STATE ** -0.5
    ssm_c_re = jax.random.normal(ks[11], (L, SSM_GROUPS, SSM_GROUP, SSM_STATE), f32) * c_scale
    ssm_c_im = jax.random.normal(ks[12], (L, SSM_GROUPS, SSM_GROUP, SSM_STATE), f32) * c_scale
    ssm_d = jax.random.normal(ks[13], (L, SSM_WIDTH), f32)
    w_glu = jax.random.normal(ks[14], (L, SSM_WIDTH, 2 * SSM_WIDTH), f32) * SSM_WIDTH ** -0.5
    w_proj_ssm = jax.random.normal(ks[15], (L, SSM_WIDTH, D_MODEL), f32) * (SSM_WIDTH ** -0.5 * DEEPNORM_BETA)
    w_proj_attn = jax.random.normal(ks[16], (L, ATTN_WIDTH, D_MODEL), f32) * (ATTN_WIDTH ** -0.5 * DEEPNORM_BETA)
    w_out = jax.random.normal(ks[17], (L, D_MODEL, D_MODEL), f32) * (D_MODEL ** -0.5 * DEEPNORM_BETA)
    ln_g = 1.0 + 0.01 * jax.random.normal(ks[18], (L, D_MODEL), f32)
    ln_b = 0.01 * jax.random.normal(ks[19], (L, D_MODEL), f32)
    return {"x": x, "c": c, "w_ada": w_ada, "b_ada": b_ada, "w_in": w_in, "b_f": b_f,
            "lam_re": lam_re, "lam_im": lam_im, "log_dt": log_dt,
            "ssm_b_re": ssm_b_re, "ssm_b_im": ssm_b_im, "ssm_c_re": ssm_c_re, "ssm_c_im": ssm_c_im,
            "ssm_d": ssm_d, "w_glu": w_glu, "w_proj_ssm": w_proj_ssm, "w_proj_attn": w_proj_attn,
            "w_out": w_out, "ln_g": ln_g, "ln_b": ln_b}


def reference(x, c, w_ada, b_ada, w_in, b_f, lam_re, lam_im, log_dt, ssm_b_re, ssm_b_im,
              ssm_c_re, ssm_c_im, ssm_d, w_glu, w_proj_ssm, w_proj_attn, w_out, ln_g, ln_b):
    bsz, s_len, _ = x.shape
    split_idx = [int(v) for v in np.cumsum(IN_SPLITS)[:-1]]
    c_act = jax.nn.silu(c.astype(jnp.float32))
    h = x.astype(jnp.float32)
    for l in range(DEPTH):
        mod = c_act @ w_ada[l].astype(jnp.float32) + b_ada[l].astype(jnp.float32)
        shift, scale, gate = jnp.split(mod, 3, axis=-1)
        u = _layer_norm(h) * (1.0 + scale[:, None, :]) + shift[:, None, :]
        proj = u @ w_in[l].astype(jnp.float32)
        (x_ssm, z_ssm, q, k, v, f_logit, z_attn, g_ssm, g_attn) = jnp.split(proj, split_idx, axis=-1)
        y_ssm = _s5_mixer(x_ssm, lam_re[l], lam_im[l], log_dt[l], ssm_b_re[l], ssm_b_im[l],
                          ssm_c_re[l], ssm_c_im[l], ssm_d[l], w_glu[l])
        y_ssm = (y_ssm * jax.nn.silu(z_ssm)) @ w_proj_ssm[l].astype(jnp.float32)
        to_heads = lambda t: t.reshape(bsz, s_len, ATTN_HEADS, ATTN_HEAD_DIM).transpose(0, 2, 1, 3)
        log_f = jax.nn.log_sigmoid(f_logit + b_f[l].astype(jnp.float32)).transpose(0, 2, 1)
        o = _forgetting_attention(to_heads(q), to_heads(k), to_heads(v), log_f)
        o = o.transpose(0, 2, 1, 3).reshape(bsz, s_len, ATTN_WIDTH)
        y_attn = (o * jax.nn.silu(z_attn)) @ w_proj_attn[l].astype(jnp.float32)
        merged = jax.nn.sigmoid(g_ssm) * y_ssm + jax.nn.sigmoid(g_attn) * y_attn
        sub = (merged @ w_out[l].astype(jnp.float32)) * gate[:, None, :]
        h = _layer_norm(DEEPNORM_ALPHA * h + sub, ln_g[l], ln_b[l])
    return h.astype(x.dtype)
```

```python
import contextlib
import math
import numpy as np
import concourse.bass as bass
import concourse.mybir as mybir
from concourse.bass_utils import run_bass_kernel_spmd

F32 = mybir.dt.float32
BF16 = mybir.dt.bfloat16
I32 = mybir.dt.int32
ALU = mybir.AluOpType
AF = mybir.ActivationFunctionType

COMPUTE = ("pe", "act", "dve", "pool")
DMAQ = ("sp",)
ENGS = COMPUTE + DMAQ
N_DMA_SEMS = 32
import os
PREFETCH_FC = os.environ.get("PF", "1") == "1"
TICK = int(os.environ.get("TICK", "4"))
UTA = int(os.environ.get("UTA", "2"))
DDV = os.environ.get("DDV", "0") == "1"
PRETICK = int(os.environ.get("PRETICK", "0"))

S_LEN = 2048
D = 1024
NKV = 25
ALPHA = 2.0 ** 0.25
LN_EPS = 1e-5
TWO_PI = 2.0 * math.pi
CW_C1 = 6.28125
CW_C2 = TWO_PI - CW_C1


class Op:
    __slots__ = ("eng", "fn", "preds", "idx", "gid", "signal", "is_dma", "dsem", "dval", "dprev",
                 "cost", "tag", "nbytes", "prio", "nsucc", "succs", "npend", "ready", "fin", "pos")

    def __init__(self, eng, fn, idx, gid, is_dma, cost, tag, nbytes):
        self.eng = eng; self.fn = fn; self.idx = idx; self.gid = gid
        self.preds = set()
        self.signal = False; self.is_dma = is_dma
        self.dsem = None; self.dval = None; self.dprev = None
        self.cost = cost; self.tag = tag; self.nbytes = nbytes
        self.prio = 0.0; self.succs = []; self.npend = 0; self.ready = 0.0; self.fin = 0.0; self.pos = -1


ACT_SETS = {"Exp": 0, "Tanh": 0, "Identity": -1, "Copy": -1, "Gelu": 1, "Sigmoid": 2, "Silu": 3, "Sin": 4, "Sqrt": 5, "Ln": 6}
XLAT = float(os.environ.get("XLAT", "0.5"))
DMA_BW = float(os.environ.get("DMABW", "400e3"))
DMA_LAT = 2.0


class Sched:
    def __init__(self):
        self.ops = {e: [] for e in ENGS}
        self.all = []
        self.last_write = {}
        self.reads = {}
        self.dma_counts = [0] * N_DMA_SEMS
        self.reorder = True

    def add(self, eng, fn, reads=(), writes=(), cost=0.1, tag=None, nbytes=0):
        op = Op(eng, fn, len(self.ops[eng]), len(self.all), eng in DMAQ, cost, tag, nbytes)
        for b in reads:
            if b in self.last_write:
                op.preds.add(self.last_write[b])
        for b in writes:
            if b in self.last_write:
                op.preds.add(self.last_write[b])
            for r in self.reads.get(b, ()):
                op.preds.add(r)
        op.preds.discard(op)
        self.ops[eng].append(op)
        self.all.append(op)
        for b in reads:
            self.reads.setdefault(b, []).append(op)
        for b in writes:
            self.last_write[b] = op
            self.reads[b] = []
        return op

    def schedule(self):
        ops = self.all
        for op in ops:
            op.succs = []
        for op in ops:
            for p in op.preds:
                p.succs.append(op)
        for op in reversed(ops):
            m = 0.0
            for s_ in op.succs:
                if s_.prio > m:
                    m = s_.prio
            op.prio = op.cost + m + (DMA_LAT if op.is_dma else 0.0)
        if not self.reorder:
            return {e: list(self.ops[e]) for e in ENGS}
        for op in ops:
            op.npend = len(op.preds); op.ready = 0.0
        cand = {e: [] for e in ENGS}
        for op in ops:
            if op.npend == 0:
                cand[op.eng].append(op)
        free = {e: 0.0 for e in ENGS}
        last_set = [None]
        dma_free = [0.0]
        final = {e: [] for e in ENGS}
        n_left = len(ops)
        while n_left:
            best = None; bkey = None
            for e in ENGS:
                cl = cand[e]
                if not cl:
                    continue
                fe = free[e]
                for op in cl:
                    st_ = op.ready if op.ready > fe else fe
                    pen = 0.0
                    if e == "act" and op.tag is not None:
                        ts_ = ACT_SETS.get(op.tag, 9)
                        if ts_ >= 0 and last_set[0] is not None and ts_ != last_set[0]:
                            pen = 1.3
                    key = (st_ + pen, -op.prio, op.gid)
                    if bkey is None or key < bkey:
                        bkey = key; best = op
            op = best
            e = op.eng
            st_ = max(op.ready, free[e])
            if e == "act" and op.tag is not None:
                ts_ = ACT_SETS.get(op.tag, 9)
                if ts_ >= 0:
                    if last_set[0] is not None and ts_ != last_set[0]:
                        st_ += 1.3
                    last_set[0] = ts_
            if op.is_dma:
                free[e] = st_ + 0.06
                t0 = max(st_ + DMA_LAT, dma_free[0])
                op.fin = t0 + op.nbytes / DMA_BW
                dma_free[0] = op.fin
            else:
                op.fin = st_ + op.cost
                free[e] = op.fin
            cand[e].remove(op)
            op.pos = len(final[e]); final[e].append(op)
            n_left -= 1
            for s_ in op.succs:
                lat = op.fin + (0.0 if (s_.eng == e and not op.is_dma) else XLAT)
                if lat > s_.ready:
                    s_.ready = lat
                s_.npend -= 1
                if s_.npend == 0:
                    cand[s_.eng].append(s_)
        self.makespan = max(op.fin for op in ops)
        return final

    def emit(self, nc):
        final = self.schedule()
        for e in ENGS:
            for i, op in enumerate(final[e]):
                op.pos = i
        rr = 0
        for op in final["sp"]:
            s_ = rr; rr = (rr + 1) % N_DMA_SEMS
            op.dsem = s_; op.dprev = self.dma_counts[s_]
            self.dma_counts[s_] += 16; op.dval = self.dma_counts[s_]
        for op in self.all:
            for p in op.preds:
                if p.eng == "pe" and op.eng == "pe":
                    continue
                p.signal = True
        sigcount = {}
        for e in COMPUTE:
            c = 0; arr = []
            for op in final[e]:
                if op.signal:
                    c += 1
                arr.append(c)
            sigcount[e] = arr
        with contextlib.ExitStack() as st:
            sems = {e: st.enter_context(nc.semaphore("s_" + e)) for e in COMPUTE}
            dsems = [st.enter_context(nc.semaphore("d%d" % i)) for i in range(N_DMA_SEMS)]
            block = st.enter_context(nc.Block())
            sched = self

            def run(e, h):
                waited = {}
                for op in final[e]:
                    need = {}
                    for p in op.preds:
                        if p.is_dma:
                            wk = ("d", p.dsem)
                            if need.get(wk, -1) < p.dval:
                                need[wk] = p.dval
                        else:
                            if p.eng == "pe" and e == "pe":
                                continue
                            v = sigcount[p.eng][p.pos]
                            if need.get(p.eng, -1) < v:
                                need[p.eng] = v
                    if op.is_dma and op.dprev > 0:
                        wk = ("d", op.dsem)
                        if need.get(wk, -1) < op.dprev:
                            need[wk] = op.dprev
                    for wk, v in need.items():
                        if waited.get(wk, -1) < v:
                            if isinstance(wk, tuple):
                                h.wait_ge(dsems[wk[1]], v)
                            else:
                                h.wait_ge(sems[wk], v)
                            waited[wk] = v
                    ins = op.fn(h)
                    if op.is_dma:
                        ins.then_inc(dsems[op.dsem], 16)
                    elif op.signal:
                        ins.then_inc(sems[e], 1)
                if e == "sp":
                    for s_ in range(N_DMA_SEMS):
                        if sched.dma_counts[s_] > 0 and waited.get(("d", s_), -1) < sched.dma_counts[s_]:
                            h.wait_ge(dsems[s_], sched.dma_counts[s_])

            @block.tensor
            def _(h):
                run("pe", h)

            @block.scalar
            def _(h):
                run("act", h)

            @block.vector
            def _(h):
                run("dve", h)

            @block.gpsimd
            def _(h):
                run("pool", h)

            @block.sync
            def _(h):
                run("sp", h)


class Buf:
    _uid = 0

    def __init__(self, arena, off, nbytes, prior, name):
        Buf._uid += 1
        self.uid = Buf._uid
        self.arena = arena; self.off = off; self.nbytes = nbytes; self.asize = (nbytes + 63) // 64 * 64
        self.prior = prior; self.keys = set(); self.name = name

    def k(self, sub=None):
        key = (self.uid, sub)
        if key not in self.keys:
            self.keys.add(key)
            self.arena.S.reads[key] = list(self.prior)
        return key

    def ap(self, dt=BF16):
        a = self.arena.t[:, self.off // 2:(self.off + self.nbytes) // 2]
        if dt != BF16:
            a = a.bitcast(dt)
        return a


class Arena:
    def __init__(self, S, tensor, nbytes):
        self.S = S; self.t = tensor
        self.free = [[0, nbytes, [], 0]]
        self.peak = 0; self.used = 0; self.clock = 0

    LONG = {"uT", "yT", "s5o", "oz", "mg", "Vaug", "QA", "KA", "w_xs", "w_out", "Tt", "Wt", "Vt", "Esin", "Ecos",
            "Xs", "Ybf", "xsT", "ident", "ident_bf", "sel_bf", "cmask_bf", "tmask", "kvals", "cvals", "lre", "lim",
            "ldt", "dpair", "badaT", "cT", "Are", "Aim", "ph", "r8", "cr", "ci", "gate_b", "lng", "lnb", "modT", "sc1",
            "eps", "cact", "cact_bf", "w_q", "w_k", "w_za", "w_v", "w_glu", "w_zs", "w_ps", "w_pa", "w_gs", "w_ga"}

    def alloc(self, nbytes, name=""):
        req = nbytes
        nbytes = (nbytes + 63) // 64 * 64
        longlived = name.rstrip("0123456789") in self.LONG
        fr = self.free
        best = None; bkey = None
        for i in range(len(fr)):
            tot = 0; mx = 0; j = i
            while True:
                tot += fr[j][1]; mx = max(mx, fr[j][3])
                if tot >= nbytes:
                    if longlived:
                        key = (-(fr[j][0] + fr[j][1]), 0)
                    else:
                        key = (mx, fr[i][0])
                    if bkey is None or key < bkey:
                        bkey = key; best = (i, j)
                    break
                if j + 1 < len(fr) and fr[j][0] + fr[j][1] == fr[j + 1][0]:
                    j += 1
                else:
                    break
        if best is None:
            raise RuntimeError("arena OOM for %s (%d bytes), used %d" % (name, nbytes, self.used))
        i, j = best
        acc = set()
        need = nbytes
        if longlived:
            end = fr[j][0] + fr[j][1]
            off = end - nbytes
            newfree = fr[:i]
            mid = []
            for k in range(j, i - 1, -1):
                seg = fr[k]
                if need <= 0:
                    mid.append(seg); continue
                acc.update(seg[2])
                if seg[1] <= need:
                    need -= seg[1]
                else:
                    mid.append([seg[0], seg[1] - need, seg[2], seg[3]])
                    need = 0
            newfree += list(reversed(mid))
            newfree += fr[j + 1:]
        else:
            off = fr[i][0]
            newfree = fr[:i]
            for k in range(i, j + 1):
                seg = fr[k]
                if need <= 0:
                    newfree.append(seg); continue
                acc.update(seg[2])
                if seg[1] <= need:
                    need -= seg[1]
                else:
                    newfree.append([seg[0] + need, seg[1] - need, seg[2], seg[3]])
                    need = 0
            newfree += fr[j + 1:]
        self.free = newfree
        self.used += nbytes
        self.peak = max(self.peak, self.used)
        return Buf(self, off, req, list(acc), name)

    def release(self, b):
        acc = list(b.prior) if not b.keys else []
        for key in b.keys:
            if key in self.S.last_write:
                acc.append(self.S.last_write[key])
            acc += self.S.reads.get(key, [])
        acc = list(set(acc))
        self.used -= b.asize
        self.clock += 1
        self.free.append([b.off, b.asize, acc, self.clock])
        self.free.sort(key=lambda x: x[0])


def build(debug=()):
    nc = bass.Bass("TRN2", target_bir_lowering=False)
    S = Sched()

    def din(name, shape):
        return nc.dram_tensor(name, list(shape), F32, kind="ExternalInput").ap()

    x = din("x", [S_LEN, D]); cT = din("cT", [128, 8]); w_ada = din("w_ada", [24 * 128, 8 * 128])
    b_adaT = din("b_adaT", [128, 24]); b_ada_row = din("b_ada_row", [1, 3 * D])
    w_in = din("w_in", [40 * 128, 8 * 128]); w_fd = din("w_f", [128, 64]); b_f = din("b_f", [8, 1])
    lam_re = din("lam_re", [128, 16]); lam_im = din("lam_im", [128, 16]); ldt = din("ldt", [128, 16])
    Bre_d = din("Bre", [128, 16 * 32]); Bim_d = din("Bim", [128, 16 * 32])
    Cre_d = din("Cre", [128, 16 * 32]); Cim_d = din("Cim", [128, 16 * 32])
    dpair_d = din("dpair", [128, 16])
    w_glu = din("w_glu", [8 * 128, 4 * 128]); w_ps = din("w_ps", [8 * 128, 4 * 128]); w_pa = din("w_pa", [8 * 128, 4 * 128])
    w_out = din("w_out", [8 * 128, 8 * 128]); ln_g = din("ln_g", [1, D]); ln_b = din("ln_b", [1, D])
    ident_d = din("ident", [128, 128]); sel_d = din("sel", [128, 16 * 128]); cmask_d = din("cmask", [128, 128])
    tmask_d = din("tmask", [128, 256]); kvals_d = din("kvals", [128, NKV]); cvals_d = din("cvals", [128, 256])
    out = nc.dram_tensor("out", [S_LEN, D], F32, kind="ExternalOutput").ap()
    dbg_out = {}

    ARENA_BYTES = 198 * 1024
    st = contextlib.ExitStack()
    arena_t = st.enter_context(nc.sbuf_tensor("arena", [128, ARENA_BYTES // 2], BF16))
    psum = [st.enter_context(nc.psum_tensor("ps%d" % i, [128, 512], F32)) for i in range(8)]
    ti_ts = [st.enter_context(nc.sbuf_tensor("ra_int%d" % i, [128, 1024], I32)) for i in range(2)]
    ti_rr = [0]
    AR = Arena(S, arena_t, ARENA_BYTES)
    ps_rr = [0]

    ps_mode = ["all"]
    qk_rr = [0]

    def ps_next():
        if ps_mode[0] == "all":
            i = ps_rr[0]; ps_rr[0] = (i + 1) % 6
        else:
            i = 4 + ps_rr[0] % 2; ps_rr[0] = (ps_rr[0] + 1) % 2
        return psum[i], ("ps", i)

    def ps_qk():
        i = qk_rr[0]; qk_rr[0] = (i + 1) % 4
        return psum[i], ("ps", i)

    def A(nbytes, name=""):
        return AR.alloc(nbytes, name)

    def fcols(ap):
        n = 1
        for d_ in ap.shape[1:]:
            n *= int(d_)
        return n

    def esize(ap):
        return 2 if ap.dtype == BF16 else 4

    def vcost(eng, ap, mult=1.0):
        c = fcols(ap) * mult
        return (0.12 + c / 900.0) if eng == "dve" else (0.15 + c / 300.0)

    def dma(out_ap, in_ap, reads, writes):
        nb = int(out_ap.shape[0]) * fcols(out_ap) * esize(out_ap)
        S.add("sp", lambda h: h.dma_start(out=out_ap, in_=in_ap), reads=reads, writes=writes, cost=0.06, nbytes=nb)

    def mm(out_ap, lhsT, rhs, start, stop, reads, writes):
        n = max(fcols(rhs), 64)
        c = 0.086 if n <= 128 else (0.26 if n == 256 else n / 2100.0 + 0.012)
        if int(lhsT.shape[0]) == 70:
            c += 0.03
        if rhs.dtype == F32:
            c *= 4.0
        S.add("pe", lambda h: h.matmul(out_ap, lhsT, rhs, start=start, stop=stop), reads=reads, writes=writes, cost=c)

    def act(out_ap, in_ap, func, reads, writes, bias=0.0, scale=1.0):
        S.add("act", lambda h: h.activation(out=out_ap, in_=in_ap, func=func, bias=bias, scale=scale),
              reads=reads, writes=writes, cost=(0.08 + fcols(out_ap) / 1150.0) * (3.0 if func == AF.Gelu else 1.0), tag=func.name)

    def tt(eng, out_ap, a, b, op, reads, writes):
        S.add(eng, lambda h: h.tensor_tensor(out=out_ap, in0=a, in1=b, op=op), reads=reads, writes=writes,
              cost=vcost(eng, out_ap))

    def ts(eng, out_ap, a, s1, s2, op0, op1, reads, writes):
        if s2 is None:
            S.add(eng, lambda h: h.tensor_scalar(out=out_ap, in0=a, scalar1=s1, scalar2=None, op0=op0),
                  reads=reads, writes=writes, cost=vcost(eng, out_ap))
        else:
            S.add(eng, lambda h: h.tensor_scalar(out=out_ap, in0=a, scalar1=s1, scalar2=s2, op0=op0, op1=op1),
                  reads=reads, writes=writes, cost=vcost(eng, out_ap))

    def stt(eng, out_ap, a, s, b, op0, op1, reads, writes):
        S.add(eng, lambda h: h.scalar_tensor_tensor(out=out_ap, in0=a, scalar=s, in1=b, op0=op0, op1=op1),
              reads=reads, writes=writes, cost=vcost(eng, out_ap))

    def cp(eng, out_ap, in_ap, reads, writes):
        if eng == "act":
            S.add(eng, lambda h: h.activation(out=out_ap, in_=in_ap, func=AF.Identity), reads=reads, writes=writes,
                  cost=0.08 + fcols(out_ap) / 1150.0, tag="Identity")
        else:
            S.add(eng, lambda h: h.tensor_copy(out=out_ap, in_=in_ap), reads=reads, writes=writes,
                  cost=vcost(eng, out_ap, 0.6 if eng == "dve" else 1.5))

    def memset(eng, ap, val, writes):
        S.add(eng, lambda h: h.memset(ap, val), writes=writes, cost=vcost(eng, ap, 0.5))

    def dump(name, ap, shape, key):
        if name in debug:
            t = nc.dram_tensor("dbg_" + name, list(shape), ap.dtype, kind="ExternalOutput").ap()
            dbg_out[name] = t
            dma(t, ap, [key], [])

    def load_const(dram, cols, name, to_bf=False):
        b = A(cols * 4, name)
        dma(b.ap(F32), dram, [], [b.k()])
        if not to_bf:
            return b
        bb = A(cols * 2, name + "_bf")
        cp("pool", bb.ap(), b.ap(F32), [b.k()], [bb.k()])
        AR.release(b)
        return bb

    ident = load_const(ident_d, 128, "ident")
    ident_bf = A(256, "ident_bf")
    cp("pool", ident_bf.ap(), ident.ap(F32), [ident.k()], [ident_bf.k()])
    sel_bf = load_const(sel_d, 16 * 128, "sel", True)
    cmask_bf = load_const(cmask_d, 128, "cmask", True)
    tmask = load_const(tmask_d, 256, "tmask")
    kvals = load_const(kvals_d, NKV, "kvals")
    cvals = load_const(cvals_d, 256, "cvals")
    lre = load_const(lam_re, 16, "lre"); lim = load_const(lam_im, 16, "lim"); ldtb = load_const(ldt, 16, "ldt")
    dpair = load_const(dpair_d, 16, "dpair")
    badaT = load_const(b_adaT, 24, "badaT")
    cTb = load_const(cT, 8, "cT")
    sel3 = sel_bf.ap().rearrange("p (a d) -> p a d", d=128)

    def SEL(a, b):
        return sel3[:, a * 4 + b, :]

    cast_rr = [0]
    def load_w(blk2d, j0, nblk, kcn, name, eng=("dve", "act")):
        ncols = nblk * 128
        wb = A(kcn * ncols * 2, name)
        wv = wb.ap().rearrange("p (k n) -> p k n", n=ncols)
        step = max(1, (2048 // kcn) // 128)
        jj = 0; pi = 0
        while jj < nblk:
            nb = min(step, nblk - jj)
            sg = A(nb * kcn * 128 * 4, name + "_stg")
            dma(sg.ap(F32).rearrange("p (j c) -> p j c", j=nb),
                blk2d[(j0 + jj) * 128:(j0 + jj + nb) * 128, :].rearrange("(j p) c -> p j c", p=128), [], [sg.k()])
            e = eng if isinstance(eng, str) else eng[cast_rr[0] % len(eng)]
            cast_rr[0] += 1
            cp(e, wv[:, :, jj * 128:(jj + nb) * 128].rearrange("p k (j n) -> p k j n", j=nb),
               sg.ap(F32).rearrange("p (j k n) -> p k j n", j=nb, k=kcn), [sg.k()], [wb.k(pi)])
            AR.release(sg)
            jj += nb; pi += 1
        return wb, wv, [wb.k(i) for i in range(pi)]

    def tmp(cols, name):
        return A(cols * 4, name)

    dt_t = tmp(16, "dt")
    act(dt_t.ap(F32), ldtb.ap(F32), AF.Exp, [ldtb.k()], [dt_t.k()])
    lrdt = tmp(16, "lrdt"); lidt = tmp(16, "lidt")
    tt("dve", lrdt.ap(F32), lre.ap(F32), dt_t.ap(F32), ALU.mult, [lre.k(), dt_t.k()], [lrdt.k()])
    tt("dve", lidt.ap(F32), lim.ap(F32), dt_t.ap(F32), ALU.mult, [lim.k(), dt_t.k()], [lidt.k()])
    NT = 16 * NKV
    kv3 = kvals.ap(F32).unsqueeze(1).broadcast_to([128, 16, NKV])

    def v3(b, n=NKV):
        return b.ap(F32).rearrange("p (a k) -> p a k", k=n)

    def reduce_angle(dst, dstk, src, srck, cols, shift, view):
        y = tmp(cols, "ra_y"); t1 = tmp(cols, "ra_t1"); tf = tmp(cols, "ra_tf")
        if cols >= 512:
            sh_t = A(4, "ra_sh"); memset("pool", sh_t.ap(F32), float(shift), [sh_t.k()])
            act(y.ap(F32), src, AF.Identity, [srck, sh_t.k()], [y.k()], bias=sh_t.ap(F32)[:, 0:1], scale=1.0)
            act(t1.ap(F32), y.ap(F32), AF.Identity, [y.k()], [t1.k()], scale=1.0 / TWO_PI)
            AR.release(sh_t)
        else:
            ts("dve", y.ap(F32), src, shift, None, ALU.add, None, [srck], [y.k()])
            ts("dve", t1.ap(F32), y.ap(F32), 1.0 / TWO_PI, None, ALU.mult, None, [y.k()], [t1.k()])
        ti_i = ti_rr[0] % 2; ti_rr[0] += 1
        ti_t = ti_ts[ti_i]
        cp("dve", ti_t[:, 0:cols], t1.ap(F32), [t1.k()], ["ra_int%d" % ti_i])
        cp("dve", tf.ap(F32), ti_t[:, 0:cols], ["ra_int%d" % ti_i], [tf.k()])
        stt("dve", t1.ap(F32), tf.ap(F32), -CW_C1, y.ap(F32), ALU.mult, ALU.add, [tf.k(), y.k()], [t1.k()])
        stt("dve", y.ap(F32), tf.ap(F32), -CW_C2, t1.ap(F32), ALU.mult, ALU.add, [tf.k(), t1.k()], [y.k()])
        ts("dve", dst, y.ap(F32), -math.pi, math.pi, ALU.max, ALU.min, [y.k()], [dstk])
        for b in (y, t1, tf):
            AR.release(b)

    magk = tmp(NT, "magk"); angk = tmp(NT, "angk")
    tt("dve", v3(magk), lrdt.ap(F32).unsqueeze(2).broadcast_to([128, 16, NKV]), kv3, ALU.mult, [lrdt.k(), kvals.k()], [magk.k()])
    act(magk.ap(F32), magk.ap(F32), AF.Exp, [magk.k()], [magk.k()])
    tt("dve", v3(angk), lidt.ap(F32).unsqueeze(2).broadcast_to([128, 16, NKV]), kv3, ALU.mult, [lidt.k(), kvals.k()], [angk.k()])
    sred = tmp(NT, "sred"); cred = tmp(NT, "cred")
    reduce_angle(sred.ap(F32), sred.k(), angk.ap(F32), angk.k(), NT, 0.0, None)
    reduce_angle(cred.ap(F32), cred.k(), angk.ap(F32), angk.k(), NT, math.pi / 2, None)
    Are = tmp(NT, "Are"); Aim = tmp(NT, "Aim")
    act(sred.ap(F32), sred.ap(F32), AF.Sin, [sred.k()], [sred.k()])
    act(cred.ap(F32), cred.ap(F32), AF.Sin, [cred.k()], [cred.k()])
    tt("dve", Are.ap(F32), magk.ap(F32), cred.ap(F32), ALU.mult, [magk.k(), cred.k()], [Are.k()])
    tt("dve", Aim.ap(F32), magk.ap(F32), sred.ap(F32), ALU.mult, [magk.k(), sred.k()], [Aim.k()])
    ph = tmp(16, "ph"); r8 = tmp(16, "r8")
    ang8 = tmp(16, "ang8")
    cp("dve", ang8.ap(F32), v3(angk)[:, :, 16], [angk.k()], [ang8.k()])
    reduce_angle(ph.ap(F32), ph.k(), ang8.ap(F32), ang8.k(), 16, 0.0, None)
    cp("dve", r8.ap(F32), v3(magk)[:, :, 16], [magk.k()], [r8.k()])
    AR.release(ang8)
    am1 = tmp(16, "am1"); abi = tmp(16, "abi"); den = tmp(16, "den"); t_a = tmp(16, "t_a"); t_b = tmp(16, "t_b")
    cr = tmp(16, "cr"); ci = tmp(16, "ci")
    ts("dve", am1.ap(F32), v3(Are)[:, :, 9], -1.0, None, ALU.add, None, [Are.k()], [am1.k()])
    cp("dve", abi.ap(F32), v3(Aim)[:, :, 9], [Aim.k()], [abi.k()])
    tt("dve", den.ap(F32), lre.ap(F32), lre.ap(F32), ALU.mult, [lre.k()], [den.k()])
    tt("dve", t_a.ap(F32), lim.ap(F32), lim.ap(F32), ALU.mult, [lim.k()], [t_a.k()])
    tt("dve", den.ap(F32), den.ap(F32), t_a.ap(F32), ALU.add, [den.k(), t_a.k()], [den.k()])
    S.add("dve", lambda h: h.reciprocal(out=den.ap(F32), in_=den.ap(F32)), reads=[den.k()], writes=[den.k()], cost=0.2)
    tt("dve", t_a.ap(F32), am1.ap(F32), lre.ap(F32), ALU.mult, [am1.k(), lre.k()], [t_a.k()])
    tt("dve", t_b.ap(F32), abi.ap(F32), lim.ap(F32), ALU.mult, [abi.k(), lim.k()], [t_b.k()])
    tt("dve", t_a.ap(F32), t_a.ap(F32), t_b.ap(F32), ALU.add, [t_a.k(), t_b.k()], [t_a.k()])
    tt("dve", cr.ap(F32), t_a.ap(F32), den.ap(F32), ALU.mult, [t_a.k(), den.k()], [cr.k()])
    tt("dve", t_a.ap(F32), abi.ap(F32), lre.ap(F32), ALU.mult, [abi.k(), lre.k()], [t_a.k()])
    tt("dve", t_b.ap(F32), am1.ap(F32), lim.ap(F32), ALU.mult, [am1.k(), lim.k()], [t_b.k()])
    tt("dve", t_a.ap(F32), t_a.ap(F32), t_b.ap(F32), ALU.subtract, [t_a.k(), t_b.k()], [t_a.k()])
    tt("dve", ci.ap(F32), t_a.ap(F32), den.ap(F32), ALU.mult, [t_a.k(), den.k()], [ci.k()])
    for b in (am1, abi, den, t_a, t_b, magk, angk, sred, cred, dt_t, lrdt, lidt):
        AR.release(b)

    def cmul(eng, o_re, o_im, ore_k, oim_k, a_re, a_im, b_re, b_im, rk, cols, neg_im=False, shape=None):
        t1 = tmp(cols, "cm1"); t2 = tmp(cols, "cm2")
        v = (lambda b_: b_.ap(F32)) if shape is None else (lambda b_: b_.ap(F32).rearrange(shape[0], **shape[1]))
        tt(eng, v(t1), a_re, b_re, ALU.mult, rk, [t1.k()])
        tt(eng, v(t2), a_im, b_im, ALU.mult, rk, [t2.k()])
        tt(eng, o_re, v(t1), v(t2), ALU.subtract, [t1.k(), t2.k()], [ore_k])
        tt(eng, v(t1), a_re, b_im, ALU.mult, rk, [t1.k()])
        tt(eng, v(t2), a_im, b_re, ALU.mult, rk, [t2.k()])
        if neg_im:
            stt(eng, o_im, v(t1), -1.0, v(t2), ALU.mult, ALU.subtract, [t1.k(), t2.k()], [oim_k])
        else:
            tt(eng, o_im, v(t1), v(t2), ALU.add, [t1.k(), t2.k()], [oim_k])
        AR.release(t1); AR.release(t2)

    Are3 = v3(Are); Aim3 = v3(Aim)
    def load_hc(hc):
        return (load_w(w_in, 8 + hc, 1, 8, "w_q"),
                load_w(w_in, 12 + hc, 1, 8, "w_k"),
                load_w(w_in, 20 + hc, 1, 8, "w_za"),
                load_w(w_in, 16 + hc, 1, 8, "w_v"))
    hc_w = load_hc(0)
    cact = A(8 * 4, "cact")
    act(cact.ap(F32), cTb.ap(F32), AF.Silu, [cTb.k()], [cact.k()])
    cact_bf = A(8 * 2, "cact_bf")
    cp("dve", cact_bf.ap(), cact.ap(F32), [cact.k()], [cact_bf.k()])
    mrow = A(2048 * 4, "mrow")
    browA = A(2048 * 4, "browA")
    dma(browA.ap(F32)[0:1, :], b_ada_row[0:1, 0:2048], [], [browA.k()])
    for blk in range(4):
        sg = A(8 * 512 * 4, "wada_stg")
        sv = sg.ap(F32).rearrange("p (k n) -> p k n", n=512)
        dma(sg.ap(F32).rearrange("p (j c) -> p j c", j=4),
            w_ada[blk * 512:(blk + 1) * 512, :].rearrange("(j p) c -> p j c", p=128), [], [sg.k()])
        s16 = A(8 * 512 * 2, "wada_bf")
        s16v = s16.ap().rearrange("p (k n) -> p k n", n=512)
        cp("dve" if blk % 2 == 0 else "act", s16v.rearrange("p k (j n) -> p k j n", j=4),
           sg.ap(F32).rearrange("p (j k n) -> p k j n", j=4, k=8), [sg.k()], [s16.k()])
        AR.release(sg)
        pg, pgk = ps_next()
        for kc in range(8):
            mm(pg[0:1, :], cact_bf.ap()[:, kc:kc + 1], s16v[:, kc, :], kc == 0, kc == 7, [s16.k(), cact_bf.k()], [pgk])
        AR.release(s16)
        tt("dve", mrow.ap(F32)[0:1, blk * 512:(blk + 1) * 512], pg[0:1, :], browA.ap(F32)[0:1, blk * 512:(blk + 1) * 512], ALU.add,
           [pgk, browA.k()], [mrow.k(blk)])
    one1 = A(4, "one1")
    memset("pool", one1.ap(F32)[0:1, :], 1.0, [one1.k()])
    modT = A(16 * 4, "modT")
    pm, pmk = ps_next()
    for j in range(16):
        mm(pm[:, j:j + 1], mrow.ap(F32)[0:1, j * 128:(j + 1) * 128], one1.ap(F32)[0:1, 0:1], True, True,
           [mrow.k(j // 4), one1.k()], [pmk])
    cp("dve", modT.ap(F32), pm[:, 0:16], [pmk], [modT.k()])
    sc1 = A(8 * 4, "sc1")
    ts("dve", sc1.ap(F32), modT.ap(F32)[:, 8:16], 1.0, None, ALU.add, None, [modT.k()], [sc1.k()])
    AR.release(mrow); AR.release(browA); AR.release(one1)
    uT = [A(S_LEN * 2, "uT%d" % i) for i in range(8)]
    eps_t = A(4, "eps")
    memset("pool", eps_t.ap(F32), LN_EPS, [eps_t.k()])

    def layer_norm_stats_multi(srcs, tag):
        n = len(srcs)
        stts = [A(2 * 6 * 4, "bnst" + tag) for _ in range(n)]
        mvs = [A(2 * 4, "mv" + tag) for _ in range(n)]
        sds = [A(4, "sd" + tag) for _ in range(n)]
        rstds = [A(4, "rstd" + tag) for _ in range(n)]
        nmrs = [A(4, "nmr" + tag) for _ in range(n)]
        for i, (src_ap, src_key) in enumerate(srcs):
            sv = stts[i].ap(F32).rearrange("p (a b) -> p a b", b=6)
            for hf in range(2):
                S.add("dve", lambda h, hf=hf, sv=sv, src_ap=src_ap: h.bn_stats(out=sv[:, hf, :], in_=src_ap[:, hf * 512:(hf + 1) * 512]),
                      reads=[src_key[hf] if isinstance(src_key, list) else src_key], writes=[stts[i].k(hf)], cost=0.65)
            S.add("dve", lambda h, i=i: h.bn_aggr(out=mvs[i].ap(F32), in_=stts[i].ap(F32)),
                  reads=[stts[i].k(0), stts[i].k(1)], writes=[mvs[i].k()], cost=0.2)
        for i in range(n):
            act(sds[i].ap(F32), mvs[i].ap(F32)[:, 1:2], AF.Sqrt, [mvs[i].k(), eps_t.k()], [sds[i].k()], bias=eps_t.ap(F32)[:, 0:1])
        for i in range(n):
            S.add("dve", lambda h, i=i: h.reciprocal(out=rstds[i].ap(F32), in_=sds[i].ap(F32)), reads=[sds[i].k()], writes=[rstds[i].k()], cost=0.15)
            stt("dve", nmrs[i].ap(F32), mvs[i].ap(F32)[:, 0:1], -1.0, rstds[i].ap(F32), ALU.mult, ALU.mult,
                [mvs[i].k(), rstds[i].k()], [nmrs[i].k()])
        for b in stts + mvs + sds:
            AR.release(b)
        return list(zip(rstds, nmrs))

    for g4 in range(4):
        xn = A(4 * 1024 * 2, "xn")
        xnv = xn.ap().rearrange("p (a n) -> p a n", n=1024)
        xts = []
        for t4 in range(4):
            tti = g4 * 4 + t4
            xt = A(1024 * 4, "xt")
            dma(xt.ap(F32), x[tti * 128:(tti + 1) * 128, :], [], [xt.k()])
            xts.append(xt)
        stats = layer_norm_stats_multi([(xt.ap(F32), xt.k()) for xt in xts], "a")
        for t4 in range(4):
            xt = xts[t4]; rstd, nmr = stats[t4]
            act(xnv[:, t4, :], xt.ap(F32), AF.Identity, [xt.k(), rstd.k(), nmr.k()], [xn.k(t4)],
                bias=nmr.ap(F32)[:, 0:1], scale=rstd.ap(F32)[:, 0:1])
            AR.release(xt); AR.release(rstd); AR.release(nmr)
        for kc in range(8):
            p_, pk_ = ps_next()
            for t4 in range(4):
                mm(p_[:, t4 * 128:(t4 + 1) * 128], xnv[:, t4, kc * 128:(kc + 1) * 128], ident_bf.ap(), True, True,
                   [xn.k(t4), ident_bf.k()], [pk_])
            if UTA and kc % UTA == UTA - 1:
                act(uT[kc].ap()[:, g4 * 512:(g4 + 1) * 512], p_[:, :], AF.Identity, [pk_, sc1.k(), modT.k()], [uT[kc].k(g4)],
                    bias=modT.ap(F32)[:, kc:kc + 1], scale=sc1.ap(F32)[:, kc:kc + 1])
            else:
                ts("dve", uT[kc].ap()[:, g4 * 512:(g4 + 1) * 512], p_[:, :],
                   sc1.ap(F32)[:, kc:kc + 1], modT.ap(F32)[:, kc:kc + 1], ALU.mult, ALU.add,
                   [pk_, sc1.k(), modT.k()], [uT[kc].k(g4)])
        AR.release(xn)
    for kc in range(8):
        dump("uT%d" % kc, uT[kc].ap(), [128, S_LEN], uT[kc].k(3))

    def uT_keys(kc, tcb):
        return uT[kc].k(tcb)

    def proj_fm(wv, wkeys, col0, M, tcb, p_, pk_, kcn=8, src=None, srckeys=None):
        for kc in range(kcn):
            rhs = (uT[kc].ap() if src is None else src[kc].ap())[:, tcb * 512:(tcb + 1) * 512]
            rk = uT[kc].k(tcb) if src is None else srckeys(kc, tcb)
            mm(p_[0:M, :], wv[:, kc, col0:col0 + M], rhs, kc == 0, kc == kcn - 1, list(wkeys) + [rk], [pk_])

    w_xs, w_xs_v, w_xs_k = load_w(w_in, 0, 4, 8, "w_xs", eng="act")
    yT = [A(S_LEN * 2, "yT%d" % i) for i in range(4)]
    e3 = lambda b_: b_.ap(F32).rearrange("p (a c) -> p a c", c=256)

    def s5_gen():
      for cc in range(4):
          p0 = cc * 4
          Bre = tmp(128, "Bre"); Bim = tmp(128, "Bim"); Cre = tmp(128, "Cre"); Cim = tmp(128, "Cim")
          dma(Bre.ap(F32), Bre_d[:, p0 * 32:(p0 + 4) * 32], [], [Bre.k()])
          dma(Bim.ap(F32), Bim_d[:, p0 * 32:(p0 + 4) * 32], [], [Bim.k()])
          dma(Cre.ap(F32), Cre_d[:, p0 * 32:(p0 + 4) * 32], [], [Cre.k()])
          dma(Cim.ap(F32), Cim_d[:, p0 * 32:(p0 + 4) * 32], [], [Cim.k()])
          Bbre = tmp(128, "Bbre"); Bbim = tmp(128, "Bbim")
          sh43 = ("p (a b) -> p a b", dict(b=32))
          b3 = lambda b_: b_.ap(F32).rearrange("p (a b) -> p a b", b=32)
          crb = cr.ap(F32)[:, p0:p0 + 4].unsqueeze(2).broadcast_to([128, 4, 32])
          cib = ci.ap(F32)[:, p0:p0 + 4].unsqueeze(2).broadcast_to([128, 4, 32])
          cmul("dve", b3(Bbre), b3(Bbim), Bbre.k(), Bbim.k(), crb, cib, b3(Bre), b3(Bim),
               [cr.k(), ci.k(), Bre.k(), Bim.k()], 128, shape=sh43)
          yield
          sh8 = ("p (a i b) -> p a i b", dict(i=8, b=32)); sh9 = ("p (a i b) -> p a i b", dict(i=9, b=32))
          v8 = lambda b_: b_.ap(F32).rearrange(sh8[0], **sh8[1])
          v9 = lambda b_: b_.ap(F32).rearrange(sh9[0], **sh9[1])
          def abc(c0, n):
              return (Are3[:, p0:p0 + 4, c0:c0 + n].unsqueeze(3).broadcast_to([128, 4, n, 32]),
                      Aim3[:, p0:p0 + 4, c0:c0 + n].unsqueeze(3).broadcast_to([128, 4, n, 32]))

          def bb(bre, bim, n):
              return (b3(bre).unsqueeze(2).broadcast_to([128, 4, n, 32]), b3(bim).unsqueeze(2).broadcast_to([128, 4, n, 32]))
          Tt = A(4 * 256 * 2, "Tt"); Wt = A(4 * 4 * 128 * 2, "Wt"); Vt = A(2 * 1024 * 2, "Vt")
          Ttv = Tt.ap().rearrange("p (a n) -> p a n", n=256)
          Wtv = Wt.ap().rearrange("p (a b n) -> p a b n", b=4, n=128)
          Vtv = Vt.ap().rearrange("p (s a j b) -> p s a j b", s=2, a=4, j=8)
          WAre = tmp(1024, "WAre"); WAim = tmp(1024, "WAim")
          br_, bi_ = bb(Bbre, Bbim, 8)
          ar_, ai_ = abc(17, 8)
          cmul("dve", v8(WAre), v8(WAim), WAre.k(), WAim.k(), ar_, ai_, br_, bi_, [Are.k(), Aim.k(), Bbre.k(), Bbim.k()], 1024, shape=sh8)
          yield
          yield
          WAb = A(2 * 1024 * 2, "WAb")
          WAbv = WAb.ap().rearrange("p (s a i b) -> p s a i b", s=2, a=4, i=8)
          cp("act", WAbv[:, 0], v8(WAre), [WAre.k()], [WAb.k(0)])
          cp("pool", WAbv[:, 1], v8(WAim), [WAim.k()], [WAb.k(1)])
          AR.release(WAre); AR.release(WAim)
          for pr in range(4):
              p_, pk_ = ps_next()
              for pl in range(2):
                  for kt in range(2):
                      q = pl * 2 + kt
                      mm(p_[:, q * 128:(q + 1) * 128], WAbv[:, pl, pr, kt * 4:(kt + 1) * 4, :].rearrange("p a b -> p (a b)"), ident_bf.ap(),
                         True, True, [WAb.k(pl), ident_bf.k()], [pk_])
              cp("act", Wtv[:, pr, :, :].rearrange("p b n -> p (b n)"), p_[:, :], [pk_], [Wt.k(pr)])
          AR.release(WAb)
          yield
          BAre = tmp(1024, "BAre"); BAim = tmp(1024, "BAim")
          ar_, ai_ = abc(0, 8)
          cmul("dve", v8(BAre), v8(BAim), BAre.k(), BAim.k(), ar_, ai_, br_, bi_, [Are.k(), Aim.k(), Bbre.k(), Bbim.k()], 1024, shape=sh8)
          yield
          yield
          CAre = tmp(1152, "CAre"); CAimN = tmp(1152, "CAimN")
          ar_, ai_ = abc(8, 9); br_, bi_ = bb(Cre, Cim, 9)
          cmul("dve", v9(CAre), v9(CAimN), CAre.k(), CAimN.k(), ar_, ai_, br_, bi_, [Are.k(), Aim.k(), Cre.k(), Cim.k()], 1152, neg_im=True, shape=sh9)
          yield
          yield
          cp("pool", Vtv[:, 0], v9(CAre)[:, :, 1:9, :], [CAre.k()], [Vt.k(0)])
          cp("pool", Vtv[:, 1], v9(CAimN)[:, :, 1:9, :], [CAimN.k()], [Vt.k(1)])
          for pr in range(4):
              p_, pk_ = ps_next()
              mm(p_[:, 0:256], v8(BAre)[:, pr, 0:4, :].rearrange("p a b -> p (a b)"), v9(CAre)[:, pr, 0:8, :].rearrange("p a b -> p (a b)"),
                 True, False, [BAre.k(), CAre.k()], [pk_])
              mm(p_[:, 0:256], v8(BAim)[:, pr, 0:4, :].rearrange("p a b -> p (a b)"), v9(CAimN)[:, pr, 0:8, :].rearrange("p a b -> p (a b)"),
                 False, True, [BAim.k(), CAimN.k()], [pk_])
              tt("dve", Ttv[:, pr, :], p_[:, 0:256], tmask.ap(F32), ALU.mult, [pk_, tmask.k()], [Tt.k(pr)])
          for b in (Bre, Bim, Cre, Cim, Bbre, Bbim, BAre, BAim, CAre, CAimN):
              AR.release(b)
          yield
          Esin = tmp(1024, "Esin"); Ecos = tmp(1024, "Ecos"); ang = tmp(1024, "ang")
          e3 = lambda b_: b_.ap(F32).rearrange("p (a c) -> p a c", c=256)
          tt("dve", e3(ang), ph.ap(F32)[:, p0:p0 + 4].unsqueeze(2).broadcast_to([128, 4, 256]),
             cvals.ap(F32).unsqueeze(1).broadcast_to([128, 4, 256]), ALU.mult, [ph.k(), cvals.k()], [ang.k()])
          reduce_angle(Esin.ap(F32), Esin.k(), ang.ap(F32), ang.k(), 1024, 0.0, None)
          yield
          yield
          reduce_angle(Ecos.ap(F32), Ecos.k(), ang.ap(F32), ang.k(), 1024, math.pi / 2, None)
          act(Esin.ap(F32), Esin.ap(F32), AF.Sin, [Esin.k()], [Esin.k()])
          act(Ecos.ap(F32), Ecos.ap(F32), AF.Sin, [Ecos.k()], [Ecos.k()])
          AR.release(ang)
          yield
          yield
          xsT = A(S_LEN * 2, "xsT")
          for tcb in range(4):
              p_, pk_ = ps_next()
              proj_fm(w_xs_v, w_xs_k, cc * 128, 128, tcb, p_, pk_)
              cp("act", xsT.ap()[:, tcb * 512:(tcb + 1) * 512], p_[:, :], [pk_], [xsT.k(tcb)])
          Xs = A(4 * 2 * 256 * 2, "Xs")
          yield
          Xsv = Xs.ap().rearrange("p (a h c) -> p a h c", h=2, c=256)
          xs_all = [xsT.k(t) for t in range(4)]
          for pw in range(4):
              p_, pk_ = ps_next()
              for hf in range(2):
                  for i4 in range(4):
                      i = hf * 4 + i4
                      mm(p_[:, hf * 256:(hf + 1) * 256], SEL(pw, i4), xsT.ap()[:, i:S_LEN:8], i4 == 0, i4 == 3,
                         xs_all + [sel_bf.k()], [pk_])
              cp("dve" if pw % 2 else "act", Xsv[:, pw, :, :].rearrange("p h c -> p (h c)"), p_[:, :], [pk_], [Xs.k(pw)])
          AR.release(xsT)
          yield
          Ybf = A(4 * 2 * 256 * 2, "Ybf")
          Ybv = Ybf.ap().rearrange("p (a h c) -> p a h c", h=2, c=256)
          stg = []
          for pw in range(4):
              pz, pzk = ps_next()
              for pl in range(2):
                  for kt in range(2):
                      mm(pz[:, pl * 256:(pl + 1) * 256], Wtv[:, pw, pl * 2 + kt, :], Xsv[:, pw, kt, :], kt == 0, kt == 1,
                         [Wt.k(pw), Xs.k(pw)], [pzk])
              zre = pz[:, 0:256]; zim = pz[:, 256:512]
              ec = e3(Ecos)[:, pw, :]; es = e3(Esin)[:, pw, :]
              t1 = tmp(256, "l2a"); t2 = tmp(256, "l2b"); gr = tmp(256, "gr"); gi = tmp(256, "gi")
              t3 = tmp(256, "l2c"); t4 = tmp(256, "l2d")
              tt("dve", t1.ap(F32), zre, ec, ALU.mult, [pzk, Ecos.k()], [t1.k()])
              tt("dve", t2.ap(F32), zim, es, ALU.mult, [pzk, Esin.k()], [t2.k()])
              tt("dve", t3.ap(F32), zim, ec, ALU.mult, [pzk, Ecos.k()], [t3.k()])
              tt("dve", t4.ap(F32), zre, es, ALU.mult, [pzk, Esin.k()], [t4.k()])
              tt("dve", gr.ap(F32), t1.ap(F32), t2.ap(F32), ALU.add, [t1.k(), t2.k()], [gr.k()])
              tt("dve", gi.ap(F32), t3.ap(F32), t4.ap(F32), ALU.subtract, [t3.k(), t4.k()], [gi.k()])
              for b in (t1, t2, t3, t4):
                  AR.release(b)
              stg.append(dict(gr=gr, gi=gi, ec=ec, es=es))
          yield
          for pw in range(4):
              d = stg[pw]; pair = p0 + pw
              Gr = tmp(256, "Gr"); Gi = tmp(256, "Gi")
              r8b = r8.ap(F32)[:, pair:pair + 1].broadcast_to([128, 256])
              S.add("dve", lambda h, Gr=Gr, gr=d["gr"], r8b=r8b: h.tensor_tensor_scan(out=Gr.ap(F32), data0=r8b, data1=gr.ap(F32), initial=0.0, op0=ALU.mult, op1=ALU.add),
                    reads=[d["gr"].k(), r8.k()], writes=[Gr.k()], cost=0.65)
              S.add("dve", lambda h, Gi=Gi, gi=d["gi"], r8b=r8b: h.tensor_tensor_scan(out=Gi.ap(F32), data0=r8b, data1=gi.ap(F32), initial=0.0, op0=ALU.mult, op1=ALU.add),
                    reads=[d["gi"].k(), r8.k()], writes=[Gi.k()], cost=0.65)
              d["Gr"] = Gr; d["Gi"] = Gi
              AR.release(d["gr"]); AR.release(d["gi"])
          yield
          for pw in range(4):
              d = stg[pw]
              Gr, Gi, ec, es = d["Gr"], d["Gi"], d["ec"], d["es"]
              t1 = tmp(256, "l3a"); t2 = tmp(256, "l3b"); t3 = tmp(256, "l3c"); t4 = tmp(256, "l3d")
              Hb = A(2 * 256 * 2, "Hb")
              Hbv = Hb.ap().rearrange("p (s c) -> p s c", c=256)
              memset("pool", Hbv[:, :, 0:1], 0.0, [Hb.k()])
              tt("dve", t1.ap(F32), Gr.ap(F32), ec, ALU.mult, [Gr.k(), Ecos.k()], [t1.k()])
              tt("dve", t2.ap(F32), Gi.ap(F32), es, ALU.mult, [Gi.k(), Esin.k()], [t2.k()])
              tt("dve", Hbv[:, 0, 1:256], t1.ap(F32)[:, 0:255], t2.ap(F32)[:, 0:255], ALU.subtract, [t1.k(), t2.k()], [Hb.k()])
              tt("dve", t3.ap(F32), Gr.ap(F32), es, ALU.mult, [Gr.k(), Esin.k()], [t3.k()])
              tt("dve", t4.ap(F32), Gi.ap(F32), ec, ALU.mult, [Gi.k(), Ecos.k()], [t4.k()])
              tt("dve", Hbv[:, 1, 1:256], t3.ap(F32)[:, 0:255], t4.ap(F32)[:, 0:255], ALU.add, [t3.k(), t4.k()], [Hb.k()])
              for b in (t1, t2, t3, t4, Gr, Gi):
                  AR.release(b)
              d["Hb"] = Hb; d["Hbv"] = Hbv
          yield
          yield
          for pw in range(4):
              d = stg[pw]; pair = p0 + pw
              Hb, Hbv = d["Hb"], d["Hbv"]
              py, pyk = ps_next()
              for mt in range(2):
                  o = py[:, mt * 256:(mt + 1) * 256]
                  jl = slice(mt * 4, mt * 4 + 4)
                  if mt == 0:
                      mm(o, Ttv[:, pw, 0:128], Xsv[:, pw, 0, :], True, False, [Tt.k(pw), Xs.k(pw)], [pyk])
                  else:
                      mm(o, Ttv[:, pw, 128:256], Xsv[:, pw, 0, :], True, False, [Tt.k(pw), Xs.k(pw)], [pyk])
                      mm(o, Ttv[:, pw, 0:128], Xsv[:, pw, 1, :], False, False, [Tt.k(pw), Xs.k(pw)], [pyk])
                  mm(o, Vtv[:, 0, pw, jl, :].rearrange("p j b -> p (j b)"), Hbv[:, 0, :], False, False, [Vt.k(0), Hb.k()], [pyk])
                  mm(o, Vtv[:, 1, pw, jl, :].rearrange("p j b -> p (j b)"), Hbv[:, 1, :], False, True, [Vt.k(1), Hb.k()], [pyk])
              AR.release(Hb)
              for mt in range(2):
                  stt("dve", Ybv[:, pw, mt, :], Xsv[:, pw, mt, :], dpair.ap(F32)[:, pair:pair + 1], py[:, mt * 256:(mt + 1) * 256],
                      ALU.mult, ALU.add, [Xs.k(pw), dpair.k(), pyk], [Ybf.k((pw, mt))])
          yield
          for j in range(8):
              p_, pk_ = ps_next()
              for pw in range(4):
                  mm(p_[:, 0:256], SEL(j % 4, pw), Ybv[:, pw, j // 4, :], pw == 0, pw == 3, [sel_bf.k(), Ybf.k((pw, j // 4))], [pk_])
              act(yT[cc].ap()[:, j:S_LEN:8], p_[:, 0:256], AF.Gelu, [pk_], [yT[cc].k()])
          for b in (Tt, Wt, Vt, Esin, Ecos, Xs, Ybf):
              AR.release(b)
          yield
    s5g = s5_gen()

    w_f32 = A(64 * 4, "w_f32"); dma(w_f32.ap(F32), w_fd, [], [w_f32.k()])
    w_f = A(64 * 2, "w_f"); cp("dve", w_f.ap(), w_f32.ap(F32), [w_f32.k()], [w_f.k()])
    AR.release(w_f32)
    w_f_v = w_f.ap().rearrange("p (k n) -> p k n", n=8); w_f_k = [w_f.k()]
    bfb = A(4, "bf"); dma(bfb.ap(F32)[0:8, :], b_f, [], [bfb.k()])
    nbf = A(4, "nbf")
    ts("dve", nbf.ap(F32)[0:8, :], bfb.ap(F32)[0:8, :], -1.0, None, ALU.mult, None, [bfb.k()], [nbf.k()])
    lg = A(S_LEN * 4, "lg"); cumf = A(S_LEN * 4, "cumf")
    for tcb in range(4):
        p_, pk_ = ps_next()
        proj_fm(w_f_v, w_f_k, 0, 8, tcb, p_, pk_)
        act(lg.ap(F32)[0:8, tcb * 512:(tcb + 1) * 512], p_[0:8, :], AF.Exp, [pk_, nbf.k()], [lg.k(tcb)], bias=nbf.ap(F32)[0:8, 0:1], scale=-1.0)
    one8 = A(4, "one8"); memset("pool", one8.ap(F32)[0:8, :], 1.0, [one8.k()])
    lgk = [lg.k(t) for t in range(4)]
    act(lg.ap(F32)[0:8, :], lg.ap(F32)[0:8, :], AF.Ln, lgk + [one8.k()], lgk, bias=one8.ap(F32)[0:8, 0:1])
    S.add("dve", lambda h: h.tensor_tensor_scan(out=cumf.ap(F32)[0:8, :], data0=one8.ap(F32)[0:8, 0:1].broadcast_to([8, S_LEN]),
                                                data1=lg.ap(F32)[0:8, :], initial=0.0, op0=ALU.mult, op1=ALU.subtract),
          reads=lgk + [one8.k()], writes=[cumf.k()], cost=4.5)
    CF = A(3 * S_LEN * 2, "CF"); NCF = A(3 * S_LEN * 2, "NCF")
    CFv = CF.ap().rearrange("p (a n) -> p a n", n=S_LEN); NCFv = NCF.ap().rearrange("p (a n) -> p a n", n=S_LEN)
    r1 = A(S_LEN * 4, "r1")
    cp("dve", CFv[0:8, 0, :], cumf.ap(F32)[0:8, :], [cumf.k()], [CF.k(0)])
    tt("dve", r1.ap(F32)[0:8, :], cumf.ap(F32)[0:8, :], CFv[0:8, 0, :], ALU.subtract, [cumf.k(), CF.k(0)], [r1.k()])
    cp("dve", CFv[0:8, 1, :], r1.ap(F32)[0:8, :], [r1.k()], [CF.k(1)])
    tt("dve", lg.ap(F32)[0:8, :], r1.ap(F32)[0:8, :], CFv[0:8, 1, :], ALU.subtract, [r1.k(), CF.k(1)], lgk)
    cp("dve", CFv[0:8, 2, :], lg.ap(F32)[0:8, :], lgk, [CF.k(2)])
    ts("dve", NCF.ap()[0:8, :], CF.ap()[0:8, :], -1.0, None, ALU.mult, None, [CF.k(0), CF.k(1), CF.k(2)], [NCF.k()])
    dump("cumf", cumf.ap(F32)[0:8, :], [8, S_LEN], cumf.k())
    cf_scr = nc.dram_tensor("cf_scr", [8, 6, S_LEN], BF16, kind="Internal").ap()
    dma(cf_scr[:, 0:3, :], CFv[0:8, :, :], [CF.k(0), CF.k(1), CF.k(2)], ["cf_scr"])
    dma(cf_scr[:, 3:6, :], NCFv[0:8, :, :], [NCF.k()], ["cf_scr"])
    for b in (lg, r1, cumf, w_f, one8, bfb, nbf, CF, NCF):
        AR.release(b)
    oz = [A(S_LEN * 2, "oz%d" % i) for i in range(4)]
    ps_mode[0] = "gen"
    kt_count = [0]
    for _ in range(PRETICK):
        next(s5g, None)
    for hc in range(4):
        (w_q, w_q_v, w_q_k), (w_k, w_k_v, w_k_k), (w_za, w_za_v, w_za_k), (w_v, w_v_v, w_v_k) = hc_w
        Vaug = A(16 * 2 * 128 * 2, "Vaug")
        Vv = Vaug.ap().rearrange("p (t e n) -> p t e n", e=2, n=128)
        memset("pool", Vaug.ap(), 1.0, [Vaug.k()])
        for g4 in range(4):
            p_, pk_ = ps_next()
            for t4 in range(4):
                tti = g4 * 4 + t4
                for kc in range(8):
                    mm(p_[:, t4 * 128:(t4 + 1) * 128], uT[kc].ap()[:, tti * 128:(tti + 1) * 128], w_v_v[:, kc, :], kc == 0, kc == 7,
                       list(w_v_k) + [uT[kc].k(g4)], [pk_])
            pv = p_[:, :].rearrange("p (t n) -> p t n", n=128)
            cp("dve", Vv[:, g4 * 4:(g4 + 1) * 4, 0, 0:64], pv[:, :, 0:64], [pk_], [Vaug.k()])
            cp("dve", Vv[:, g4 * 4:(g4 + 1) * 4, 1, 64:128], pv[:, :, 64:128], [pk_], [Vaug.k()])
        QA = [A(S_LEN * 2, "QA%d" % e) for e in range(2)]
        KA = [A(S_LEN * 2, "KA%d" % e) for e in range(2)]
        for e in range(2):
            h_ = hc * 2 + e
            memset("pool", QA[e].ap()[64:70, :], 1.0, [QA[e].k("x")])
            memset("pool", KA[e].ap()[64:70, :], 1.0, [KA[e].k("x")])
            dma(QA[e].ap()[64:67, :], cf_scr[h_, 0:3, :], ["cf_scr"], [QA[e].k("x")])
            dma(KA[e].ap()[67:70, :], cf_scr[h_, 3:6, :], ["cf_scr"], [KA[e].k("x")])
        for tcb in range(4):
            p_, pk_ = ps_next()
            proj_fm(w_q_v, w_q_k, 0, 128, tcb, p_, pk_)
            for e in range(2):
                act(QA[e].ap()[0:64, tcb * 512:(tcb + 1) * 512], p_[e * 64:(e + 1) * 64, :], AF.Identity, [pk_], [QA[e].k(tcb)], scale=0.125)
            p_, pk_ = ps_next()
            proj_fm(w_k_v, w_k_k, 0, 128, tcb, p_, pk_)
            for e in range(2):
                cp("dve", KA[e].ap()[0:64, tcb * 512:(tcb + 1) * 512], p_[e * 64:(e + 1) * 64, :], [pk_], [KA[e].k(tcb)])
        if hc == 0:
            dump("QA", QA[0].ap(), [128, S_LEN], QA[0].k(3))
            dump("KA", KA[0].ap(), [128, S_LEN], KA[0].k(3))
        if hc + 1 < 4:
            hc_w = load_hc(hc + 1)
        else:
            w_g_, w_g_v, w_g_k = load_w(w_glu, 0, 8, 4, "w_glu")
            w_z, w_z_v, w_z_k = load_w(w_in, 4, 4, 8, "w_zs")
        steps = [(qc, e, kt) for qc in range(4) for e in range(2) for kt in range(4 * qc + 4)]
        qk = {}

        def issue_qk(si):
            qc, e, kt = steps[si]
            q0 = qc * 512
            d_ = kt - 4 * qc
            coff = max(0, d_) * 128
            N = 512 - coff
            p_, pk_ = ps_qk()
            mm(p_[:, 0:N], KA[e].ap()[0:70, kt * 128:(kt + 1) * 128], QA[e].ap()[0:70, q0 + coff:q0 + 512], True, d_ < 0,
               [KA[e].k("x"), KA[e].k(kt // 4), QA[e].k("x"), QA[e].k(qc)], [pk_])
            if d_ >= 0:
                mm(p_[:, 0:128], ident_bf.ap(), cmask_bf.ap(), False, True, [ident_bf.k(), cmask_bf.k()], [pk_])
            qk[si] = (p_, pk_, coff, N)
        issue_qk(0); issue_qk(1); issue_qk(2)
        osbs = {}
        po_dd = {}
        for si, (qc, e, kt) in enumerate(steps):
            q0 = qc * 512
            nkt = 4 * qc + 4
            pacc, pacck = psum[6 + e], ("ps", 6 + e)
            if (qc, 0) not in osbs and e == 0 and kt == 0:
                osbs[qc] = tmp(512, "osb")
            osb = osbs[qc]
            p_, pk_, coff, N = qk.pop(si)
            PT = A(512 * 2, "PT")
            act(PT.ap()[:, 0:N], p_[:, 0:N], AF.Exp, [pk_], [PT.k()])
            if si + 3 < len(steps):
                issue_qk(si + 3)
            mm(pacc[:, coff:512], Vv[:, kt, e, :], PT.ap()[:, 0:N], kt == 0, kt == nkt - 1, [Vaug.k(), PT.k()], [pacck])
            AR.release(PT)
            kt_count[0] += 1
            if kt_count[0] % TICK == 0:
                next(s5g, None)
            if kt == nkt - 1:
                lo, hi = (0, 64) if e == 0 else (64, 128)
                dl, dh = (64, 128) if e == 0 else (0, 64)
                if e == 0:
                    po_dd[qc] = (tmp(512, "po"), tmp(512, "dd"))
                po, dd = po_dd[qc]
                act(po.ap(F32)[lo:hi, :], pacc[lo:hi, :], AF.Identity, [pacck], [po.k(e)])
                cp("dve" if DDV else "act", dd.ap(F32)[lo:hi, :], pacc[dl:dh, :], [pacck], [dd.k(e)])
                if e == 1:
                    S.add("dve", lambda h, dd=dd: h.reciprocal(out=dd.ap(F32), in_=dd.ap(F32)),
                          reads=[dd.k(0), dd.k(1)], writes=[dd.k(0), dd.k(1)], cost=3.4)
                    tt("dve", osb.ap(F32), po.ap(F32), dd.ap(F32), ALU.mult, [po.k(0), po.k(1), dd.k(0), dd.k(1)], [osb.k(0), osb.k(1)])
                    AR.release(po); AR.release(dd)
                if e == 1:
                    pz, pzk = ps_next()
                    proj_fm(w_za_v, w_za_k, 0, 128, qc, pz, pzk)
                    sz = tmp(512, "sza")
                    act(sz.ap(F32), pz[:, :], AF.Tanh, [pzk], [sz.k()], scale=0.5)
                    stt("dve", sz.ap(F32), sz.ap(F32), 1.0, pz[:, :], ALU.add, ALU.mult, [sz.k(), pzk], [sz.k()])
                    stt("dve", oz[hc].ap()[:, q0:q0 + 512], sz.ap(F32), 0.5, osb.ap(F32), ALU.mult, ALU.mult,
                        [osb.k(0), osb.k(1), sz.k()], [oz[hc].k(qc)])
                    AR.release(osb); AR.release(sz)
        for b in QA + KA + [w_q, w_k, w_za, w_v, Vaug]:
            AR.release(b)
    for _ in s5g:
        pass
    ps_mode[0] = "all"
    for hc in range(4):
        dump("oz%d" % hc, oz[hc].ap(), [128, S_LEN], oz[hc].k(3))
    AR.release(w_xs)
    for cc in range(4):
        dump("yT%d" % cc, yT[cc].ap(), [128, S_LEN], yT[cc].k())

    s5o = [A(S_LEN * 2, "s5o%d" % i) for i in range(4)]
    for fc in range(4):
        for tcb in range(4):
            pa, pak = ps_next(); pb, pbk = ps_next(); pz, pzk = ps_next()
            proj_fm(w_g_v, w_g_k, fc * 128, 128, tcb, pa, pak, kcn=4, src=yT, srckeys=lambda kc, t: yT[kc].k())
            proj_fm(w_g_v, w_g_k, 512 + fc * 128, 128, tcb, pb, pbk, kcn=4, src=yT, srckeys=lambda kc, t: yT[kc].k())
            proj_fm(w_z_v, w_z_k, fc * 128, 128, tcb, pz, pzk)
            sb_ = tmp(512, "sgb"); sz = tmp(512, "sz"); t_ = tmp(512, "glt")
            act(sb_.ap(F32), pb[:, :], AF.Sigmoid, [pbk], [sb_.k()])
            act(sz.ap(F32), pz[:, :], AF.Silu, [pzk], [sz.k()])
            tt("dve", t_.ap(F32), pa[:, :], sb_.ap(F32), ALU.mult, [pak, sb_.k()], [t_.k()])
            tt("pool", s5o[fc].ap()[:, tcb * 512:(tcb + 1) * 512], t_.ap(F32), sz.ap(F32), ALU.mult, [t_.k(), sz.k()], [s5o[fc].k(tcb)])
            for b in (sb_, sz, t_):
                AR.release(b)
    AR.release(w_g_); AR.release(w_z)
    for b in yT:
        AR.release(b)
    for fc in range(4):
        dump("s5o%d" % fc, s5o[fc].ap(), [128, S_LEN], s5o[fc].k(3))


    gate_row = A(1024 * 4, "gate_row")
    brow = A(1024 * 4, "brow")
    dma(brow.ap(F32)[0:1, :], b_ada_row[0:1, 2048:3072], [], [brow.k()])
    for hf in range(2):
        sg = A(8 * 512 * 4, "wada_stg")
        sv = sg.ap(F32).rearrange("p (k n) -> p k n", n=512)
        dma(sg.ap(F32).rearrange("p (j c) -> p j c", j=4),
            w_ada[2048 + hf * 512:2048 + (hf + 1) * 512, :].rearrange("(j p) c -> p j c", p=128), [], [sg.k()])
        s16 = A(8 * 512 * 2, "wada_bf")
        s16v = s16.ap().rearrange("p (k n) -> p k n", n=512)
        cp("dve" if hf % 2 == 0 else "act", s16v.rearrange("p k (j n) -> p k j n", j=4),
           sg.ap(F32).rearrange("p (j k n) -> p k j n", j=4, k=8), [sg.k()], [s16.k()])
        AR.release(sg)
        pg, pgk = ps_next()
        for kc in range(8):
            mm(pg[0:1, :], cact_bf.ap()[:, kc:kc + 1], s16v[:, kc, :], kc == 0, kc == 7, [s16.k(), cact_bf.k()], [pgk])
        AR.release(s16)
        tt("dve", gate_row.ap(F32)[0:1, hf * 512:(hf + 1) * 512], pg[0:1, :], brow.ap(F32)[0:1, hf * 512:(hf + 1) * 512], ALU.add,
           [pgk, brow.k()], [gate_row.k(hf)])
    AR.release(brow)
    ones_row = A(128 * 4, "ones_row")
    memset("pool", ones_row.ap(F32)[0:1, :], 1.0, [ones_row.k()])
    gate_b = A(1024 * 4, "gate_b")
    for hf in range(2):
        p_, pk_ = ps_next()
        mm(p_[:, :], ones_row.ap(F32)[0:1, :], gate_row.ap(F32)[0:1, hf * 512:(hf + 1) * 512], True, True,
           [ones_row.k(), gate_row.k(hf)], [pk_])
        cp("dve", gate_b.ap(F32)[:, hf * 512:(hf + 1) * 512], p_[:, :], [pk_], [gate_b.k()])
    AR.release(gate_row); AR.release(ones_row)
    lng = A(1024 * 4, "lng"); lnb = A(1024 * 4, "lnb")
    dma(lng.ap(F32), ln_g[0:1, :].broadcast_to([128, D]), [], [lng.k()])
    dma(lnb.ap(F32), ln_b[0:1, :].broadcast_to([128, D]), [], [lnb.k()])
    wo = A(8 * 1024 * 2, "w_out")
    wov = wo.ap().rearrange("p (k n) -> p k n", n=1024)
    wok = []
    for jj in range(0, 8, 4):
        sg = A(4 * 8 * 128 * 4, "w_out_stg")
        dma(sg.ap(F32).rearrange("p (j c) -> p j c", j=4),
            w_out[jj * 128:(jj + 4) * 128, :].rearrange("(j p) c -> p j c", p=128), [], [sg.k()])
        for j in range(4):
            c0 = (jj + j) * 128
            tt("dve", wov[:, :, c0:c0 + 128],
               sg.ap(F32).rearrange("p (j k n) -> p j k n", j=4, k=8)[:, j, :, :],
               gate_b.ap(F32)[:, c0:c0 + 128].unsqueeze(1).broadcast_to([128, 8, 128]), ALU.mult,
               [sg.k(), gate_b.k()], [wo.k((jj, j))])
            wok.append(wo.k((jj, j)))
        AR.release(sg)
    mg = [A(S_LEN * 2, "mg%d" % i) for i in range(8)]
    def load_fc(fc):
        return (load_w(w_ps, fc, 1, 4, "w_ps", eng="dve"), load_w(w_pa, fc, 1, 4, "w_pa", eng="dve"),
                load_w(w_in, 24 + fc, 1, 8, "w_gs", eng="dve"), load_w(w_in, 32 + fc, 1, 8, "w_ga", eng="dve"))
    fc_w = load_fc(0)
    for fc in range(8):
        (w1, w1v, w1k), (w2, w2v, w2k), (wgs, wgsv, wgsk), (wga, wgav, wgak) = fc_w
        if fc + 1 < 8 and PREFETCH_FC:
            fc_w = load_fc(fc + 1)
        for tcb in range(4):
            p1, p1k = ps_next(); p2, p2k = ps_next(); p3, p3k = ps_next(); p4, p4k = ps_next()
            proj_fm(w1v, w1k, 0, 128, tcb, p1, p1k, kcn=4, src=s5o, srckeys=lambda kc, t: s5o[kc].k(t))
            proj_fm(w2v, w2k, 0, 128, tcb, p2, p2k, kcn=4, src=oz, srckeys=lambda kc, t: oz[kc].k(t))
            proj_fm(wgsv, wgsk, 0, 128, tcb, p3, p3k)
            proj_fm(wgav, wgak, 0, 128, tcb, p4, p4k)
            s1 = tmp(512, "sg1"); s2 = tmp(512, "sg2")
            act(s1.ap(F32), p3[:, :], AF.Sigmoid, [p3k], [s1.k()])
            act(s2.ap(F32), p4[:, :], AF.Sigmoid, [p4k], [s2.k()])
            tt("dve", s1.ap(F32), p1[:, :], s1.ap(F32), ALU.mult, [p1k, s1.k()], [s1.k()])
            tt("dve", s2.ap(F32), p2[:, :], s2.ap(F32), ALU.mult, [p2k, s2.k()], [s2.k()])
            tt("pool" if tcb % 2 else "dve", mg[fc].ap()[:, tcb * 512:(tcb + 1) * 512], s1.ap(F32), s2.ap(F32), ALU.add,
               [s1.k(), s2.k()], [mg[fc].k(tcb)])
            for b in (s1, s2):
                AR.release(b)
        for b in (w1, w2, wgs, wga):
            AR.release(b)
        if fc + 1 < 8 and not PREFETCH_FC:
            fc_w = load_fc(fc + 1)
    for b in s5o + oz + uT:
        AR.release(b)
    for tcb in range(4):
        xts = []; pres = []
        for t4 in range(4):
            tti = tcb * 4 + t4
            xt = A(1024 * 4, "xt2")
            dma(xt.ap(F32), x[tti * 128:(tti + 1) * 128, :], [], [xt.k()])
            pre = A(1024 * 4, "pre")
            for hf in range(2):
                p_, pk_ = ps_next()
                for kc in range(8):
                    mm(p_[:, :], mg[kc].ap()[:, tti * 128:(tti + 1) * 128], wov[:, kc, hf * 512:(hf + 1) * 512], kc == 0, kc == 7,
                       list(wok) + [mg[kc].k(tcb)], [pk_])
                stt("dve", pre.ap(F32)[:, hf * 512:(hf + 1) * 512], xt.ap(F32)[:, hf * 512:(hf + 1) * 512], ALPHA, p_[:, :],
                    ALU.mult, ALU.add, [xt.k(), pk_], [pre.k(hf)])
            xts.append(xt); pres.append(pre)
        stats = layer_norm_stats_multi([(pre.ap(F32), [pre.k(0), pre.k(1)]) for pre in pres], "b")
        for t4 in range(4):
            xt = xts[t4]; pre = pres[t4]; rstd, nmr = stats[t4]
            act(xt.ap(F32), pre.ap(F32), AF.Identity, [pre.k(0), pre.k(1), rstd.k(), nmr.k()], [xt.k()],
                bias=nmr.ap(F32)[:, 0:1], scale=rstd.ap(F32)[:, 0:1])
        for t4 in range(4):
            xt = xts[t4]
            tt("dve" if (tcb == 3 or t4 % 2) else "pool", xt.ap(F32), xt.ap(F32), lng.ap(F32), ALU.mult, [xt.k(), lng.k()], [xt.k()])
        for t4 in range(4):
            tti = tcb * 4 + t4
            xt = xts[t4]; pre = pres[t4]; rstd, nmr = stats[t4]
            tt("dve", pre.ap(F32), xt.ap(F32), lnb.ap(F32), ALU.add, [xt.k(), lnb.k()], [pre.k(0), pre.k(1)])
            dma(out[tti * 128:(tti + 1) * 128, :], pre.ap(F32), [pre.k(0), pre.k(1)], [])
            for b in (xt, pre, rstd, nmr):
                AR.release(b)

    S.emit(nc)
    st.close()
    return nc, dbg_out, AR.peak


def _host_consts():
    ident = np.eye(128, dtype=np.float32)
    sel = np.zeros((128, 16, 128), np.float32)
    for a in range(4):
        for b in range(4):
            for r in range(32):
                sel[a * 32 + r, a * 4 + b, b * 32 + r] = 1.0
    kk = np.arange(128)
    cmask = np.where(kk[None, :] >= kk[:, None], 0.0, -30000.0).astype(np.float32)
    tmask = np.zeros((128, 256), np.float32)
    for i4 in range(4):
        for j in range(8):
            if j >= i4:
                tmask[i4 * 32:(i4 + 1) * 32, j * 32:(j + 1) * 32] = 1.0
    kv = np.array([0, -1, -2, -3, -4, -5, -6, -7] + list(range(9)) + [7, 6, 5, 4, 3, 2, 1, 0], np.float32)
    kvals = np.tile(kv[None, :], (128, 1))
    cvals = np.tile(np.arange(256, dtype=np.float32)[None, :], (128, 1))
    return dict(ident=ident, sel=sel.reshape(128, 16 * 128), cmask=cmask, tmask=tmask, kvals=kvals, cvals=cvals)


def _pair_layout_vec(v):
    return np.ascontiguousarray(v.reshape(16, 2, 64).transpose(1, 2, 0).reshape(128, 16))


def _blk(m):
    o = np.zeros((2, 64, 16, 2, 16), np.float32)
    mm_ = m.reshape(16, 2, 64, 16)
    for gg in range(2):
        o[gg, :, :, gg, :] = mm_[:, gg].transpose(1, 0, 2)
    return np.ascontiguousarray(o.reshape(128, 16 * 32))


def _make_in_maps(inp):
    f = lambda a: np.ascontiguousarray(np.asarray(a, dtype=np.float32))
    consts = _host_consts()
    shared = dict(consts)
    def blockify(W):
        K, N = W.shape
        return np.ascontiguousarray(W.reshape(K // 128, 128, N // 128, 128).transpose(2, 1, 0, 3).reshape(N, K))
    w_in_full = f(inp["w_in"][0])
    segs = [(0, 512), (512, 1024), (1024, 1536), (1536, 2048), (2048, 2560), (2568, 3080), (3080, 4104), (4104, 5128)]
    shared["w_in"] = np.concatenate([blockify(w_in_full[:, a:b]) for a, b in segs], axis=0)
    shared["w_f"] = np.ascontiguousarray(w_in_full[:, 2560:2568].reshape(8, 128, 8).transpose(1, 0, 2).reshape(128, 64))
    shared["w_ada"] = blockify(f(inp["w_ada"][0]))
    shared["b_adaT"] = f(inp["b_ada"][0].reshape(24, 128).T)
    shared["b_ada_row"] = f(inp["b_ada"][0].reshape(1, 3 * D))
    shared["b_f"] = f(inp["b_f"][0].reshape(8, 1))
    shared["lam_re"] = _pair_layout_vec(f(inp["lam_re"][0]))
    shared["lam_im"] = _pair_layout_vec(f(inp["lam_im"][0]))
    shared["ldt"] = _pair_layout_vec(np.repeat(f(inp["log_dt"][0])[:, None], 64, axis=1))
    shared["Bre"] = _blk(f(inp["ssm_b_re"][0])); shared["Bim"] = _blk(f(inp["ssm_b_im"][0]))
    shared["Cre"] = _blk(f(inp["ssm_c_re"][0]).transpose(0, 2, 1)); shared["Cim"] = _blk(f(inp["ssm_c_im"][0]).transpose(0, 2, 1))
    dvec = f(inp["ssm_d"][0]).reshape(16, 32)
    shared["dpair"] = np.ascontiguousarray(np.tile(dvec.T, (4, 1)))
    shared["w_glu"] = blockify(f(inp["w_glu"][0])); shared["w_ps"] = blockify(f(inp["w_proj_ssm"][0]))
    shared["w_pa"] = blockify(f(inp["w_proj_attn"][0]))
    shared["w_out"] = blockify(f(inp["w_out"][0])); shared["ln_g"] = f(inp["ln_g"][0].reshape(1, D)); shared["ln_b"] = f(inp["ln_b"][0].reshape(1, D))
    maps = []
    xs = f(inp["x"]); cs = f(inp["c"])
    for b in range(8):
        m = dict(shared)
        m["x"] = xs[b]
        m["cT"] = np.ascontiguousarray(cs[b].reshape(8, 128).T)
        maps.append(m)
    return maps


_CACHE = {}


def kernel(**inputs):
    if "nc" not in _CACHE:
        _CACHE["nc"] = build()[0]
    nc = _CACHE["nc"]
    maps = _make_in_maps(inputs)
    res = run_bass_kernel_spmd(nc, maps, core_ids=list(range(8)))
    return np.stack([np.asarray(r["out"], dtype=np.float32) for r in res.results], axis=0)
```

```python
import contextlib
import math
import numpy as np
import concourse.bass as bass
import concourse.mybir as mybir
from concourse.bass_utils import run_bass_kernel_spmd

F32 = mybir.dt.float32
BF16 = mybir.dt.bfloat16
I32 = mybir.dt.int32
ALU = mybir.AluOpType
AF = mybir.ActivationFunctionType

COMPUTE = ("pe", "act", "dve", "pool")
DMAQ = ("sp",)
ENGS = COMPUTE + DMAQ
N_DMA_SEMS = 32
import os
PREFETCH_FC = os.environ.get("PF", "1") == "1"
TICK = int(os.environ.get("TICK", "4"))
UTA = int(os.environ.get("UTA", "2"))
DDV = os.environ.get("DDV", "0") == "1"
PRETICK = int(os.environ.get("PRETICK", "0"))

S_LEN = 2048
D = 1024
NKV = 25
ALPHA = 2.0 ** 0.25
LN_EPS = 1e-5
TWO_PI = 2.0 * math.pi
CW_C1 = 6.28125
CW_C2 = TWO_PI - CW_C1


class Op:
    __slots__ = ("eng", "fn", "preds", "idx", "gid", "signal", "is_dma", "dsem", "dval", "dprev",
                 "cost", "tag", "nbytes", "prio", "nsucc", "succs", "npend", "ready", "fin", "pos")

    def __init__(self, eng, fn, idx, gid, is_dma, cost, tag, nbytes):
        self.eng = eng; self.fn = fn; self.idx = idx; self.gid = gid
        self.preds = set()
        self.signal = False; self.is_dma = is_dma
        self.dsem = None; self.dval = None; self.dprev = None
        self.cost = cost; self.tag = tag; self.nbytes = nbytes
        self.prio = 0.0; self.succs = []; self.npend = 0; self.ready = 0.0; self.fin = 0.0; self.pos = -1


ACT_SETS = {"Exp": 0, "Tanh": 0, "Identity": -1, "Copy": -1, "Gelu": 1, "Sigmoid": 2, "Silu": 3, "Sin": 4, "Sqrt": 5, "Ln": 6}
XLAT = float(os.environ.get("XLAT", "0.5"))
DMA_BW = float(os.environ.get("DMABW", "400e3"))
DMA_LAT = 2.0


class Sched:
    def __init__(self):
        self.ops = {e: [] for e in ENGS}
        self.all = []
        self.last_write = {}
        self.reads = {}
        self.dma_counts = [0] * N_DMA_SEMS
        self.reorder = True

    def add(self, eng, fn, reads=(), writes=(), cost=0.1, tag=None, nbytes=0):
        op = Op(eng, fn, len(self.ops[eng]), len(self.all), eng in DMAQ, cost, tag, nbytes)
        for b in reads:
            if b in self.last_write:
                op.preds.add(self.last_write[b])
        for b in writes:
            if b in self.last_write:
                op.preds.add(self.last_write[b])
            for r in self.reads.get(b, ()):
                op.preds.add(r)
        op.preds.discard(op)
        self.ops[eng].append(op)
        self.all.append(op)
        for b in reads:
            self.reads.setdefault(b, []).append(op)
        for b in writes:
            self.last_write[b] = op
            self.reads[b] = []
        return op

    def schedule(self):
        ops = self.all
        for op in ops:
            op.succs = []
        for op in ops:
            for p in op.preds:
                p.succs.append(op)
        for op in reversed(ops):
            m = 0.0
            for s_ in op.succs:
                if s_.prio > m:
                    m = s_.prio
            op.prio = op.cost + m + (DMA_LAT if op.is_dma else 0.0)
        if not self.reorder:
            return {e: list(self.ops[e]) for e in ENGS}
        for op in ops:
            op.npend = len(op.preds); op.ready = 0.0
        cand = {e: [] for e in ENGS}
        for op in ops:
            if op.npend == 0:
                cand[op.eng].append(op)
        free = {e: 0.0 for e in ENGS}
        last_set = [None]
        dma_free = [0.0]
        final = {e: [] for e in ENGS}
        n_left = len(ops)
        while n_left:
            best = None; bkey = None
            for e in ENGS:
                cl = cand[e]
                if not cl:
                    continue
                fe = free[e]
                for op in cl:
                    st_ = op.ready if op.ready > fe else fe
                    pen = 0.0
                    if e == "act" and op.tag is not None:
                        ts_ = ACT_SETS.get(op.tag, 9)
                        if ts_ >= 0 and last_set[0] is not None and ts_ != last_set[0]:
                            pen = 1.3
                    key = (st_ + pen, -op.prio, op.gid)
                    if bkey is None or key < bkey:
                        bkey = key; best = op
            op = best
            e = op.eng
            st_ = max(op.ready, free[e])
            if e == "act" and op.tag is not None:
                ts_ = ACT_SETS.get(op.tag, 9)
                if ts_ >= 0:
                    if last_set[0] is not None and ts_ != last_set[0]:
                        st_ += 1.3
                    last_set[0] = ts_
            if op.is_dma:
                free[e] = st_ + 0.06
                t0 = max(st_ + DMA_LAT, dma_free[0])
                op.fin = t0 + op.nbytes / DMA_BW
                dma_free[0] = op.fin
            else:
                op.fin = st_ + op.cost
                free[e] = op.fin
            cand[e].remove(op)
            op.pos = len(final[e]); final[e].append(op)
            n_left -= 1
            for s_ in op.succs:
                lat = op.fin + (0.0 if (s_.eng == e and not op.is_dma) else XLAT)
                if lat > s_.ready:
                    s_.ready = lat
                s_.npend -= 1
                if s_.npend == 0:
                    cand[s_.eng].append(s_)
        self.makespan = max(op.fin for op in ops)
        return final

    def emit(self, nc):
        final = self.schedule()
        for e in ENGS:
            for i, op in enumerate(final[e]):
                op.pos = i
        rr = 0
        for op in final["sp"]:
            s_ = rr; rr = (rr + 1) % N_DMA_SEMS
            op.dsem = s_; op.dprev = self.dma_counts[s_]
            self.dma_counts[s_] += 16; op.dval = self.dma_counts[s_]
        for op in self.all:
            for p in op.preds:
                if p.eng == "pe" and op.eng == "pe":
                    continue
                p.signal = True
        sigcount = {}
        for e in COMPUTE:
            c = 0; arr = []
            for op in final[e]:
                if op.signal:
                    c += 1
                arr.append(c)
            sigcount[e] = arr
        with contextlib.ExitStack() as st:
            sems = {e: st.enter_context(nc.semaphore("s_" + e)) for e in COMPUTE}
            dsems = [st.enter_context(nc.semaphore("d%d" % i)) for i in range(N_DMA_SEMS)]
            block = st.enter_context(nc.Block())
            sched = self

            def run(e, h):
                waited = {}
                for op in final[e]:
                    need = {}
                    for p in op.preds:
                        if p.is_dma:
                            wk = ("d", p.dsem)
                            if need.get(wk, -1) < p.dval:
                                need[wk] = p.dval
                        else:
                            if p.eng == "pe" and e == "pe":
                                continue
                            v = sigcount[p.eng][p.pos]
                            if need.get(p.eng, -1) < v:
                                need[p.eng] = v
                    if op.is_dma and op.dprev > 0:
                        wk = ("d", op.dsem)
                        if need.get(wk, -1) < op.dprev:
                            need[wk] = op.dprev
                    for wk, v in need.items():
                        if waited.get(wk, -1) < v:
                            if isinstance(wk, tuple):
                                h.wait_ge(dsems[wk[1]], v)
                            else:
                                h.wait_ge(sems[wk], v)
                            waited[wk] = v
                    ins = op.fn(h)
                    if op.is_dma:
                        ins.then_inc(dsems[op.dsem], 16)
                    elif op.signal:
                        ins.then_inc(sems[e], 1)
                if e == "sp":
                    for s_ in range(N_DMA_SEMS):
                        if sched.dma_counts[s_] > 0 and waited.get(("d", s_), -1) < sched.dma_counts[s_]:
                            h.wait_ge(dsems[s_], sched.dma_counts[s_])

            @block.tensor
            def _(h):
                run("pe", h)

            @block.scalar
            def _(h):
                run("act", h)

            @block.vector
            def _(h):
                run("dve", h)

            @block.gpsimd
            def _(h):
                run("pool", h)

            @block.sync
            def _(h):
                run("sp", h)


class Buf:
    _uid = 0

    def __init__(self, arena, off, nbytes, prior, name):
        Buf._uid += 1
        self.uid = Buf._uid
        self.arena = arena; self.off = off; self.nbytes = nbytes; self.asize = (nbytes + 63) // 64 * 64
        self.prior = prior; self.keys = set(); self.name = name

    def k(self, sub=None):
        key = (self.uid, sub)
        if key not in self.keys:
            self.keys.add(key)
            self.arena.S.reads[key] = list(self.prior)
        return key

    def ap(self, dt=BF16):
        a = self.arena.t[:, self.off // 2:(self.off + self.nbytes) // 2]
        if dt != BF16:
            a = a.bitcast(dt)
        return a


class Arena:
    def __init__(self, S, tensor, nbytes):
        self.S = S; self.t = tensor
        self.free = [[0, nbytes, [], 0]]
        self.peak = 0; self.used = 0; self.clock = 0

    LONG = {"uT", "yT", "s5o", "oz", "mg", "Vaug", "QA", "KA", "w_xs", "w_out", "Tt", "Wt", "Vt", "Esin", "Ecos",
            "Xs", "Ybf", "xsT", "ident", "ident_bf", "sel_bf", "cmask_bf", "tmask", "kvals", "cvals", "lre", "lim",
            "ldt", "dpair", "badaT", "cT", "Are", "Aim", "ph", "r8", "cr", "ci", "gate_b", "lng", "lnb", "modT", "sc1",
            "eps", "cact", "cact_bf", "w_q", "w_k", "w_za", "w_v", "w_glu", "w_zs", "w_ps", "w_pa", "w_gs", "w_ga"}

    def alloc(self, nbytes, name=""):
        req = nbytes
        nbytes = (nbytes + 63) // 64 * 64
        longlived = name.rstrip("0123456789") in self.LONG
        fr = self.free
        best = None; bkey = None
        for i in range(len(fr)):
            tot = 0; mx = 0; j = i
            while True:
                tot += fr[j][1]; mx = max(mx, fr[j][3])
                if tot >= nbytes:
                    if longlived:
                        key = (-(fr[j][0] + fr[j][1]), 0)
                    else:
                        key = (mx, fr[i][0])
                    if bkey is None or key < bkey:
                        bkey = key; best = (i, j)
                    break
                if j + 1 < len(fr) and fr[j][0] + fr[j][1] == fr[j + 1][0]:
                    j += 1
                else:
                    break
        if best is None:
            raise RuntimeError("arena OOM for %s (%d bytes), used %d" % (name, nbytes, self.used))
        i, j = best
        acc = set()
        need = nbytes
        if longlived:
            end = fr[j][0] + fr[j][1]
            off = end - nbytes
            newfree = fr[:i]
            mid = []
            for k in range(j, i - 1, -1):
                seg = fr[k]
                if need <= 0:
                    mid.append(seg); continue
                acc.update(seg[2])
                if seg[1] <= need:
                    need -= seg[1]
                else:
                    mid.append([seg[0], seg[1] - need, seg[2], seg[3]])
                    need = 0
            newfree += list(reversed(mid))
            newfree += fr[j + 1:]
        else:
            off = fr[i][0]
            newfree = fr[:i]
            for k in range(i, j + 1):
                seg = fr[k]
                if need <= 0:
                    newfree.append(seg); continue
                acc.update(seg[2])
                if seg[1] <= need:
                    need -= seg[1]
                else:
                    newfree.append([seg[0] + need, seg[1] - need, seg[2], seg[3]])
                    need = 0
            newfree += fr[j + 1:]
        self.free = newfree
        self.used += nbytes
        self.peak = max(self.peak, self.used)
        return Buf(self, off, req, list(acc), name)

    def release(self, b):
        acc = list(b.prior) if not b.keys else []
        for key in b.keys:
            if key in self.S.last_write:
                acc.append(self.S.last_write[key])
            acc += self.S.reads.get(key, [])
        acc = list(set(acc))
        self.used -= b.asize
        self.clock += 1
        self.free.append([b.off, b.asize, acc, self.clock])
        self.free.sort(key=lambda x: x[0])


def build(debug=()):
    nc = bass.Bass("TRN2", target_bir_lowering=False)
    S = Sched()

    def din(name, shape):
        return nc.dram_tensor(name, list(shape), F32, kind="ExternalInput").ap()

    x = din("x", [S_LEN, D]); cT = din("cT", [128, 8]); w_ada = din("w_ada", [24 * 128, 8 * 128])
    b_adaT = din("b_adaT", [128, 24]); b_ada_row = din("b_ada_row", [1, 3 * D])
    w_in = din("w_in", [40 * 128, 8 * 128]); w_fd = din("w_f", [128, 64]); b_f = din("b_f", [8, 1])
    lam_re = din("lam_re", [128, 16]); lam_im = din("lam_im", [128, 16]); ldt = din("ldt", [128, 16])
    Bre_d = din("Bre", [128, 16 * 32]); Bim_d = din("Bim", [128, 16 * 32])
    Cre_d = din("Cre", [128, 16 * 32]); Cim_d = din("Cim", [128, 16 * 32])
    dpair_d = din("dpair", [128, 16])
    w_glu = din("w_glu", [8 * 128, 4 * 128]); w_ps = din("w_ps", [8 * 128, 4 * 128]); w_pa = din("w_pa", [8 * 128, 4 * 128])
    w_out = din("w_out", [8 * 128, 8 * 128]); ln_g = din("ln_g", [1, D]); ln_b = din("ln_b", [1, D])
    ident_d = din("ident", [128, 128]); sel_d = din("sel", [128, 16 * 128]); cmask_d = din("cmask", [128, 128])
    tmask_d = din("tmask", [128, 256]); kvals_d = din("kvals", [128, NKV]); cvals_d = din("cvals", [128, 256])
    out = nc.dram_tensor("out", [S_LEN, D], F32, kind="ExternalOutput").ap()
    dbg_out = {}

    ARENA_BYTES = 198 * 1024
    st = contextlib.ExitStack()
    arena_t = st.enter_context(nc.sbuf_tensor("arena", [128, ARENA_BYTES // 2], BF16))
    psum = [st.enter_context(nc.psum_tensor("ps%d" % i, [128, 512], F32)) for i in range(8)]
    ti_ts = [st.enter_context(nc.sbuf_tensor("ra_int%d" % i, [128, 1024], I32)) for i in range(2)]
    ti_rr = [0]
    AR = Arena(S, arena_t, ARENA_BYTES)
    ps_rr = [0]

    ps_mode = ["all"]
    qk_rr = [0]

    def ps_next():
        if ps_mode[0] == "all8":
            i = ps_rr[0] % 8; ps_rr[0] = (i + 1) % 8
        elif ps_mode[0] == "all":
            i = ps_rr[0] % 6; ps_rr[0] = (i + 1) % 6
        else:
            i = 4 + ps_rr[0] % 2; ps_rr[0] = (ps_rr[0] + 1) % 2
        return psum[i], ("ps", i)

    def ps_qk():
        i = qk_rr[0]; qk_rr[0] = (i + 1) % 4
        return psum[i], ("ps", i)

    def A(nbytes, name=""):
        return AR.alloc(nbytes, name)

    def fcols(ap):
        n = 1
        for d_ in ap.shape[1:]:
            n *= int(d_)
        return n

    def esize(ap):
        return 2 if ap.dtype == BF16 else 4

    def vcost(eng, ap, mult=1.0):
        c = fcols(ap) * mult
        return (0.12 + c / 900.0) if eng == "dve" else (0.15 + c / 300.0)

    def dma(out_ap, in_ap, reads, writes):
        nb = int(out_ap.shape[0]) * fcols(out_ap) * esize(out_ap)
        S.add("sp", lambda h: h.dma_start(out=out_ap, in_=in_ap), reads=reads, writes=writes, cost=0.06, nbytes=nb)

    def mm(out_ap, lhsT, rhs, start, stop, reads, writes):
        n = max(fcols(rhs), 64)
        c = 0.086 if n <= 128 else (0.26 if n == 256 else n / 2100.0 + 0.012)
        if int(lhsT.shape[0]) == 70:
            c += 0.03
        if rhs.dtype == F32:
            c *= 4.0
        S.add("pe", lambda h: h.matmul(out_ap, lhsT, rhs, start=start, stop=stop), reads=reads, writes=writes, cost=c)

    def act(out_ap, in_ap, func, reads, writes, bias=0.0, scale=1.0):
        S.add("act", lambda h: h.activation(out=out_ap, in_=in_ap, func=func, bias=bias, scale=scale),
              reads=reads, writes=writes, cost=(0.08 + fcols(out_ap) / 1150.0) * (3.0 if func == AF.Gelu else 1.0), tag=func.name)

    def tt(eng, out_ap, a, b, op, reads, writes):
        S.add(eng, lambda h: h.tensor_tensor(out=out_ap, in0=a, in1=b, op=op), reads=reads, writes=writes,
              cost=vcost(eng, out_ap))

    def ts(eng, out_ap, a, s1, s2, op0, op1, reads, writes):
        if s2 is None:
            S.add(eng, lambda h: h.tensor_scalar(out=out_ap, in0=a, scalar1=s1, scalar2=None, op0=op0),
                  reads=reads, writes=writes, cost=vcost(eng, out_ap))
        else:
            S.add(eng, lambda h: h.tensor_scalar(out=out_ap, in0=a, scalar1=s1, scalar2=s2, op0=op0, op1=op1),
                  reads=reads, writes=writes, cost=vcost(eng, out_ap))

    def stt(eng, out_ap, a, s, b, op0, op1, reads, writes):
        S.add(eng, lambda h: h.scalar_tensor_tensor(out=out_ap, in0=a, scalar=s, in1=b, op0=op0, op1=op1),
              reads=reads, writes=writes, cost=vcost(eng, out_ap))

    def cp(eng, out_ap, in_ap, reads, writes):
        if eng == "act":
            S.add(eng, lambda h: h.activation(out=out_ap, in_=in_ap, func=AF.Identity), reads=reads, writes=writes,
                  cost=0.08 + fcols(out_ap) / 1150.0, tag="Identity")
        else:
            S.add(eng, lambda h: h.tensor_copy(out=out_ap, in_=in_ap), reads=reads, writes=writes,
                  cost=vcost(eng, out_ap, 0.6 if eng == "dve" else 1.5))

    def memset(eng, ap, val, writes):
        S.add(eng, lambda h: h.memset(ap, val), writes=writes, cost=vcost(eng, ap, 0.5))

    def dump(name, ap, shape, key):
        if name in debug:
            t = nc.dram_tensor("dbg_" + name, list(shape), ap.dtype, kind="ExternalOutput").ap()
            dbg_out[name] = t
            dma(t, ap, [key], [])

    def load_const(dram, cols, name, to_bf=False):
        b = A(cols * 4, name)
        dma(b.ap(F32), dram, [], [b.k()])
        if not to_bf:
            return b
        bb = A(cols * 2, name + "_bf")
        cp("pool", bb.ap(), b.ap(F32), [b.k()], [bb.k()])
        AR.release(b)
        return bb

    ident = load_const(ident_d, 128, "ident")
    ident_bf = A(256, "ident_bf")
    cp("pool", ident_bf.ap(), ident.ap(F32), [ident.k()], [ident_bf.k()])
    sel_bf = load_const(sel_d, 16 * 128, "sel", True)
    cmask_bf = load_const(cmask_d, 128, "cmask", True)
    tmask = load_const(tmask_d, 256, "tmask")
    kvals = load_const(kvals_d, NKV, "kvals")
    cvals = load_const(cvals_d, 256, "cvals")
    lre = load_const(lam_re, 16, "lre"); lim = load_const(lam_im, 16, "lim"); ldtb = load_const(ldt, 16, "ldt")
    dpair = load_const(dpair_d, 16, "dpair")
    badaT = load_const(b_adaT, 24, "badaT")
    cTb = load_const(cT, 8, "cT")
    sel3 = sel_bf.ap().rearrange("p (a d) -> p a d", d=128)

    def SEL(a, b):
        return sel3[:, a * 4 + b, :]

    cast_rr = [0]
    def load_w(blk2d, j0, nblk, kcn, name, eng=("dve", "act")):
        ncols = nblk * 128
        wb = A(kcn * ncols * 2, name)
        wv = wb.ap().rearrange("p (k n) -> p k n", n=ncols)
        step = max(1, (2048 // kcn) // 128)
        jj = 0; pi = 0
        while jj < nblk:
            nb = min(step, nblk - jj)
            sg = A(nb * kcn * 128 * 4, name + "_stg")
            dma(sg.ap(F32).rearrange("p (j c) -> p j c", j=nb),
                blk2d[(j0 + jj) * 128:(j0 + jj + nb) * 128, :].rearrange("(j p) c -> p j c", p=128), [], [sg.k()])
            e = eng if isinstance(eng, str) else eng[cast_rr[0] % len(eng)]
            cast_rr[0] += 1
            cp(e, wv[:, :, jj * 128:(jj + nb) * 128].rearrange("p k (j n) -> p k j n", j=nb),
               sg.ap(F32).rearrange("p (j k n) -> p k j n", j=nb, k=kcn), [sg.k()], [wb.k(pi)])
            AR.release(sg)
            jj += nb; pi += 1
        return wb, wv, [wb.k(i) for i in range(pi)]

    def tmp(cols, name):
        return A(cols * 4, name)

    dt_t = tmp(16, "dt")
    act(dt_t.ap(F32), ldtb.ap(F32), AF.Exp, [ldtb.k()], [dt_t.k()])
    lrdt = tmp(16, "lrdt"); lidt = tmp(16, "lidt")
    tt("dve", lrdt.ap(F32), lre.ap(F32), dt_t.ap(F32), ALU.mult, [lre.k(), dt_t.k()], [lrdt.k()])
    tt("dve", lidt.ap(F32), lim.ap(F32), dt_t.ap(F32), ALU.mult, [lim.k(), dt_t.k()], [lidt.k()])
    NT = 16 * NKV
    kv3 = kvals.ap(F32).unsqueeze(1).broadcast_to([128, 16, NKV])

    def v3(b, n=NKV):
        return b.ap(F32).rearrange("p (a k) -> p a k", k=n)

    def reduce_angle(dst, dstk, src, srck, cols, shift, view):
        y = tmp(cols, "ra_y"); t1 = tmp(cols, "ra_t1"); tf = tmp(cols, "ra_tf")
        ts("dve", y.ap(F32), src, shift, None, ALU.add, None, [srck], [y.k()])
        ts("dve", t1.ap(F32), y.ap(F32), 1.0 / TWO_PI, None, ALU.mult, None, [y.k()], [t1.k()])
        ti_i = ti_rr[0] % 2; ti_rr[0] += 1
        ti_t = ti_ts[ti_i]
        cp("dve", ti_t[:, 0:cols], t1.ap(F32), [t1.k()], ["ra_int%d" % ti_i])
        cp("dve", tf.ap(F32), ti_t[:, 0:cols], ["ra_int%d" % ti_i], [tf.k()])
        stt("dve", t1.ap(F32), tf.ap(F32), -CW_C1, y.ap(F32), ALU.mult, ALU.add, [tf.k(), y.k()], [t1.k()])
        stt("dve", y.ap(F32), tf.ap(F32), -CW_C2, t1.ap(F32), ALU.mult, ALU.add, [tf.k(), t1.k()], [y.k()])
        ts("dve", dst, y.ap(F32), -math.pi, math.pi, ALU.max, ALU.min, [y.k()], [dstk])
        for b in (y, t1, tf):
            AR.release(b)

    magk = tmp(NT, "magk"); angk = tmp(NT, "angk")
    tt("dve", v3(magk), lrdt.ap(F32).unsqueeze(2).broadcast_to([128, 16, NKV]), kv3, ALU.mult, [lrdt.k(), kvals.k()], [magk.k()])
    act(magk.ap(F32), magk.ap(F32), AF.Exp, [magk.k()], [magk.k()])
    tt("dve", v3(angk), lidt.ap(F32).unsqueeze(2).broadcast_to([128, 16, NKV]), kv3, ALU.mult, [lidt.k(), kvals.k()], [angk.k()])
    sred = tmp(NT, "sred"); cred = tmp(NT, "cred")
    reduce_angle(sred.ap(F32), sred.k(), angk.ap(F32), angk.k(), NT, 0.0, None)
    reduce_angle(cred.ap(F32), cred.k(), angk.ap(F32), angk.k(), NT, math.pi / 2, None)
    Are = tmp(NT, "Are"); Aim = tmp(NT, "Aim")
    act(sred.ap(F32), sred.ap(F32), AF.Sin, [sred.k()], [sred.k()])
    act(cred.ap(F32), cred.ap(F32), AF.Sin, [cred.k()], [cred.k()])
    tt("dve", Are.ap(F32), magk.ap(F32), cred.ap(F32), ALU.mult, [magk.k(), cred.k()], [Are.k()])
    tt("dve", Aim.ap(F32), magk.ap(F32), sred.ap(F32), ALU.mult, [magk.k(), sred.k()], [Aim.k()])
    ph = tmp(16, "ph"); r8 = tmp(16, "r8")
    ang8 = tmp(16, "ang8")
    cp("dve", ang8.ap(F32), v3(angk)[:, :, 16], [angk.k()], [ang8.k()])
    reduce_angle(ph.ap(F32), ph.k(), ang8.ap(F32), ang8.k(), 16, 0.0, None)
    cp("dve", r8.ap(F32), v3(magk)[:, :, 16], [magk.k()], [r8.k()])
    AR.release(ang8)
    am1 = tmp(16, "am1"); abi = tmp(16, "abi"); den = tmp(16, "den"); t_a = tmp(16, "t_a"); t_b = tmp(16, "t_b")
    cr = tmp(16, "cr"); ci = tmp(16, "ci")
    ts("dve", am1.ap(F32), v3(Are)[:, :, 9], -1.0, None, ALU.add, None, [Are.k()], [am1.k()])
    cp("dve", abi.ap(F32), v3(Aim)[:, :, 9], [Aim.k()], [abi.k()])
    tt("dve", den.ap(F32), lre.ap(F32), lre.ap(F32), ALU.mult, [lre.k()], [den.k()])
    tt("dve", t_a.ap(F32), lim.ap(F32), lim.ap(F32), ALU.mult, [lim.k()], [t_a.k()])
    tt("dve", den.ap(F32), den.ap(F32), t_a.ap(F32), ALU.add, [den.k(), t_a.k()], [den.k()])
    S.add("dve", lambda h: h.reciprocal(out=den.ap(F32), in_=den.ap(F32)), reads=[den.k()], writes=[den.k()], cost=0.2)
    tt("dve", t_a.ap(F32), am1.ap(F32), lre.ap(F32), ALU.mult, [am1.k(), lre.k()], [t_a.k()])
    tt("dve", t_b.ap(F32), abi.ap(F32), lim.ap(F32), ALU.mult, [abi.k(), lim.k()], [t_b.k()])
    tt("dve", t_a.ap(F32), t_a.ap(F32), t_b.ap(F32), ALU.add, [t_a.k(), t_b.k()], [t_a.k()])
    tt("dve", cr.ap(F32), t_a.ap(F32), den.ap(F32), ALU.mult, [t_a.k(), den.k()], [cr.k()])
    tt("dve", t_a.ap(F32), abi.ap(F32), lre.ap(F32), ALU.mult, [abi.k(), lre.k()], [t_a.k()])
    tt("dve", t_b.ap(F32), am1.ap(F32), lim.ap(F32), ALU.mult, [am1.k(), lim.k()], [t_b.k()])
    tt("dve", t_a.ap(F32), t_a.ap(F32), t_b.ap(F32), ALU.subtract, [t_a.k(), t_b.k()], [t_a.k()])
    tt("dve", ci.ap(F32), t_a.ap(F32), den.ap(F32), ALU.mult, [t_a.k(), den.k()], [ci.k()])
    for b in (am1, abi, den, t_a, t_b, magk, angk, sred, cred, dt_t, lrdt, lidt):
        AR.release(b)

    def cmul(eng, o_re, o_im, ore_k, oim_k, a_re, a_im, b_re, b_im, rk, cols, neg_im=False, shape=None):
        t1 = tmp(cols, "cm1"); t2 = tmp(cols, "cm2")
        v = (lambda b_: b_.ap(F32)) if shape is None else (lambda b_: b_.ap(F32).rearrange(shape[0], **shape[1]))
        tt(eng, v(t1), a_re, b_re, ALU.mult, rk, [t1.k()])
        tt(eng, v(t2), a_im, b_im, ALU.mult, rk, [t2.k()])
        tt(eng, o_re, v(t1), v(t2), ALU.subtract, [t1.k(), t2.k()], [ore_k])
        tt(eng, v(t1), a_re, b_im, ALU.mult, rk, [t1.k()])
        tt(eng, v(t2), a_im, b_re, ALU.mult, rk, [t2.k()])
        if neg_im:
            stt(eng, o_im, v(t1), -1.0, v(t2), ALU.mult, ALU.subtract, [t1.k(), t2.k()], [oim_k])
        else:
            tt(eng, o_im, v(t1), v(t2), ALU.add, [t1.k(), t2.k()], [oim_k])
        AR.release(t1); AR.release(t2)

    Are3 = v3(Are); Aim3 = v3(Aim)
    def load_hc(hc):
        return (load_w(w_in, 8 + hc, 1, 8, "w_q"),
                load_w(w_in, 12 + hc, 1, 8, "w_k"),
                load_w(w_in, 20 + hc, 1, 8, "w_za"),
                load_w(w_in, 16 + hc, 1, 8, "w_v"))
    hc_w = load_hc(0)
    cact = A(8 * 4, "cact")
    act(cact.ap(F32), cTb.ap(F32), AF.Silu, [cTb.k()], [cact.k()])
    cact_bf = A(8 * 2, "cact_bf")
    cp("dve", cact_bf.ap(), cact.ap(F32), [cact.k()], [cact_bf.k()])
    mrow = A(2048 * 4, "mrow")
    browA = A(2048 * 4, "browA")
    dma(browA.ap(F32)[0:1, :], b_ada_row[0:1, 0:2048], [], [browA.k()])
    for blk in range(4):
        sg = A(8 * 512 * 4, "wada_stg")
        sv = sg.ap(F32).rearrange("p (k n) -> p k n", n=512)
        dma(sg.ap(F32).rearrange("p (j c) -> p j c", j=4),
            w_ada[blk * 512:(blk + 1) * 512, :].rearrange("(j p) c -> p j c", p=128), [], [sg.k()])
        s16 = A(8 * 512 * 2, "wada_bf")
        s16v = s16.ap().rearrange("p (k n) -> p k n", n=512)
        cp("dve" if blk % 2 == 0 else "act", s16v.rearrange("p k (j n) -> p k j n", j=4),
           sg.ap(F32).rearrange("p (j k n) -> p k j n", j=4, k=8), [sg.k()], [s16.k()])
        AR.release(sg)
        pg, pgk = ps_next()
        for kc in range(8):
            mm(pg[0:1, :], cact_bf.ap()[:, kc:kc + 1], s16v[:, kc, :], kc == 0, kc == 7, [s16.k(), cact_bf.k()], [pgk])
        AR.release(s16)
        tt("dve", mrow.ap(F32)[0:1, blk * 512:(blk + 1) * 512], pg[0:1, :], browA.ap(F32)[0:1, blk * 512:(blk + 1) * 512], ALU.add,
           [pgk, browA.k()], [mrow.k(blk)])
    one1 = A(4, "one1")
    memset("pool", one1.ap(F32)[0:1, :], 1.0, [one1.k()])
    modT = A(16 * 4, "modT")
    pm, pmk = ps_next()
    for j in range(16):
        mm(pm[:, j:j + 1], mrow.ap(F32)[0:1, j * 128:(j + 1) * 128], one1.ap(F32)[0:1, 0:1], True, True,
           [mrow.k(j // 4), one1.k()], [pmk])
    cp("dve", modT.ap(F32), pm[:, 0:16], [pmk], [modT.k()])
    sc1 = A(8 * 4, "sc1")
    ts("dve", sc1.ap(F32), modT.ap(F32)[:, 8:16], 1.0, None, ALU.add, None, [modT.k()], [sc1.k()])
    AR.release(mrow); AR.release(browA); AR.release(one1)
    uT = [A(S_LEN * 2, "uT%d" % i) for i in range(8)]
    eps_t = A(4, "eps")
    memset("pool", eps_t.ap(F32), LN_EPS, [eps_t.k()])

    def layer_norm_stats_multi(srcs, tag):
        n = len(srcs)
        stts = [A(2 * 6 * 4, "bnst" + tag) for _ in range(n)]
        mvs = [A(2 * 4, "mv" + tag) for _ in range(n)]
        sds = [A(4, "sd" + tag) for _ in range(n)]
        rstds = [A(4, "rstd" + tag) for _ in range(n)]
        nmrs = [A(4, "nmr" + tag) for _ in range(n)]
        for i, (src_ap, src_key) in enumerate(srcs):
            sv = stts[i].ap(F32).rearrange("p (a b) -> p a b", b=6)
            for hf in range(2):
                S.add("dve", lambda h, hf=hf, sv=sv, src_ap=src_ap: h.bn_stats(out=sv[:, hf, :], in_=src_ap[:, hf * 512:(hf + 1) * 512]),
                      reads=[src_key[hf] if isinstance(src_key, list) else src_key], writes=[stts[i].k(hf)], cost=0.65)
            S.add("dve", lambda h, i=i: h.bn_aggr(out=mvs[i].ap(F32), in_=stts[i].ap(F32)),
                  reads=[stts[i].k(0), stts[i].k(1)], writes=[mvs[i].k()], cost=0.2)
        for i in range(n):
            act(sds[i].ap(F32), mvs[i].ap(F32)[:, 1:2], AF.Sqrt, [mvs[i].k(), eps_t.k()], [sds[i].k()], bias=eps_t.ap(F32)[:, 0:1])
        for i in range(n):
            S.add("dve", lambda h, i=i: h.reciprocal(out=rstds[i].ap(F32), in_=sds[i].ap(F32)), reads=[sds[i].k()], writes=[rstds[i].k()], cost=0.15)
            stt("dve", nmrs[i].ap(F32), mvs[i].ap(F32)[:, 0:1], -1.0, rstds[i].ap(F32), ALU.mult, ALU.mult,
                [mvs[i].k(), rstds[i].k()], [nmrs[i].k()])
        for b in stts + mvs + sds:
            AR.release(b)
        return list(zip(rstds, nmrs))

    for g4 in range(4):
        xn = A(4 * 1024 * 2, "xn")
        xnv = xn.ap().rearrange("p (a n) -> p a n", n=1024)
        xts = []
        for t4 in range(4):
            tti = g4 * 4 + t4
            xt = A(1024 * 4, "xt")
            dma(xt.ap(F32), x[tti * 128:(tti + 1) * 128, :], [], [xt.k()])
            xts.append(xt)
        stats = layer_norm_stats_multi([(xt.ap(F32), xt.k()) for xt in xts], "a")
        for t4 in range(4):
            xt = xts[t4]; rstd, nmr = stats[t4]
            act(xnv[:, t4, :], xt.ap(F32), AF.Identity, [xt.k(), rstd.k(), nmr.k()], [xn.k(t4)],
                bias=nmr.ap(F32)[:, 0:1], scale=rstd.ap(F32)[:, 0:1])
            AR.release(xt); AR.release(rstd); AR.release(nmr)
        for kc in range(8):
            p_, pk_ = ps_next()
            for t4 in range(4):
                mm(p_[:, t4 * 128:(t4 + 1) * 128], xnv[:, t4, kc * 128:(kc + 1) * 128], ident_bf.ap(), True, True,
                   [xn.k(t4), ident_bf.k()], [pk_])
            if UTA and kc % UTA == UTA - 1:
                act(uT[kc].ap()[:, g4 * 512:(g4 + 1) * 512], p_[:, :], AF.Identity, [pk_, sc1.k(), modT.k()], [uT[kc].k(g4)],
                    bias=modT.ap(F32)[:, kc:kc + 1], scale=sc1.ap(F32)[:, kc:kc + 1])
            else:
                ts("dve", uT[kc].ap()[:, g4 * 512:(g4 + 1) * 512], p_[:, :],
                   sc1.ap(F32)[:, kc:kc + 1], modT.ap(F32)[:, kc:kc + 1], ALU.mult, ALU.add,
                   [pk_, sc1.k(), modT.k()], [uT[kc].k(g4)])
        AR.release(xn)
    for kc in range(8):
        dump("uT%d" % kc, uT[kc].ap(), [128, S_LEN], uT[kc].k(3))

    def uT_keys(kc, tcb):
        return uT[kc].k(tcb)

    def proj_fm(wv, wkeys, col0, M, tcb, p_, pk_, kcn=8, src=None, srckeys=None):
        for kc in range(kcn):
            rhs = (uT[kc].ap() if src is None else src[kc].ap())[:, tcb * 512:(tcb + 1) * 512]
            rk = uT[kc].k(tcb) if src is None else srckeys(kc, tcb)
            mm(p_[0:M, :], wv[:, kc, col0:col0 + M], rhs, kc == 0, kc == kcn - 1, list(wkeys) + [rk], [pk_])

    w_xs, w_xs_v, w_xs_k = load_w(w_in, 0, 4, 8, "w_xs", eng="act")
    yT = [A(S_LEN * 2, "yT%d" % i) for i in range(4)]
    e3 = lambda b_: b_.ap(F32).rearrange("p (a c) -> p a c", c=256)

    def s5_gen():
      for cc in range(4):
          p0 = cc * 4
          Bre = tmp(128, "Bre"); Bim = tmp(128, "Bim"); Cre = tmp(128, "Cre"); Cim = tmp(128, "Cim")
          dma(Bre.ap(F32), Bre_d[:, p0 * 32:(p0 + 4) * 32], [], [Bre.k()])
          dma(Bim.ap(F32), Bim_d[:, p0 * 32:(p0 + 4) * 32], [], [Bim.k()])
          dma(Cre.ap(F32), Cre_d[:, p0 * 32:(p0 + 4) * 32], [], [Cre.k()])
          dma(Cim.ap(F32), Cim_d[:, p0 * 32:(p0 + 4) * 32], [], [Cim.k()])
          Bbre = tmp(128, "Bbre"); Bbim = tmp(128, "Bbim")
          sh43 = ("p (a b) -> p a b", dict(b=32))
          b3 = lambda b_: b_.ap(F32).rearrange("p (a b) -> p a b", b=32)
          crb = cr.ap(F32)[:, p0:p0 + 4].unsqueeze(2).broadcast_to([128, 4, 32])
          cib = ci.ap(F32)[:, p0:p0 + 4].unsqueeze(2).broadcast_to([128, 4, 32])
          cmul("dve", b3(Bbre), b3(Bbim), Bbre.k(), Bbim.k(), crb, cib, b3(Bre), b3(Bim),
               [cr.k(), ci.k(), Bre.k(), Bim.k()], 128, shape=sh43)
          yield
          sh8 = ("p (a i b) -> p a i b", dict(i=8, b=32)); sh9 = ("p (a i b) -> p a i b", dict(i=9, b=32))
          v8 = lambda b_: b_.ap(F32).rearrange(sh8[0], **sh8[1])
          v9 = lambda b_: b_.ap(F32).rearrange(sh9[0], **sh9[1])
          def abc(c0, n):
              return (Are3[:, p0:p0 + 4, c0:c0 + n].unsqueeze(3).broadcast_to([128, 4, n, 32]),
                      Aim3[:, p0:p0 + 4, c0:c0 + n].unsqueeze(3).broadcast_to([128, 4, n, 32]))

          def bb(bre, bim, n):
              return (b3(bre).unsqueeze(2).broadcast_to([128, 4, n, 32]), b3(bim).unsqueeze(2).broadcast_to([128, 4, n, 32]))
          Tt = A(4 * 256 * 2, "Tt"); Wt = A(4 * 4 * 128 * 2, "Wt"); Vt = A(2 * 1024 * 2, "Vt")
          Ttv = Tt.ap().rearrange("p (a n) -> p a n", n=256)
          Wtv = Wt.ap().rearrange("p (a b n) -> p a b n", b=4, n=128)
          Vtv = Vt.ap().rearrange("p (s a j b) -> p s a j b", s=2, a=4, j=8)
          WAre = tmp(1024, "WAre"); WAim = tmp(1024, "WAim")
          br_, bi_ = bb(Bbre, Bbim, 8)
          ar_, ai_ = abc(17, 8)
          cmul("dve", v8(WAre), v8(WAim), WAre.k(), WAim.k(), ar_, ai_, br_, bi_, [Are.k(), Aim.k(), Bbre.k(), Bbim.k()], 1024, shape=sh8)
          yield
          yield
          WAb = A(2 * 1024 * 2, "WAb")
          WAbv = WAb.ap().rearrange("p (s a i b) -> p s a i b", s=2, a=4, i=8)
          cp("act", WAbv[:, 0], v8(WAre), [WAre.k()], [WAb.k(0)])
          cp("pool", WAbv[:, 1], v8(WAim), [WAim.k()], [WAb.k(1)])
          AR.release(WAre); AR.release(WAim)
          for pr in range(4):
              p_, pk_ = ps_next()
              for pl in range(2):
                  for kt in range(2):
                      q = pl * 2 + kt
                      mm(p_[:, q * 128:(q + 1) * 128], WAbv[:, pl, pr, kt * 4:(kt + 1) * 4, :].rearrange("p a b -> p (a b)"), ident_bf.ap(),
                         True, True, [WAb.k(pl), ident_bf.k()], [pk_])
              cp("act", Wtv[:, pr, :, :].rearrange("p b n -> p (b n)"), p_[:, :], [pk_], [Wt.k(pr)])
          AR.release(WAb)
          yield
          BAre = tmp(1024, "BAre"); BAim = tmp(1024, "BAim")
          ar_, ai_ = abc(0, 8)
          cmul("dve", v8(BAre), v8(BAim), BAre.k(), BAim.k(), ar_, ai_, br_, bi_, [Are.k(), Aim.k(), Bbre.k(), Bbim.k()], 1024, shape=sh8)
          yield
          yield
          CAre = tmp(1152, "CAre"); CAimN = tmp(1152, "CAimN")
          ar_, ai_ = abc(8, 9); br_, bi_ = bb(Cre, Cim, 9)
          cmul("dve", v9(CAre), v9(CAimN), CAre.k(), CAimN.k(), ar_, ai_, br_, bi_, [Are.k(), Aim.k(), Cre.k(), Cim.k()], 1152, neg_im=True, shape=sh9)
          yield
          yield
          cp("pool", Vtv[:, 0], v9(CAre)[:, :, 1:9, :], [CAre.k()], [Vt.k(0)])
          cp("pool", Vtv[:, 1], v9(CAimN)[:, :, 1:9, :], [CAimN.k()], [Vt.k(1)])
          for pr in range(4):
              p_, pk_ = ps_next()
              mm(p_[:, 0:256], v8(BAre)[:, pr, 0:4, :].rearrange("p a b -> p (a b)"), v9(CAre)[:, pr, 0:8, :].rearrange("p a b -> p (a b)"),
                 True, False, [BAre.k(), CAre.k()], [pk_])
              mm(p_[:, 0:256], v8(BAim)[:, pr, 0:4, :].rearrange("p a b -> p (a b)"), v9(CAimN)[:, pr, 0:8, :].rearrange("p a b -> p (a b)"),
                 False, True, [BAim.k(), CAimN.k()], [pk_])
              tt("dve", Ttv[:, pr, :], p_[:, 0:256], tmask.ap(F32), ALU.mult, [pk_, tmask.k()], [Tt.k(pr)])
          for b in (Bre, Bim, Cre, Cim, Bbre, Bbim, BAre, BAim, CAre, CAimN):
              AR.release(b)
          yield
          Esin = tmp(1024, "Esin"); Ecos = tmp(1024, "Ecos"); ang = tmp(1024, "ang")
          e3 = lambda b_: b_.ap(F32).rearrange("p (a c) -> p a c", c=256)
          tt("dve", e3(ang), ph.ap(F32)[:, p0:p0 + 4].unsqueeze(2).broadcast_to([128, 4, 256]),
             cvals.ap(F32).unsqueeze(1).broadcast_to([128, 4, 256]), ALU.mult, [ph.k(), cvals.k()], [ang.k()])
          reduce_angle(Esin.ap(F32), Esin.k(), ang.ap(F32), ang.k(), 1024, 0.0, None)
          yield
          yield
          reduce_angle(Ecos.ap(F32), Ecos.k(), ang.ap(F32), ang.k(), 1024, math.pi / 2, None)
          act(Esin.ap(F32), Esin.ap(F32), AF.Sin, [Esin.k()], [Esin.k()])
          act(Ecos.ap(F32), Ecos.ap(F32), AF.Sin, [Ecos.k()], [Ecos.k()])
          AR.release(ang)
          yield
          yield
          xsT = A(S_LEN * 2, "xsT")
          for tcb in range(4):
              p_, pk_ = ps_next()
              proj_fm(w_xs_v, w_xs_k, cc * 128, 128, tcb, p_, pk_)
              cp("act", xsT.ap()[:, tcb * 512:(tcb + 1) * 512], p_[:, :], [pk_], [xsT.k(tcb)])
          Xs = A(4 * 2 * 256 * 2, "Xs")
          yield
          Xsv = Xs.ap().rearrange("p (a h c) -> p a h c", h=2, c=256)
          xs_all = [xsT.k(t) for t in range(4)]
          for pw in range(4):
              p_, pk_ = ps_next()
              for hf in range(2):
                  for i4 in range(4):
                      i = hf * 4 + i4
                      mm(p_[:, hf * 256:(hf + 1) * 256], SEL(pw, i4), xsT.ap()[:, i:S_LEN:8], i4 == 0, i4 == 3,
                         xs_all + [sel_bf.k()], [pk_])
              cp("dve" if pw % 2 else "act", Xsv[:, pw, :, :].rearrange("p h c -> p (h c)"), p_[:, :], [pk_], [Xs.k(pw)])
          AR.release(xsT)
          yield
          Ybf = A(4 * 2 * 256 * 2, "Ybf")
          Ybv = Ybf.ap().rearrange("p (a h c) -> p a h c", h=2, c=256)
          stg = []
          for pw in range(4):
              pz, pzk = ps_next()
              for pl in range(2):
                  for kt in range(2):
                      mm(pz[:, pl * 256:(pl + 1) * 256], Wtv[:, pw, pl * 2 + kt, :], Xsv[:, pw, kt, :], kt == 0, kt == 1,
                         [Wt.k(pw), Xs.k(pw)], [pzk])
              zre = pz[:, 0:256]; zim = pz[:, 256:512]
              ec = e3(Ecos)[:, pw, :]; es = e3(Esin)[:, pw, :]
              t1 = tmp(256, "l2a"); t2 = tmp(256, "l2b"); gr = tmp(256, "gr"); gi = tmp(256, "gi")
              t3 = tmp(256, "l2c"); t4 = tmp(256, "l2d")
              tt("dve", t1.ap(F32), zre, ec, ALU.mult, [pzk, Ecos.k()], [t1.k()])
              tt("dve", t2.ap(F32), zim, es, ALU.mult, [pzk, Esin.k()], [t2.k()])
              tt("dve", t3.ap(F32), zim, ec, ALU.mult, [pzk, Ecos.k()], [t3.k()])
              tt("dve", t4.ap(F32), zre, es, ALU.mult, [pzk, Esin.k()], [t4.k()])
              tt("dve", gr.ap(F32), t1.ap(F32), t2.ap(F32), ALU.add, [t1.k(), t2.k()], [gr.k()])
              tt("dve", gi.ap(F32), t3.ap(F32), t4.ap(F32), ALU.subtract, [t3.k(), t4.k()], [gi.k()])
              for b in (t1, t2, t3, t4):
                  AR.release(b)
              stg.append(dict(gr=gr, gi=gi, ec=ec, es=es))
          yield
          for pw in range(4):
              d = stg[pw]; pair = p0 + pw
              Gr = tmp(256, "Gr"); Gi = tmp(256, "Gi")
              r8b = r8.ap(F32)[:, pair:pair + 1].broadcast_to([128, 256])
              S.add("dve", lambda h, Gr=Gr, gr=d["gr"], r8b=r8b: h.tensor_tensor_scan(out=Gr.ap(F32), data0=r8b, data1=gr.ap(F32), initial=0.0, op0=ALU.mult, op1=ALU.add),
                    reads=[d["gr"].k(), r8.k()], writes=[Gr.k()], cost=0.65)
              S.add("dve", lambda h, Gi=Gi, gi=d["gi"], r8b=r8b: h.tensor_tensor_scan(out=Gi.ap(F32), data0=r8b, data1=gi.ap(F32), initial=0.0, op0=ALU.mult, op1=ALU.add),
                    reads=[d["gi"].k(), r8.k()], writes=[Gi.k()], cost=0.65)
              d["Gr"] = Gr; d["Gi"] = Gi
              AR.release(d["gr"]); AR.release(d["gi"])
          yield
          for pw in range(4):
              d = stg[pw]
              Gr, Gi, ec, es = d["Gr"], d["Gi"], d["ec"], d["es"]
              t1 = tmp(256, "l3a"); t2 = tmp(256, "l3b"); t3 = tmp(256, "l3c"); t4 = tmp(256, "l3d")
              Hb = A(2 * 256 * 2, "Hb")
              Hbv = Hb.ap().rearrange("p (s c) -> p s c", c=256)
              memset("pool", Hbv[:, :, 0:1], 0.0, [Hb.k()])
              tt("dve", t1.ap(F32), Gr.ap(F32), ec, ALU.mult, [Gr.k(), Ecos.k()], [t1.k()])
              tt("dve", t2.ap(F32), Gi.ap(F32), es, ALU.mult, [Gi.k(), Esin.k()], [t2.k()])
              tt("dve", Hbv[:, 0, 1:256], t1.ap(F32)[:, 0:255], t2.ap(F32)[:, 0:255], ALU.subtract, [t1.k(), t2.k()], [Hb.k()])
              tt("dve", t3.ap(F32), Gr.ap(F32), es, ALU.mult, [Gr.k(), Esin.k()], [t3.k()])
              tt("dve", t4.ap(F32), Gi.ap(F32), ec, ALU.mult, [Gi.k(), Ecos.k()], [t4.k()])
              tt("dve", Hbv[:, 1, 1:256], t3.ap(F32)[:, 0:255], t4.ap(F32)[:, 0:255], ALU.add, [t3.k(), t4.k()], [Hb.k()])
              for b in (t1, t2, t3, t4, Gr, Gi):
                  AR.release(b)
              d["Hb"] = Hb; d["Hbv"] = Hbv
          yield
          yield
          for pw in range(4):
              d = stg[pw]; pair = p0 + pw
              Hb, Hbv = d["Hb"], d["Hbv"]
              py, pyk = ps_next()
              for mt in range(2):
                  o = py[:, mt * 256:(mt + 1) * 256]
                  jl = slice(mt * 4, mt * 4 + 4)
                  if mt == 0:
                      mm(o, Ttv[:, pw, 0:128], Xsv[:, pw, 0, :], True, False, [Tt.k(pw), Xs.k(pw)], [pyk])
                  else:
                      mm(o, Ttv[:, pw, 128:256], Xsv[:, pw, 0, :], True, False, [Tt.k(pw), Xs.k(pw)], [pyk])
                      mm(o, Ttv[:, pw, 0:128], Xsv[:, pw, 1, :], False, False, [Tt.k(pw), Xs.k(pw)], [pyk])
                  mm(o, Vtv[:, 0, pw, jl, :].rearrange("p j b -> p (j b)"), Hbv[:, 0, :], False, False, [Vt.k(0), Hb.k()], [pyk])
                  mm(o, Vtv[:, 1, pw, jl, :].rearrange("p j b -> p (j b)"), Hbv[:, 1, :], False, True, [Vt.k(1), Hb.k()], [pyk])
              AR.release(Hb)
              for mt in range(2):
                  stt("dve", Ybv[:, pw, mt, :], Xsv[:, pw, mt, :], dpair.ap(F32)[:, pair:pair + 1], py[:, mt * 256:(mt + 1) * 256],
                      ALU.mult, ALU.add, [Xs.k(pw), dpair.k(), pyk], [Ybf.k((pw, mt))])
          yield
          for j in range(8):
              p_, pk_ = ps_next()
              for pw in range(4):
                  mm(p_[:, 0:256], SEL(j % 4, pw), Ybv[:, pw, j // 4, :], pw == 0, pw == 3, [sel_bf.k(), Ybf.k((pw, j // 4))], [pk_])
              act(yT[cc].ap()[:, j:S_LEN:8], p_[:, 0:256], AF.Gelu, [pk_], [yT[cc].k()])
          for b in (Tt, Wt, Vt, Esin, Ecos, Xs, Ybf):
              AR.release(b)
          yield
    s5g = s5_gen()

    w_f32 = A(64 * 4, "w_f32"); dma(w_f32.ap(F32), w_fd, [], [w_f32.k()])
    w_f = A(64 * 2, "w_f"); cp("dve", w_f.ap(), w_f32.ap(F32), [w_f32.k()], [w_f.k()])
    AR.release(w_f32)
    w_f_v = w_f.ap().rearrange("p (k n) -> p k n", n=8); w_f_k = [w_f.k()]
    bfb = A(4, "bf"); dma(bfb.ap(F32)[0:8, :], b_f, [], [bfb.k()])
    nbf = A(4, "nbf")
    ts("dve", nbf.ap(F32)[0:8, :], bfb.ap(F32)[0:8, :], -1.0, None, ALU.mult, None, [bfb.k()], [nbf.k()])
    lg = A(S_LEN * 4, "lg"); cumf = A(S_LEN * 4, "cumf")
    for tcb in range(4):
        p_, pk_ = ps_next()
        proj_fm(w_f_v, w_f_k, 0, 8, tcb, p_, pk_)
        act(lg.ap(F32)[0:8, tcb * 512:(tcb + 1) * 512], p_[0:8, :], AF.Exp, [pk_, nbf.k()], [lg.k(tcb)], bias=nbf.ap(F32)[0:8, 0:1], scale=-1.0)
    one8 = A(4, "one8"); memset("pool", one8.ap(F32)[0:8, :], 1.0, [one8.k()])
    lgk = [lg.k(t) for t in range(4)]
    act(lg.ap(F32)[0:8, :], lg.ap(F32)[0:8, :], AF.Ln, lgk + [one8.k()], lgk, bias=one8.ap(F32)[0:8, 0:1])
    S.add("dve", lambda h: h.tensor_tensor_scan(out=cumf.ap(F32)[0:8, :], data0=one8.ap(F32)[0:8, 0:1].broadcast_to([8, S_LEN]),
                                                data1=lg.ap(F32)[0:8, :], initial=0.0, op0=ALU.mult, op1=ALU.subtract),
          reads=lgk + [one8.k()], writes=[cumf.k()], cost=4.5)
    CF = A(3 * S_LEN * 2, "CF"); NCF = A(3 * S_LEN * 2, "NCF")
    CFv = CF.ap().rearrange("p (a n) -> p a n", n=S_LEN); NCFv = NCF.ap().rearrange("p (a n) -> p a n", n=S_LEN)
    r1 = A(S_LEN * 4, "r1")
    cp("dve", CFv[0:8, 0, :], cumf.ap(F32)[0:8, :], [cumf.k()], [CF.k(0)])
    tt("dve", r1.ap(F32)[0:8, :], cumf.ap(F32)[0:8, :], CFv[0:8, 0, :], ALU.subtract, [cumf.k(), CF.k(0)], [r1.k()])
    cp("dve", CFv[0:8, 1, :], r1.ap(F32)[0:8, :], [r1.k()], [CF.k(1)])
    tt("dve", lg.ap(F32)[0:8, :], r1.ap(F32)[0:8, :], CFv[0:8, 1, :], ALU.subtract, [r1.k(), CF.k(1)], lgk)
    cp("dve", CFv[0:8, 2, :], lg.ap(F32)[0:8, :], lgk, [CF.k(2)])
    ts("dve", NCF.ap()[0:8, :], CF.ap()[0:8, :], -1.0, None, ALU.mult, None, [CF.k(0), CF.k(1), CF.k(2)], [NCF.k()])
    dump("cumf", cumf.ap(F32)[0:8, :], [8, S_LEN], cumf.k())
    cf_scr = nc.dram_tensor("cf_scr", [8, 6, S_LEN], BF16, kind="Internal").ap()
    dma(cf_scr[:, 0:3, :], CFv[0:8, :, :], [CF.k(0), CF.k(1), CF.k(2)], ["cf_scr"])
    dma(cf_scr[:, 3:6, :], NCFv[0:8, :, :], [NCF.k()], ["cf_scr"])
    for b in (lg, r1, cumf, w_f, one8, bfb, nbf, CF, NCF):
        AR.release(b)
    oz = [A(S_LEN * 2, "oz%d" % i) for i in range(4)]
    ps_mode[0] = "gen"
    kt_count = [0]
    for _ in range(PRETICK):
        next(s5g, None)
    for hc in range(4):
        (w_q, w_q_v, w_q_k), (w_k, w_k_v, w_k_k), (w_za, w_za_v, w_za_k), (w_v, w_v_v, w_v_k) = hc_w
        Vaug = A(16 * 2 * 128 * 2, "Vaug")
        Vv = Vaug.ap().rearrange("p (t e n) -> p t e n", e=2, n=128)
        memset("pool", Vaug.ap(), 1.0, [Vaug.k()])
        for g4 in range(4):
            p_, pk_ = ps_next()
            for t4 in range(4):
                tti = g4 * 4 + t4
                for kc in range(8):
                    mm(p_[:, t4 * 128:(t4 + 1) * 128], uT[kc].ap()[:, tti * 128:(tti + 1) * 128], w_v_v[:, kc, :], kc == 0, kc == 7,
                       list(w_v_k) + [uT[kc].k(g4)], [pk_])
            pv = p_[:, :].rearrange("p (t n) -> p t n", n=128)
            cp("dve", Vv[:, g4 * 4:(g4 + 1) * 4, 0, 0:64], pv[:, :, 0:64], [pk_], [Vaug.k()])
            cp("dve", Vv[:, g4 * 4:(g4 + 1) * 4, 1, 64:128], pv[:, :, 64:128], [pk_], [Vaug.k()])
        QA = [A(S_LEN * 2, "QA%d" % e) for e in range(2)]
        KA = [A(S_LEN * 2, "KA%d" % e) for e in range(2)]
        for e in range(2):
            h_ = hc * 2 + e
            memset("pool", QA[e].ap()[64:70, :], 1.0, [QA[e].k("x")])
            memset("pool", KA[e].ap()[64:70, :], 1.0, [KA[e].k("x")])
            dma(QA[e].ap()[64:67, :], cf_scr[h_, 0:3, :], ["cf_scr"], [QA[e].k("x")])
            dma(KA[e].ap()[67:70, :], cf_scr[h_, 3:6, :], ["cf_scr"], [KA[e].k("x")])
        for tcb in range(4):
            p_, pk_ = ps_next()
            proj_fm(w_q_v, w_q_k, 0, 128, tcb, p_, pk_)
            for e in range(2):
                act(QA[e].ap()[0:64, tcb * 512:(tcb + 1) * 512], p_[e * 64:(e + 1) * 64, :], AF.Identity, [pk_], [QA[e].k(tcb)], scale=0.125)
            p_, pk_ = ps_next()
            proj_fm(w_k_v, w_k_k, 0, 128, tcb, p_, pk_)
            for e in range(2):
                cp("dve", KA[e].ap()[0:64, tcb * 512:(tcb + 1) * 512], p_[e * 64:(e + 1) * 64, :], [pk_], [KA[e].k(tcb)])
        if hc == 0:
            dump("QA", QA[0].ap(), [128, S_LEN], QA[0].k(3))
            dump("KA", KA[0].ap(), [128, S_LEN], KA[0].k(3))
        if hc + 1 < 4:
            hc_w = load_hc(hc + 1)
        else:
            w_g_, w_g_v, w_g_k = load_w(w_glu, 0, 8, 4, "w_glu")
            w_z, w_z_v, w_z_k = load_w(w_in, 4, 4, 8, "w_zs")
        steps = [(qc, e, kt) for qc in range(4) for e in range(2) for kt in range(4 * qc + 4)]
        qk = {}

        def issue_qk(si):
            qc, e, kt = steps[si]
            q0 = qc * 512
            d_ = kt - 4 * qc
            coff = max(0, d_) * 128
            N = 512 - coff
            p_, pk_ = ps_qk()
            mm(p_[:, 0:N], KA[e].ap()[0:70, kt * 128:(kt + 1) * 128], QA[e].ap()[0:70, q0 + coff:q0 + 512], True, d_ < 0,
               [KA[e].k("x"), KA[e].k(kt // 4), QA[e].k("x"), QA[e].k(qc)], [pk_])
            if d_ >= 0:
                mm(p_[:, 0:128], ident_bf.ap(), cmask_bf.ap(), False, True, [ident_bf.k(), cmask_bf.k()], [pk_])
            qk[si] = (p_, pk_, coff, N)
        issue_qk(0); issue_qk(1); issue_qk(2)
        osbs = {}
        po_dd = {}
        for si, (qc, e, kt) in enumerate(steps):
            q0 = qc * 512
            nkt = 4 * qc + 4
            pacc, pacck = psum[6 + e], ("ps", 6 + e)
            if (qc, 0) not in osbs and e == 0 and kt == 0:
                osbs[qc] = tmp(512, "osb")
            osb = osbs[qc]
            p_, pk_, coff, N = qk.pop(si)
            PT = A(512 * 2, "PT")
            act(PT.ap()[:, 0:N], p_[:, 0:N], AF.Exp, [pk_], [PT.k()])
            if si + 3 < len(steps):
                issue_qk(si + 3)
            mm(pacc[:, coff:512], Vv[:, kt, e, :], PT.ap()[:, 0:N], kt == 0, kt == nkt - 1, [Vaug.k(), PT.k()], [pacck])
            AR.release(PT)
            kt_count[0] += 1
            if kt_count[0] % TICK == 0:
                next(s5g, None)
            if kt == nkt - 1:
                lo, hi = (0, 64) if e == 0 else (64, 128)
                dl, dh = (64, 128) if e == 0 else (0, 64)
                if e == 0:
                    po_dd[qc] = (tmp(512, "po"), tmp(512, "dd"))
                po, dd = po_dd[qc]
                act(po.ap(F32)[lo:hi, :], pacc[lo:hi, :], AF.Identity, [pacck], [po.k(e)])
                cp("dve" if DDV else "act", dd.ap(F32)[lo:hi, :], pacc[dl:dh, :], [pacck], [dd.k(e)])
                if e == 1:
                    S.add("dve", lambda h, dd=dd: h.reciprocal(out=dd.ap(F32), in_=dd.ap(F32)),
                          reads=[dd.k(0), dd.k(1)], writes=[dd.k(0), dd.k(1)], cost=3.4)
                    tt("dve", osb.ap(F32), po.ap(F32), dd.ap(F32), ALU.mult, [po.k(0), po.k(1), dd.k(0), dd.k(1)], [osb.k(0), osb.k(1)])
                    AR.release(po); AR.release(dd)
                if e == 1:
                    pz, pzk = ps_next()
                    proj_fm(w_za_v, w_za_k, 0, 128, qc, pz, pzk)
                    sz = tmp(512, "sza")
                    act(sz.ap(F32), pz[:, :], AF.Tanh, [pzk], [sz.k()], scale=0.5)
                    stt("dve", sz.ap(F32), sz.ap(F32), 1.0, pz[:, :], ALU.add, ALU.mult, [sz.k(), pzk], [sz.k()])
                    stt("dve", oz[hc].ap()[:, q0:q0 + 512], sz.ap(F32), 0.5, osb.ap(F32), ALU.mult, ALU.mult,
                        [osb.k(0), osb.k(1), sz.k()], [oz[hc].k(qc)])
                    AR.release(osb); AR.release(sz)
        for b in QA + KA + [w_q, w_k, w_za, w_v, Vaug]:
            AR.release(b)
    for _ in s5g:
        pass
    ps_mode[0] = "all8"
    for hc in range(4):
        dump("oz%d" % hc, oz[hc].ap(), [128, S_LEN], oz[hc].k(3))
    AR.release(w_xs)
    for cc in range(4):
        dump("yT%d" % cc, yT[cc].ap(), [128, S_LEN], yT[cc].k())

    s5o = [A(S_LEN * 2, "s5o%d" % i) for i in range(4)]
    for fc in range(4):
        for tcb in range(4):
            pa, pak = ps_next(); pb, pbk = ps_next(); pz, pzk = ps_next()
            proj_fm(w_g_v, w_g_k, fc * 128, 128, tcb, pa, pak, kcn=4, src=yT, srckeys=lambda kc, t: yT[kc].k())
            proj_fm(w_g_v, w_g_k, 512 + fc * 128, 128, tcb, pb, pbk, kcn=4, src=yT, srckeys=lambda kc, t: yT[kc].k())
            proj_fm(w_z_v, w_z_k, fc * 128, 128, tcb, pz, pzk)
            sb_ = tmp(512, "sgb"); sz = tmp(512, "sz"); t_ = tmp(512, "glt")
            act(sb_.ap(F32), pb[:, :], AF.Sigmoid, [pbk], [sb_.k()])
            act(sz.ap(F32), pz[:, :], AF.Silu, [pzk], [sz.k()])
            tt("dve", t_.ap(F32), pa[:, :], sb_.ap(F32), ALU.mult, [pak, sb_.k()], [t_.k()])
            tt("pool", s5o[fc].ap()[:, tcb * 512:(tcb + 1) * 512], t_.ap(F32), sz.ap(F32), ALU.mult, [t_.k(), sz.k()], [s5o[fc].k(tcb)])
            for b in (sb_, sz, t_):
                AR.release(b)
    AR.release(w_g_); AR.release(w_z)
    for b in yT:
        AR.release(b)
    for fc in range(4):
        dump("s5o%d" % fc, s5o[fc].ap(), [128, S_LEN], s5o[fc].k(3))


    gate_row = A(1024 * 4, "gate_row")
    brow = A(1024 * 4, "brow")
    dma(brow.ap(F32)[0:1, :], b_ada_row[0:1, 2048:3072], [], [brow.k()])
    for hf in range(2):
        sg = A(8 * 512 * 4, "wada_stg")
        sv = sg.ap(F32).rearrange("p (k n) -> p k n", n=512)
        dma(sg.ap(F32).rearrange("p (j c) -> p j c", j=4),
            w_ada[2048 + hf * 512:2048 + (hf + 1) * 512, :].rearrange("(j p) c -> p j c", p=128), [], [sg.k()])
        s16 = A(8 * 512 * 2, "wada_bf")
        s16v = s16.ap().rearrange("p (k n) -> p k n", n=512)
        cp("dve" if hf % 2 == 0 else "act", s16v.rearrange("p k (j n) -> p k j n", j=4),
           sg.ap(F32).rearrange("p (j k n) -> p k j n", j=4, k=8), [sg.k()], [s16.k()])
        AR.release(sg)
        pg, pgk = ps_next()
        for kc in range(8):
            mm(pg[0:1, :], cact_bf.ap()[:, kc:kc + 1], s16v[:, kc, :], kc == 0, kc == 7, [s16.k(), cact_bf.k()], [pgk])
        AR.release(s16)
        tt("dve", gate_row.ap(F32)[0:1, hf * 512:(hf + 1) * 512], pg[0:1, :], brow.ap(F32)[0:1, hf * 512:(hf + 1) * 512], ALU.add,
           [pgk, brow.k()], [gate_row.k(hf)])
    AR.release(brow)
    ones_row = A(128 * 4, "ones_row")
    memset("pool", ones_row.ap(F32)[0:1, :], 1.0, [ones_row.k()])
    gate_b = A(1024 * 4, "gate_b")
    for hf in range(2):
        p_, pk_ = ps_next()
        mm(p_[:, :], ones_row.ap(F32)[0:1, :], gate_row.ap(F32)[0:1, hf * 512:(hf + 1) * 512], True, True,
           [ones_row.k(), gate_row.k(hf)], [pk_])
        cp("dve", gate_b.ap(F32)[:, hf * 512:(hf + 1) * 512], p_[:, :], [pk_], [gate_b.k()])
    AR.release(gate_row); AR.release(ones_row)
    lng = A(1024 * 4, "lng"); lnb = A(1024 * 4, "lnb")
    dma(lng.ap(F32), ln_g[0:1, :].broadcast_to([128, D]), [], [lng.k()])
    dma(lnb.ap(F32), ln_b[0:1, :].broadcast_to([128, D]), [], [lnb.k()])
    wo = A(8 * 1024 * 2, "w_out")
    wov = wo.ap().rearrange("p (k n) -> p k n", n=1024)
    wok = []
    for jj in range(0, 8, 4):
        sg = A(4 * 8 * 128 * 4, "w_out_stg")
        dma(sg.ap(F32).rearrange("p (j c) -> p j c", j=4),
            w_out[jj * 128:(jj + 4) * 128, :].rearrange("(j p) c -> p j c", p=128), [], [sg.k()])
        for j in range(4):
            c0 = (jj + j) * 128
            tt("dve", wov[:, :, c0:c0 + 128],
               sg.ap(F32).rearrange("p (j k n) -> p j k n", j=4, k=8)[:, j, :, :],
               gate_b.ap(F32)[:, c0:c0 + 128].unsqueeze(1).broadcast_to([128, 8, 128]), ALU.mult,
               [sg.k(), gate_b.k()], [wo.k((jj, j))])
            wok.append(wo.k((jj, j)))
        AR.release(sg)
    mg = [A(S_LEN * 2, "mg%d" % i) for i in range(8)]
    def load_fc(fc):
        return (load_w(w_ps, fc, 1, 4, "w_ps", eng="dve"), load_w(w_pa, fc, 1, 4, "w_pa", eng="dve"),
                load_w(w_in, 24 + fc, 1, 8, "w_gs", eng="dve"), load_w(w_in, 32 + fc, 1, 8, "w_ga", eng="dve"))
    fc_w = load_fc(0)
    for fc in range(8):
        (w1, w1v, w1k), (w2, w2v, w2k), (wgs, wgsv, wgsk), (wga, wgav, wgak) = fc_w
        if fc + 1 < 8 and PREFETCH_FC:
            fc_w = load_fc(fc + 1)
        for tcb in range(4):
            p1, p1k = ps_next(); p2, p2k = ps_next(); p3, p3k = ps_next(); p4, p4k = ps_next()
            proj_fm(w1v, w1k, 0, 128, tcb, p1, p1k, kcn=4, src=s5o, srckeys=lambda kc, t: s5o[kc].k(t))
            proj_fm(w2v, w2k, 0, 128, tcb, p2, p2k, kcn=4, src=oz, srckeys=lambda kc, t: oz[kc].k(t))
            proj_fm(wgsv, wgsk, 0, 128, tcb, p3, p3k)
            proj_fm(wgav, wgak, 0, 128, tcb, p4, p4k)
            s1 = tmp(512, "sg1"); s2 = tmp(512, "sg2")
            act(s1.ap(F32), p3[:, :], AF.Sigmoid, [p3k], [s1.k()])
            act(s2.ap(F32), p4[:, :], AF.Sigmoid, [p4k], [s2.k()])
            tt("dve", s1.ap(F32), p1[:, :], s1.ap(F32), ALU.mult, [p1k, s1.k()], [s1.k()])
            tt("dve", s2.ap(F32), p2[:, :], s2.ap(F32), ALU.mult, [p2k, s2.k()], [s2.k()])
            tt("pool" if tcb % 2 else "dve", mg[fc].ap()[:, tcb * 512:(tcb + 1) * 512], s1.ap(F32), s2.ap(F32), ALU.add,
               [s1.k(), s2.k()], [mg[fc].k(tcb)])
            for b in (s1, s2):
                AR.release(b)
        for b in (w1, w2, wgs, wga):
            AR.release(b)
        if fc + 1 < 8 and not PREFETCH_FC:
            fc_w = load_fc(fc + 1)
    for b in s5o + oz + uT:
        AR.release(b)
    for tcb in range(4):
        xts = []; pres = []
        for t4 in range(4):
            tti = tcb * 4 + t4
            xt = A(1024 * 4, "xt2")
            dma(xt.ap(F32), x[tti * 128:(tti + 1) * 128, :], [], [xt.k()])
            pre = A(1024 * 4, "pre")
            for hf in range(2):
                p_, pk_ = ps_next()
                for kc in range(8):
                    mm(p_[:, :], mg[kc].ap()[:, tti * 128:(tti + 1) * 128], wov[:, kc, hf * 512:(hf + 1) * 512], kc == 0, kc == 7,
                       list(wok) + [mg[kc].k(tcb)], [pk_])
                stt("dve", pre.ap(F32)[:, hf * 512:(hf + 1) * 512], xt.ap(F32)[:, hf * 512:(hf + 1) * 512], ALPHA, p_[:, :],
                    ALU.mult, ALU.add, [xt.k(), pk_], [pre.k(hf)])
            xts.append(xt); pres.append(pre)
        stats = layer_norm_stats_multi([(pre.ap(F32), [pre.k(0), pre.k(1)]) for pre in pres], "b")
        for t4 in range(4):
            xt = xts[t4]; pre = pres[t4]; rstd, nmr = stats[t4]
            act(xt.ap(F32), pre.ap(F32), AF.Identity, [pre.k(0), pre.k(1), rstd.k(), nmr.k()], [xt.k()],
                bias=nmr.ap(F32)[:, 0:1], scale=rstd.ap(F32)[:, 0:1])
        for t4 in range(4):
            xt = xts[t4]
            tt("dve" if (tcb == 3 or t4 % 2) else "pool", xt.ap(F32), xt.ap(F32), lng.ap(F32), ALU.mult, [xt.k(), lng.k()], [xt.k()])
        for t4 in range(4):
            tti = tcb * 4 + t4
            xt = xts[t4]; pre = pres[t4]; rstd, nmr = stats[t4]
            tt("dve", pre.ap(F32), xt.ap(F32), lnb.ap(F32), ALU.add, [xt.k(), lnb.k()], [pre.k(0), pre.k(1)])
            dma(out[tti * 128:(tti + 1) * 128, :], pre.ap(F32), [pre.k(0), pre.k(1)], [])
            for b in (xt, pre, rstd, nmr):
                AR.release(b)

    S.emit(nc)
    st.close()
    return nc, dbg_out, AR.peak


def _host_consts():
    ident = np.eye(128, dtype=np.float32)
    sel = np.zeros((128, 16, 128), np.float32)
    for a in range(4):
        for b in range(4):
            for r in range(32):
                sel[a * 32 + r, a * 4 + b, b * 32 + r] = 1.0
    kk = np.arange(128)
    cmask = np.where(kk[None, :] >= kk[:, None], 0.0, -30000.0).astype(np.float32)
    tmask = np.zeros((128, 256), np.float32)
    for i4 in range(4):
        for j in range(8):
            if j >= i4:
                tmask[i4 * 32:(i4 + 1) * 32, j * 32:(j + 1) * 32] = 1.0
    kv = np.array([0, -1, -2, -3, -4, -5, -6, -7] + list(range(9)) + [7, 6, 5, 4, 3, 2, 1, 0], np.float32)
    kvals = np.tile(kv[None, :], (128, 1))
    cvals = np.tile(np.arange(256, dtype=np.float32)[None, :], (128, 1))
    return dict(ident=ident, sel=sel.reshape(128, 16 * 128), cmask=cmask, tmask=tmask, kvals=kvals, cvals=cvals)


def _pair_layout_vec(v):
    return np.ascontiguousarray(v.reshape(16, 2, 64).transpose(1, 2, 0).reshape(128, 16))


def _blk(m):
    o = np.zeros((2, 64, 16, 2, 16), np.float32)
    mm_ = m.reshape(16, 2, 64, 16)
    for gg in range(2):
        o[gg, :, :, gg, :] = mm_[:, gg].transpose(1, 0, 2)
    return np.ascontiguousarray(o.reshape(128, 16 * 32))


def _make_in_maps(inp):
    f = lambda a: np.ascontiguousarray(np.asarray(a, dtype=np.float32))
    consts = _host_consts()
    shared = dict(consts)
    def blockify(W):
        K, N = W.shape
        return np.ascontiguousarray(W.reshape(K // 128, 128, N // 128, 128).transpose(2, 1, 0, 3).reshape(N, K))
    w_in_full = f(inp["w_in"][0])
    segs = [(0, 512), (512, 1024), (1024, 1536), (1536, 2048), (2048, 2560), (2568, 3080), (3080, 4104), (4104, 5128)]
    shared["w_in"] = np.concatenate([blockify(w_in_full[:, a:b]) for a, b in segs], axis=0)
    shared["w_f"] = np.ascontiguousarray(w_in_full[:, 2560:2568].reshape(8, 128, 8).transpose(1, 0, 2).reshape(128, 64))
    shared["w_ada"] = blockify(f(inp["w_ada"][0]))
    shared["b_adaT"] = f(inp["b_ada"][0].reshape(24, 128).T)
    shared["b_ada_row"] = f(inp["b_ada"][0].reshape(1, 3 * D))
    shared["b_f"] = f(inp["b_f"][0].reshape(8, 1))
    shared["lam_re"] = _pair_layout_vec(f(inp["lam_re"][0]))
    shared["lam_im"] = _pair_layout_vec(f(inp["lam_im"][0]))
    shared["ldt"] = _pair_layout_vec(np.repeat(f(inp["log_dt"][0])[:, None], 64, axis=1))
    shared["Bre"] = _blk(f(inp["ssm_b_re"][0])); shared["Bim"] = _blk(f(inp["ssm_b_im"][0]))
    shared["Cre"] = _blk(f(inp["ssm_c_re"][0]).transpose(0, 2, 1)); shared["Cim"] = _blk(f(inp["ssm_c_im"][0]).transpose(0, 2, 1))
    dvec = f(inp["ssm_d"][0]).reshape(16, 32)
    shared["dpair"] = np.ascontiguousarray(np.tile(dvec.T, (4, 1)))
    shared["w_glu"] = blockify(f(inp["w_glu"][0])); shared["w_ps"] = blockify(f(inp["w_proj_ssm"][0]))
    shared["w_pa"] = blockify(f(inp["w_proj_attn"][0]))
    shared["w_out"] = blockify(f(inp["w_out"][0])); shared["ln_g"] = f(inp["ln_g"][0].reshape(1, D)); shared["ln_b"] = f(inp["ln_b"][0].reshape(1, D))
    maps = []
    xs = f(inp["x"]); cs = f(inp["c"])
    for b in range(8):
        m = dict(shared)
        m["x"] = xs[b]
        m["cT"] = np.ascontiguousarray(cs[b].reshape(8, 128).T)
        maps.append(m)
    return maps


_CACHE = {}


def kernel(**inputs):
    if "nc" not in _CACHE:
        _CACHE["nc"] = build()[0]
    nc = _CACHE["nc"]
    maps = _make_in_maps(inputs)
    res = run_bass_kernel_spmd(nc, maps, core_ids=list(range(8)))
    return np.stack([np.asarray(r["out"], dtype=np.float32) for r in res.results], axis=0)
```

```python
import contextlib
import math
import numpy as np
import concourse.bass as bass
import concourse.mybir as mybir
from concourse.bass_utils import run_bass_kernel_spmd

F32 = mybir.dt.float32
BF16 = mybir.dt.bfloat16
I32 = mybir.dt.int32
ALU = mybir.AluOpType
AF = mybir.ActivationFunctionType

COMPUTE = ("pe", "act", "dve", "pool")
DMAQ = ("sp",)
ENGS = COMPUTE + DMAQ
N_DMA_SEMS = 32
import os
PREFETCH_FC = os.environ.get("PF", "1") == "1"
TICK = int(os.environ.get("TICK", "4"))
UTA = int(os.environ.get("UTA", "2"))
DDV = os.environ.get("DDV", "0") == "1"
PRETICK = int(os.environ.get("PRETICK", "0"))

S_LEN = 2048
D = 1024
NKV = 25
ALPHA = 2.0 ** 0.25
LN_EPS = 1e-5
TWO_PI = 2.0 * math.pi
CW_C1 = 6.28125
CW_C2 = TWO_PI - CW_C1


class Op:
    __slots__ = ("eng", "fn", "preds", "idx", "gid", "signal", "is_dma", "dsem", "dval", "dprev",
                 "cost", "tag", "nbytes", "prio", "nsucc", "succs", "npend", "ready", "fin", "pos")

    def __init__(self, eng, fn, idx, gid, is_dma, cost, tag, nbytes):
        self.eng = eng; self.fn = fn; self.idx = idx; self.gid = gid
        self.preds = set()
        self.signal = False; self.is_dma = is_dma
        self.dsem = None; self.dval = None; self.dprev = None
        self.cost = cost; self.tag = tag; self.nbytes = nbytes
        self.prio = 0.0; self.succs = []; self.npend = 0; self.ready = 0.0; self.fin = 0.0; self.pos = -1


ACT_SETS = {"Exp": 0, "Tanh": 0, "Identity": -1, "Copy": -1, "Gelu": 1, "Sigmoid": 2, "Silu": 3, "Sin": 4, "Sqrt": 5, "Ln": 6}
XLAT = float(os.environ.get("XLAT", "0.5"))
DMA_BW = float(os.environ.get("DMABW", "400e3"))
DMA_LAT = 2.0


class Sched:
    def __init__(self):
        self.ops = {e: [] for e in ENGS}
        self.all = []
        self.last_write = {}
        self.reads = {}
        self.dma_counts = [0] * N_DMA_SEMS
        self.reorder = True

    def add(self, eng, fn, reads=(), writes=(), cost=0.1, tag=None, nbytes=0):
        op = Op(eng, fn, len(self.ops[eng]), len(self.all), eng in DMAQ, cost, tag, nbytes)
        for b in reads:
            if b in self.last_write:
                op.preds.add(self.last_write[b])
        for b in writes:
            if b in self.last_write:
                op.preds.add(self.last_write[b])
            for r in self.reads.get(b, ()):
                op.preds.add(r)
        op.preds.discard(op)
        self.ops[eng].append(op)
        self.all.append(op)
        for b in reads:
            self.reads.setdefault(b, []).append(op)
        for b in writes:
            self.last_write[b] = op
            self.reads[b] = []
        return op

    def schedule(self):
        ops = self.all
        for op in ops:
            op.succs = []
        for op in ops:
            for p in op.preds:
                p.succs.append(op)
        for op in reversed(ops):
            m = 0.0
            for s_ in op.succs:
                if s_.prio > m:
                    m = s_.prio
            op.prio = op.cost + m + (DMA_LAT if op.is_dma else 0.0)
        if not self.reorder:
            return {e: list(self.ops[e]) for e in ENGS}
        for op in ops:
            op.npend = len(op.preds); op.ready = 0.0
        cand = {e: [] for e in ENGS}
        for op in ops:
            if op.npend == 0:
                cand[op.eng].append(op)
        free = {e: 0.0 for e in ENGS}
        last_set = [None]
        dma_free = [0.0]
        final = {e: [] for e in ENGS}
        n_left = len(ops)
        while n_left:
            best = None; bkey = None
            for e in ENGS:
                cl = cand[e]
                if not cl:
                    continue
                fe = free[e]
                for op in cl:
                    st_ = op.ready if op.ready > fe else fe
                    pen = 0.0
                    if e == "act" and op.tag is not None:
                        ts_ = ACT_SETS.get(op.tag, 9)
                        if ts_ >= 0 and last_set[0] is not None and ts_ != last_set[0]:
                            pen = 1.3
                    key = (st_ + pen, -op.prio, op.gid)
                    if bkey is None or key < bkey:
                        bkey = key; best = op
            op = best
            e = op.eng
            st_ = max(op.ready, free[e])
            if e == "act" and op.tag is not None:
                ts_ = ACT_SETS.get(op.tag, 9)
                if ts_ >= 0:
                    if last_set[0] is not None and ts_ != last_set[0]:
                        st_ += 1.3
                    last_set[0] = ts_
            if op.is_dma:
                free[e] = st_ + 0.06
                t0 = max(st_ + DMA_LAT, dma_free[0])
                op.fin = t0 + op.nbytes / DMA_BW
                dma_free[0] = op.fin
            else:
                op.fin = st_ + op.cost
                free[e] = op.fin
            cand[e].remove(op)
            op.pos = len(final[e]); final[e].append(op)
            n_left -= 1
            for s_ in op.succs:
                lat = op.fin + (0.0 if (s_.eng == e and not op.is_dma) else XLAT)
                if lat > s_.ready:
                    s_.ready = lat
                s_.npend -= 1
                if s_.npend == 0:
                    cand[s_.eng].append(s_)
        self.makespan = max(op.fin for op in ops)
        return final

    def emit(self, nc):
        final = self.schedule()
        for e in ENGS:
            for i, op in enumerate(final[e]):
                op.pos = i
        rr = 0
        for op in final["sp"]:
            s_ = rr; rr = (rr + 1) % N_DMA_SEMS
            op.dsem = s_; op.dprev = self.dma_counts[s_]
            self.dma_counts[s_] += 16; op.dval = self.dma_counts[s_]
        for op in self.all:
            for p in op.preds:
                if p.eng == "pe" and op.eng == "pe":
                    continue
                p.signal = True
        sigcount = {}
        for e in COMPUTE:
            c = 0; arr = []
            for op in final[e]:
                if op.signal:
                    c += 1
                arr.append(c)
            sigcount[e] = arr
        with contextlib.ExitStack() as st:
            sems = {e: st.enter_context(nc.semaphore("s_" + e)) for e in COMPUTE}
            dsems = [st.enter_context(nc.semaphore("d%d" % i)) for i in range(N_DMA_SEMS)]
            block = st.enter_context(nc.Block())
            sched = self

            def run(e, h):
                waited = {}
                for op in final[e]:
                    need = {}
                    for p in op.preds:
                        if p.is_dma:
                            wk = ("d", p.dsem)
                            if need.get(wk, -1) < p.dval:
                                need[wk] = p.dval
                        else:
                            if p.eng == "pe" and e == "pe":
                                continue
                            v = sigcount[p.eng][p.pos]
                            if need.get(p.eng, -1) < v:
                                need[p.eng] = v
                    if op.is_dma and op.dprev > 0:
                        wk = ("d", op.dsem)
                        if need.get(wk, -1) < op.dprev:
                            need[wk] = op.dprev
                    for wk, v in need.items():
                        if waited.get(wk, -1) < v:
                            if isinstance(wk, tuple):
                                h.wait_ge(dsems[wk[1]], v)
                            else:
                                h.wait_ge(sems[wk], v)
                            waited[wk] = v
                    ins = op.fn(h)
                    if op.is_dma:
                        ins.then_inc(dsems[op.dsem], 16)
                    elif op.signal:
                        ins.then_inc(sems[e], 1)
                if e == "sp":
                    for s_ in range(N_DMA_SEMS):
                        if sched.dma_counts[s_] > 0 and waited.get(("d", s_), -1) < sched.dma_counts[s_]:
                            h.wait_ge(dsems[s_], sched.dma_counts[s_])

            @block.tensor
            def _(h):
                run("pe", h)

            @block.scalar
            def _(h):
                run("act", h)

            @block.vector
            def _(h):
                run("dve", h)

            @block.gpsimd
            def _(h):
                run("pool", h)

            @block.sync
            def _(h):
                run("sp", h)


class Buf:
    _uid = 0

    def __init__(self, arena, off, nbytes, prior, name):
        Buf._uid += 1
        self.uid = Buf._uid
        self.arena = arena; self.off = off; self.nbytes = nbytes; self.asize = (nbytes + 63) // 64 * 64
        self.prior = prior; self.keys = set(); self.name = name

    def k(self, sub=None):
        key = (self.uid, sub)
        if key not in self.keys:
            self.keys.add(key)
            self.arena.S.reads[key] = list(self.prior)
        return key

    def ap(self, dt=BF16):
        a = self.arena.t[:, self.off // 2:(self.off + self.nbytes) // 2]
        if dt != BF16:
            a = a.bitcast(dt)
        return a


class Arena:
    def __init__(self, S, tensor, nbytes):
        self.S = S; self.t = tensor
        self.free = [[0, nbytes, [], 0]]
        self.peak = 0; self.used = 0; self.clock = 0

    LONG = {"uT", "yT", "s5o", "oz", "mg", "Vaug", "QA", "KA", "w_xs", "w_out", "Tt", "Wt", "Vt", "Esin", "Ecos",
            "Xs", "Ybf", "xsT", "ident", "ident_bf", "sel_bf", "cmask_bf", "tmask", "kvals", "cvals", "lre", "lim",
            "ldt", "dpair", "badaT", "cT", "Are", "Aim", "ph", "r8", "cr", "ci", "gate_b", "lng", "lnb", "modT", "sc1",
            "eps", "cact", "cact_bf", "w_q", "w_k", "w_za", "w_v", "w_glu", "w_zs", "w_ps", "w_pa", "w_gs", "w_ga"}

    def alloc(self, nbytes, name=""):
        req = nbytes
        nbytes = (nbytes + 63) // 64 * 64
        longlived = name.rstrip("0123456789") in self.LONG
        fr = self.free
        best = None; bkey = None
        for i in range(len(fr)):
            tot = 0; mx = 0; j = i
            while True:
                tot += fr[j][1]; mx = max(mx, fr[j][3])
                if tot >= nbytes:
                    if longlived:
                        key = (-(fr[j][0] + fr[j][1]), 0)
                    else:
                        key = (mx, fr[i][0])
                    if bkey is None or key < bkey:
                        bkey = key; best = (i, j)
                    break
                if j + 1 < len(fr) and fr[j][0] + fr[j][1] == fr[j + 1][0]:
                    j += 1
                else:
                    break
        if best is None:
            raise RuntimeError("arena OOM for %s (%d bytes), used %d" % (name, nbytes, self.used))
        i, j = best
        acc = set()
        need = nbytes
        if longlived:
            end = fr[j][0] + fr[j][1]
            off = end - nbytes
            newfree = fr[:i]
            mid = []
            for k in range(j, i - 1, -1):
                seg = fr[k]
                if need <= 0:
                    mid.append(seg); continue
                acc.update(seg[2])
                if seg[1] <= need:
                    need -= seg[1]
                else:
                    mid.append([seg[0], seg[1] - need, seg[2], seg[3]])
                    need = 0
            newfree += list(reversed(mid))
            newfree += fr[j + 1:]
        else:
            off = fr[i][0]
            newfree = fr[:i]
            for k in range(i, j + 1):
                seg = fr[k]
                if need <= 0:
                    newfree.append(seg); continue
                acc.update(seg[2])
                if seg[1] <= need:
                    need -= seg[1]
                else:
                    newfree.append([seg[0] + need, seg[1] - need, seg[2], seg[3]])
                    need = 0
            newfree += fr[j + 1:]
        self.free = newfree
        self.used += nbytes
        self.peak = max(self.peak, self.used)
        return Buf(self, off, req, list(acc), name)

    def release(self, b):
        acc = list(b.prior) if not b.keys else []
        for key in b.keys:
            if key in self.S.last_write:
                acc.append(self.S.last_write[key])
            acc += self.S.reads.get(key, [])
        acc = list(set(acc))
        self.used -= b.asize
        self.clock += 1
        self.free.append([b.off, b.asize, acc, self.clock])
        self.free.sort(key=lambda x: x[0])


def build(debug=()):
    nc = bass.Bass("TRN2", target_bir_lowering=False)
    S = Sched()

    def din(name, shape):
        return nc.dram_tensor(name, list(shape), F32, kind="ExternalInput").ap()

    x = din("x", [S_LEN, D]); cT = din("cT", [128, 8]); w_ada = din("w_ada", [24 * 128, 8 * 128])
    b_adaT = din("b_adaT", [128, 24]); b_ada_row = din("b_ada_row", [1, 3 * D])
    w_in = din("w_in", [40 * 128, 8 * 128]); w_fd = din("w_f", [128, 64]); b_f = din("b_f", [8, 1])
    lam_re = din("lam_re", [128, 16]); lam_im = din("lam_im", [128, 16]); ldt = din("ldt", [128, 16])
    Bre_d = din("Bre", [128, 16 * 32]); Bim_d = din("Bim", [128, 16 * 32])
    Cre_d = din("Cre", [128, 16 * 32]); Cim_d = din("Cim", [128, 16 * 32])
    dpair_d = din("dpair", [128, 16])
    w_glu = din("w_glu", [8 * 128, 4 * 128]); w_ps = din("w_ps", [8 * 128, 4 * 128]); w_pa = din("w_pa", [8 * 128, 4 * 128])
    w_out = din("w_out", [8 * 128, 8 * 128]); ln_g = din("ln_g", [1, D]); ln_b = din("ln_b", [1, D])
    ident_d = din("ident", [128, 128]); sel_d = din("sel", [128, 16 * 128]); cmask_d = din("cmask", [128, 128])
    tmask_d = din("tmask", [128, 256]); kvals_d = din("kvals", [128, NKV]); cvals_d = din("cvals", [128, 256])
    out = nc.dram_tensor("out", [S_LEN, D], F32, kind="ExternalOutput").ap()
    dbg_out = {}

    ARENA_BYTES = 198 * 1024
    st = contextlib.ExitStack()
    arena_t = st.enter_context(nc.sbuf_tensor("arena", [128, ARENA_BYTES // 2], BF16))
    psum = [st.enter_context(nc.psum_tensor("ps%d" % i, [128, 512], F32)) for i in range(8)]
    ti_ts = [st.enter_context(nc.sbuf_tensor("ra_int%d" % i, [128, 1024], I32)) for i in range(2)]
    ti_rr = [0]
    AR = Arena(S, arena_t, ARENA_BYTES)
    ps_rr = [0]

    ps_mode = ["all8"]
    qk_rr = [0]

    def ps_next():
        if ps_mode[0] == "all8":
            i = ps_rr[0] % 8; ps_rr[0] = (i + 1) % 8
        elif ps_mode[0] == "all":
            i = ps_rr[0] % 6; ps_rr[0] = (i + 1) % 6
        else:
            i = 4 + ps_rr[0] % 2; ps_rr[0] = (ps_rr[0] + 1) % 2
        return psum[i], ("ps", i)

    def ps_qk():
        i = qk_rr[0]; qk_rr[0] = (i + 1) % 4
        return psum[i], ("ps", i)

    def A(nbytes, name=""):
        return AR.alloc(nbytes, name)

    def fcols(ap):
        n = 1
        for d_ in ap.shape[1:]:
            n *= int(d_)
        return n

    def esize(ap):
        return 2 if ap.dtype == BF16 else 4

    def vcost(eng, ap, mult=1.0):
        c = fcols(ap) * mult
        return (0.12 + c / 900.0) if eng == "dve" else (0.15 + c / 300.0)

    def dma(out_ap, in_ap, reads, writes):
        nb = int(out_ap.shape[0]) * fcols(out_ap) * esize(out_ap)
        S.add("sp", lambda h: h.dma_start(out=out_ap, in_=in_ap), reads=reads, writes=writes, cost=0.06, nbytes=nb)

    def mm(out_ap, lhsT, rhs, start, stop, reads, writes):
        n = max(fcols(rhs), 64)
        c = 0.086 if n <= 128 else (0.26 if n == 256 else n / 2100.0 + 0.012)
        if int(lhsT.shape[0]) == 70:
            c += 0.03
        if rhs.dtype == F32:
            c *= 4.0
        S.add("pe", lambda h: h.matmul(out_ap, lhsT, rhs, start=start, stop=stop), reads=reads, writes=writes, cost=c)

    def act(out_ap, in_ap, func, reads, writes, bias=0.0, scale=1.0):
        S.add("act", lambda h: h.activation(out=out_ap, in_=in_ap, func=func, bias=bias, scale=scale),
              reads=reads, writes=writes, cost=(0.08 + fcols(out_ap) / 1150.0) * (3.0 if func == AF.Gelu else 1.0), tag=func.name)

    def tt(eng, out_ap, a, b, op, reads, writes):
        S.add(eng, lambda h: h.tensor_tensor(out=out_ap, in0=a, in1=b, op=op), reads=reads, writes=writes,
              cost=vcost(eng, out_ap))

    def ts(eng, out_ap, a, s1, s2, op0, op1, reads, writes):
        if s2 is None:
            S.add(eng, lambda h: h.tensor_scalar(out=out_ap, in0=a, scalar1=s1, scalar2=None, op0=op0),
                  reads=reads, writes=writes, cost=vcost(eng, out_ap))
        else:
            S.add(eng, lambda h: h.tensor_scalar(out=out_ap, in0=a, scalar1=s1, scalar2=s2, op0=op0, op1=op1),
                  reads=reads, writes=writes, cost=vcost(eng, out_ap))

    def stt(eng, out_ap, a, s, b, op0, op1, reads, writes):
        S.add(eng, lambda h: h.scalar_tensor_tensor(out=out_ap, in0=a, scalar=s, in1=b, op0=op0, op1=op1),
              reads=reads, writes=writes, cost=vcost(eng, out_ap))

    def cp(eng, out_ap, in_ap, reads, writes):
        if eng == "act":
            S.add(eng, lambda h: h.activation(out=out_ap, in_=in_ap, func=AF.Identity), reads=reads, writes=writes,
                  cost=0.08 + fcols(out_ap) / 1150.0, tag="Identity")
        else:
            S.add(eng, lambda h: h.tensor_copy(out=out_ap, in_=in_ap), reads=reads, writes=writes,
                  cost=vcost(eng, out_ap, 0.6 if eng == "dve" else 1.5))

    def memset(eng, ap, val, writes):
        S.add(eng, lambda h: h.memset(ap, val), writes=writes, cost=vcost(eng, ap, 0.5))

    def dump(name, ap, shape, key):
        if name in debug:
            t = nc.dram_tensor("dbg_" + name, list(shape), ap.dtype, kind="ExternalOutput").ap()
            dbg_out[name] = t
            dma(t, ap, [key], [])

    def load_const(dram, cols, name, to_bf=False):
        b = A(cols * 4, name)
        dma(b.ap(F32), dram, [], [b.k()])
        if not to_bf:
            return b
        bb = A(cols * 2, name + "_bf")
        cp("pool", bb.ap(), b.ap(F32), [b.k()], [bb.k()])
        AR.release(b)
        return bb

    ident = load_const(ident_d, 128, "ident")
    ident_bf = A(256, "ident_bf")
    cp("pool", ident_bf.ap(), ident.ap(F32), [ident.k()], [ident_bf.k()])
    sel_bf = load_const(sel_d, 16 * 128, "sel", True)
    cmask_bf = load_const(cmask_d, 128, "cmask", True)
    tmask = load_const(tmask_d, 256, "tmask")
    kvals = load_const(kvals_d, NKV, "kvals")
    cvals = load_const(cvals_d, 256, "cvals")
    lre = load_const(lam_re, 16, "lre"); lim = load_const(lam_im, 16, "lim"); ldtb = load_const(ldt, 16, "ldt")
    dpair = load_const(dpair_d, 16, "dpair")
    badaT = load_const(b_adaT, 24, "badaT")
    cTb = load_const(cT, 8, "cT")
    sel3 = sel_bf.ap().rearrange("p (a d) -> p a d", d=128)

    def SEL(a, b):
        return sel3[:, a * 4 + b, :]

    cast_rr = [0]
    def load_w(blk2d, j0, nblk, kcn, name, eng=("dve", "act")):
        ncols = nblk * 128
        wb = A(kcn * ncols * 2, name)
        wv = wb.ap().rearrange("p (k n) -> p k n", n=ncols)
        step = max(1, (2048 // kcn) // 128)
        jj = 0; pi = 0
        while jj < nblk:
            nb = min(step, nblk - jj)
            sg = A(nb * kcn * 128 * 4, name + "_stg")
            dma(sg.ap(F32).rearrange("p (j c) -> p j c", j=nb),
                blk2d[(j0 + jj) * 128:(j0 + jj + nb) * 128, :].rearrange("(j p) c -> p j c", p=128), [], [sg.k()])
            e = eng if isinstance(eng, str) else eng[cast_rr[0] % len(eng)]
            cast_rr[0] += 1
            cp(e, wv[:, :, jj * 128:(jj + nb) * 128].rearrange("p k (j n) -> p k j n", j=nb),
               sg.ap(F32).rearrange("p (j k n) -> p k j n", j=nb, k=kcn), [sg.k()], [wb.k(pi)])
            AR.release(sg)
            jj += nb; pi += 1
        return wb, wv, [wb.k(i) for i in range(pi)]

    def tmp(cols, name):
        return A(cols * 4, name)

    dt_t = tmp(16, "dt")
    act(dt_t.ap(F32), ldtb.ap(F32), AF.Exp, [ldtb.k()], [dt_t.k()])
    lrdt = tmp(16, "lrdt"); lidt = tmp(16, "lidt")
    tt("dve", lrdt.ap(F32), lre.ap(F32), dt_t.ap(F32), ALU.mult, [lre.k(), dt_t.k()], [lrdt.k()])
    tt("dve", lidt.ap(F32), lim.ap(F32), dt_t.ap(F32), ALU.mult, [lim.k(), dt_t.k()], [lidt.k()])
    NT = 16 * NKV
    kv3 = kvals.ap(F32).unsqueeze(1).broadcast_to([128, 16, NKV])

    def v3(b, n=NKV):
        return b.ap(F32).rearrange("p (a k) -> p a k", k=n)

    def reduce_angle(dst, dstk, src, srck, cols, shift, view):
        y = tmp(cols, "ra_y"); t1 = tmp(cols, "ra_t1"); tf = tmp(cols, "ra_tf")
        ts("dve", y.ap(F32), src, shift, None, ALU.add, None, [srck], [y.k()])
        ts("dve", t1.ap(F32), y.ap(F32), 1.0 / TWO_PI, None, ALU.mult, None, [y.k()], [t1.k()])
        ti_i = ti_rr[0] % 2; ti_rr[0] += 1
        ti_t = ti_ts[ti_i]
        cp("dve", ti_t[:, 0:cols], t1.ap(F32), [t1.k()], ["ra_int%d" % ti_i])
        cp("dve", tf.ap(F32), ti_t[:, 0:cols], ["ra_int%d" % ti_i], [tf.k()])
        stt("dve", t1.ap(F32), tf.ap(F32), -CW_C1, y.ap(F32), ALU.mult, ALU.add, [tf.k(), y.k()], [t1.k()])
        stt("dve", y.ap(F32), tf.ap(F32), -CW_C2, t1.ap(F32), ALU.mult, ALU.add, [tf.k(), t1.k()], [y.k()])
        ts("dve", dst, y.ap(F32), -math.pi, math.pi, ALU.max, ALU.min, [y.k()], [dstk])
        for b in (y, t1, tf):
            AR.release(b)

    magk = tmp(NT, "magk"); angk = tmp(NT, "angk")
    tt("dve", v3(magk), lrdt.ap(F32).unsqueeze(2).broadcast_to([128, 16, NKV]), kv3, ALU.mult, [lrdt.k(), kvals.k()], [magk.k()])
    act(magk.ap(F32), magk.ap(F32), AF.Exp, [magk.k()], [magk.k()])
    tt("dve", v3(angk), lidt.ap(F32).unsqueeze(2).broadcast_to([128, 16, NKV]), kv3, ALU.mult, [lidt.k(), kvals.k()], [angk.k()])
    sred = tmp(NT, "sred"); cred = tmp(NT, "cred")
    reduce_angle(sred.ap(F32), sred.k(), angk.ap(F32), angk.k(), NT, 0.0, None)
    reduce_angle(cred.ap(F32), cred.k(), angk.ap(F32), angk.k(), NT, math.pi / 2, None)
    Are = tmp(NT, "Are"); Aim = tmp(NT, "Aim")
    act(sred.ap(F32), sred.ap(F32), AF.Sin, [sred.k()], [sred.k()])
    act(cred.ap(F32), cred.ap(F32), AF.Sin, [cred.k()], [cred.k()])
    tt("dve", Are.ap(F32), magk.ap(F32), cred.ap(F32), ALU.mult, [magk.k(), cred.k()], [Are.k()])
    tt("dve", Aim.ap(F32), magk.ap(F32), sred.ap(F32), ALU.mult, [magk.k(), sred.k()], [Aim.k()])
    ph = tmp(16, "ph"); r8 = tmp(16, "r8")
    ang8 = tmp(16, "ang8")
    cp("dve", ang8.ap(F32), v3(angk)[:, :, 16], [angk.k()], [ang8.k()])
    reduce_angle(ph.ap(F32), ph.k(), ang8.ap(F32), ang8.k(), 16, 0.0, None)
    cp("dve", r8.ap(F32), v3(magk)[:, :, 16], [magk.k()], [r8.k()])
    AR.release(ang8)
    am1 = tmp(16, "am1"); abi = tmp(16, "abi"); den = tmp(16, "den"); t_a = tmp(16, "t_a"); t_b = tmp(16, "t_b")
    cr = tmp(16, "cr"); ci = tmp(16, "ci")
    ts("dve", am1.ap(F32), v3(Are)[:, :, 9], -1.0, None, ALU.add, None, [Are.k()], [am1.k()])
    cp("dve", abi.ap(F32), v3(Aim)[:, :, 9], [Aim.k()], [abi.k()])
    tt("dve", den.ap(F32), lre.ap(F32), lre.ap(F32), ALU.mult, [lre.k()], [den.k()])
    tt("dve", t_a.ap(F32), lim.ap(F32), lim.ap(F32), ALU.mult, [lim.k()], [t_a.k()])
    tt("dve", den.ap(F32), den.ap(F32), t_a.ap(F32), ALU.add, [den.k(), t_a.k()], [den.k()])
    S.add("dve", lambda h: h.reciprocal(out=den.ap(F32), in_=den.ap(F32)), reads=[den.k()], writes=[den.k()], cost=0.2)
    tt("dve", t_a.ap(F32), am1.ap(F32), lre.ap(F32), ALU.mult, [am1.k(), lre.k()], [t_a.k()])
    tt("dve", t_b.ap(F32), abi.ap(F32), lim.ap(F32), ALU.mult, [abi.k(), lim.k()], [t_b.k()])
    tt("dve", t_a.ap(F32), t_a.ap(F32), t_b.ap(F32), ALU.add, [t_a.k(), t_b.k()], [t_a.k()])
    tt("dve", cr.ap(F32), t_a.ap(F32), den.ap(F32), ALU.mult, [t_a.k(), den.k()], [cr.k()])
    tt("dve", t_a.ap(F32), abi.ap(F32), lre.ap(F32), ALU.mult, [abi.k(), lre.k()], [t_a.k()])
    tt("dve", t_b.ap(F32), am1.ap(F32), lim.ap(F32), ALU.mult, [am1.k(), lim.k()], [t_b.k()])
    tt("dve", t_a.ap(F32), t_a.ap(F32), t_b.ap(F32), ALU.subtract, [t_a.k(), t_b.k()], [t_a.k()])
    tt("dve", ci.ap(F32), t_a.ap(F32), den.ap(F32), ALU.mult, [t_a.k(), den.k()], [ci.k()])
    for b in (am1, abi, den, t_a, t_b, magk, angk, sred, cred, dt_t, lrdt, lidt):
        AR.release(b)

    def cmul(eng, o_re, o_im, ore_k, oim_k, a_re, a_im, b_re, b_im, rk, cols, neg_im=False, shape=None):
        t1 = tmp(cols, "cm1"); t2 = tmp(cols, "cm2")
        v = (lambda b_: b_.ap(F32)) if shape is None else (lambda b_: b_.ap(F32).rearrange(shape[0], **shape[1]))
        tt(eng, v(t1), a_re, b_re, ALU.mult, rk, [t1.k()])
        tt(eng, v(t2), a_im, b_im, ALU.mult, rk, [t2.k()])
        tt(eng, o_re, v(t1), v(t2), ALU.subtract, [t1.k(), t2.k()], [ore_k])
        tt(eng, v(t1), a_re, b_im, ALU.mult, rk, [t1.k()])
        tt(eng, v(t2), a_im, b_re, ALU.mult, rk, [t2.k()])
        if neg_im:
            stt(eng, o_im, v(t1), -1.0, v(t2), ALU.mult, ALU.subtract, [t1.k(), t2.k()], [oim_k])
        else:
            tt(eng, o_im, v(t1), v(t2), ALU.add, [t1.k(), t2.k()], [oim_k])
        AR.release(t1); AR.release(t2)

    Are3 = v3(Are); Aim3 = v3(Aim)
    def load_hc(hc):
        return (load_w(w_in, 8 + hc, 1, 8, "w_q"),
                load_w(w_in, 12 + hc, 1, 8, "w_k"),
                load_w(w_in, 20 + hc, 1, 8, "w_za"),
                load_w(w_in, 16 + hc, 1, 8, "w_v"))
    hc_w = load_hc(0)
    cact = A(8 * 4, "cact")
    act(cact.ap(F32), cTb.ap(F32), AF.Silu, [cTb.k()], [cact.k()])
    cact_bf = A(8 * 2, "cact_bf")
    cp("dve", cact_bf.ap(), cact.ap(F32), [cact.k()], [cact_bf.k()])
    mrow = A(2048 * 4, "mrow")
    browA = A(2048 * 4, "browA")
    dma(browA.ap(F32)[0:1, :], b_ada_row[0:1, 0:2048], [], [browA.k()])
    for blk in range(4):
        sg = A(8 * 512 * 4, "wada_stg")
        sv = sg.ap(F32).rearrange("p (k n) -> p k n", n=512)
        dma(sg.ap(F32).rearrange("p (j c) -> p j c", j=4),
            w_ada[blk * 512:(blk + 1) * 512, :].rearrange("(j p) c -> p j c", p=128), [], [sg.k()])
        s16 = A(8 * 512 * 2, "wada_bf")
        s16v = s16.ap().rearrange("p (k n) -> p k n", n=512)
        cp("dve" if blk % 2 == 0 else "act", s16v.rearrange("p k (j n) -> p k j n", j=4),
           sg.ap(F32).rearrange("p (j k n) -> p k j n", j=4, k=8), [sg.k()], [s16.k()])
        AR.release(sg)
        pg, pgk = ps_next()
        for kc in range(8):
            mm(pg[0:1, :], cact_bf.ap()[:, kc:kc + 1], s16v[:, kc, :], kc == 0, kc == 7, [s16.k(), cact_bf.k()], [pgk])
        AR.release(s16)
        tt("dve", mrow.ap(F32)[0:1, blk * 512:(blk + 1) * 512], pg[0:1, :], browA.ap(F32)[0:1, blk * 512:(blk + 1) * 512], ALU.add,
           [pgk, browA.k()], [mrow.k(blk)])
    one1 = A(4, "one1")
    memset("pool", one1.ap(F32)[0:1, :], 1.0, [one1.k()])
    modT = A(16 * 4, "modT")
    pm, pmk = ps_next()
    for j in range(16):
        mm(pm[:, j:j + 1], mrow.ap(F32)[0:1, j * 128:(j + 1) * 128], one1.ap(F32)[0:1, 0:1], True, True,
           [mrow.k(j // 4), one1.k()], [pmk])
    cp("dve", modT.ap(F32), pm[:, 0:16], [pmk], [modT.k()])
    sc1 = A(8 * 4, "sc1")
    ts("dve", sc1.ap(F32), modT.ap(F32)[:, 8:16], 1.0, None, ALU.add, None, [modT.k()], [sc1.k()])
    AR.release(mrow); AR.release(browA); AR.release(one1)
    uT = [A(S_LEN * 2, "uT%d" % i) for i in range(8)]
    eps_t = A(4, "eps")
    memset("pool", eps_t.ap(F32), LN_EPS, [eps_t.k()])

    def layer_norm_stats_multi(srcs, tag):
        n = len(srcs)
        stts = [A(2 * 6 * 4, "bnst" + tag) for _ in range(n)]
        mvs = [A(2 * 4, "mv" + tag) for _ in range(n)]
        sds = [A(4, "sd" + tag) for _ in range(n)]
        rstds = [A(4, "rstd" + tag) for _ in range(n)]
        nmrs = [A(4, "nmr" + tag) for _ in range(n)]
        for i, (src_ap, src_key) in enumerate(srcs):
            sv = stts[i].ap(F32).rearrange("p (a b) -> p a b", b=6)
            for hf in range(2):
                S.add("dve", lambda h, hf=hf, sv=sv, src_ap=src_ap: h.bn_stats(out=sv[:, hf, :], in_=src_ap[:, hf * 512:(hf + 1) * 512]),
                      reads=[src_key[hf] if isinstance(src_key, list) else src_key], writes=[stts[i].k(hf)], cost=0.65)
            S.add("dve", lambda h, i=i: h.bn_aggr(out=mvs[i].ap(F32), in_=stts[i].ap(F32)),
                  reads=[stts[i].k(0), stts[i].k(1)], writes=[mvs[i].k()], cost=0.2)
        for i in range(n):
            act(sds[i].ap(F32), mvs[i].ap(F32)[:, 1:2], AF.Sqrt, [mvs[i].k(), eps_t.k()], [sds[i].k()], bias=eps_t.ap(F32)[:, 0:1])
        for i in range(n):
            S.add("dve", lambda h, i=i: h.reciprocal(out=rstds[i].ap(F32), in_=sds[i].ap(F32)), reads=[sds[i].k()], writes=[rstds[i].k()], cost=0.15)
            stt("dve", nmrs[i].ap(F32), mvs[i].ap(F32)[:, 0:1], -1.0, rstds[i].ap(F32), ALU.mult, ALU.mult,
                [mvs[i].k(), rstds[i].k()], [nmrs[i].k()])
        for b in stts + mvs + sds:
            AR.release(b)
        return list(zip(rstds, nmrs))

    for g4 in range(4):
        xn = A(4 * 1024 * 2, "xn")
        xnv = xn.ap().rearrange("p (a n) -> p a n", n=1024)
        xts = []
        for t4 in range(4):
            tti = g4 * 4 + t4
            xt = A(1024 * 4, "xt")
            dma(xt.ap(F32), x[tti * 128:(tti + 1) * 128, :], [], [xt.k()])
            xts.append(xt)
        stats = layer_norm_stats_multi([(xt.ap(F32), xt.k()) for xt in xts], "a")
        for t4 in range(4):
            xt = xts[t4]; rstd, nmr = stats[t4]
            act(xnv[:, t4, :], xt.ap(F32), AF.Identity, [xt.k(), rstd.k(), nmr.k()], [xn.k(t4)],
                bias=nmr.ap(F32)[:, 0:1], scale=rstd.ap(F32)[:, 0:1])
            AR.release(xt); AR.release(rstd); AR.release(nmr)
        for kc in range(8):
            p_, pk_ = ps_next()
            for t4 in range(4):
                mm(p_[:, t4 * 128:(t4 + 1) * 128], xnv[:, t4, kc * 128:(kc + 1) * 128], ident_bf.ap(), True, True,
                   [xn.k(t4), ident_bf.k()], [pk_])
            if UTA and kc % UTA == UTA - 1:
                act(uT[kc].ap()[:, g4 * 512:(g4 + 1) * 512], p_[:, :], AF.Identity, [pk_, sc1.k(), modT.k()], [uT[kc].k(g4)],
                    bias=modT.ap(F32)[:, kc:kc + 1], scale=sc1.ap(F32)[:, kc:kc + 1])
            else:
                ts("dve", uT[kc].ap()[:, g4 * 512:(g4 + 1) * 512], p_[:, :],
                   sc1.ap(F32)[:, kc:kc + 1], modT.ap(F32)[:, kc:kc + 1], ALU.mult, ALU.add,
                   [pk_, sc1.k(), modT.k()], [uT[kc].k(g4)])
        AR.release(xn)
    for kc in range(8):
        dump("uT%d" % kc, uT[kc].ap(), [128, S_LEN], uT[kc].k(3))

    def uT_keys(kc, tcb):
        return uT[kc].k(tcb)

    def proj_fm(wv, wkeys, col0, M, tcb, p_, pk_, kcn=8, src=None, srckeys=None):
        for kc in range(kcn):
            rhs = (uT[kc].ap() if src is None else src[kc].ap())[:, tcb * 512:(tcb + 1) * 512]
            rk = uT[kc].k(tcb) if src is None else srckeys(kc, tcb)
            mm(p_[0:M, :], wv[:, kc, col0:col0 + M], rhs, kc == 0, kc == kcn - 1, list(wkeys) + [rk], [pk_])

    w_xs, w_xs_v, w_xs_k = load_w(w_in, 0, 4, 8, "w_xs", eng="act")
    yT = [A(S_LEN * 2, "yT%d" % i) for i in range(4)]
    e3 = lambda b_: b_.ap(F32).rearrange("p (a c) -> p a c", c=256)

    def s5_gen():
      for cc in range(4):
          p0 = cc * 4
          Bre = tmp(128, "Bre"); Bim = tmp(128, "Bim"); Cre = tmp(128, "Cre"); Cim = tmp(128, "Cim")
          dma(Bre.ap(F32), Bre_d[:, p0 * 32:(p0 + 4) * 32], [], [Bre.k()])
          dma(Bim.ap(F32), Bim_d[:, p0 * 32:(p0 + 4) * 32], [], [Bim.k()])
          dma(Cre.ap(F32), Cre_d[:, p0 * 32:(p0 + 4) * 32], [], [Cre.k()])
          dma(Cim.ap(F32), Cim_d[:, p0 * 32:(p0 + 4) * 32], [], [Cim.k()])
          Bbre = tmp(128, "Bbre"); Bbim = tmp(128, "Bbim")
          sh43 = ("p (a b) -> p a b", dict(b=32))
          b3 = lambda b_: b_.ap(F32).rearrange("p (a b) -> p a b", b=32)
          crb = cr.ap(F32)[:, p0:p0 + 4].unsqueeze(2).broadcast_to([128, 4, 32])
          cib = ci.ap(F32)[:, p0:p0 + 4].unsqueeze(2).broadcast_to([128, 4, 32])
          cmul("dve", b3(Bbre), b3(Bbim), Bbre.k(), Bbim.k(), crb, cib, b3(Bre), b3(Bim),
               [cr.k(), ci.k(), Bre.k(), Bim.k()], 128, shape=sh43)
          yield
          sh8 = ("p (a i b) -> p a i b", dict(i=8, b=32)); sh9 = ("p (a i b) -> p a i b", dict(i=9, b=32))
          v8 = lambda b_: b_.ap(F32).rearrange(sh8[0], **sh8[1])
          v9 = lambda b_: b_.ap(F32).rearrange(sh9[0], **sh9[1])
          def abc(c0, n):
              return (Are3[:, p0:p0 + 4, c0:c0 + n].unsqueeze(3).broadcast_to([128, 4, n, 32]),
                      Aim3[:, p0:p0 + 4, c0:c0 + n].unsqueeze(3).broadcast_to([128, 4, n, 32]))

          def bb(bre, bim, n):
              return (b3(bre).unsqueeze(2).broadcast_to([128, 4, n, 32]), b3(bim).unsqueeze(2).broadcast_to([128, 4, n, 32]))
          Tt = A(4 * 256 * 2, "Tt"); Wt = A(4 * 4 * 128 * 2, "Wt"); Vt = A(2 * 1024 * 2, "Vt")
          Ttv = Tt.ap().rearrange("p (a n) -> p a n", n=256)
          Wtv = Wt.ap().rearrange("p (a b n) -> p a b n", b=4, n=128)
          Vtv = Vt.ap().rearrange("p (s a j b) -> p s a j b", s=2, a=4, j=8)
          WAre = tmp(1024, "WAre"); WAim = tmp(1024, "WAim")
          br_, bi_ = bb(Bbre, Bbim, 8)
          ar_, ai_ = abc(17, 8)
          cmul("dve", v8(WAre), v8(WAim), WAre.k(), WAim.k(), ar_, ai_, br_, bi_, [Are.k(), Aim.k(), Bbre.k(), Bbim.k()], 1024, shape=sh8)
          yield
          yield
          WAb = A(2 * 1024 * 2, "WAb")
          WAbv = WAb.ap().rearrange("p (s a i b) -> p s a i b", s=2, a=4, i=8)
          cp("act", WAbv[:, 0], v8(WAre), [WAre.k()], [WAb.k(0)])
          cp("pool", WAbv[:, 1], v8(WAim), [WAim.k()], [WAb.k(1)])
          AR.release(WAre); AR.release(WAim)
          for pr in range(4):
              p_, pk_ = ps_next()
              for pl in range(2):
                  for kt in range(2):
                      q = pl * 2 + kt
                      mm(p_[:, q * 128:(q + 1) * 128], WAbv[:, pl, pr, kt * 4:(kt + 1) * 4, :].rearrange("p a b -> p (a b)"), ident_bf.ap(),
                         True, True, [WAb.k(pl), ident_bf.k()], [pk_])
              cp("act", Wtv[:, pr, :, :].rearrange("p b n -> p (b n)"), p_[:, :], [pk_], [Wt.k(pr)])
          AR.release(WAb)
          yield
          BAre = tmp(1024, "BAre"); BAim = tmp(1024, "BAim")
          ar_, ai_ = abc(0, 8)
          cmul("dve", v8(BAre), v8(BAim), BAre.k(), BAim.k(), ar_, ai_, br_, bi_, [Are.k(), Aim.k(), Bbre.k(), Bbim.k()], 1024, shape=sh8)
          yield
          yield
          CAre = tmp(1152, "CAre"); CAimN = tmp(1152, "CAimN")
          ar_, ai_ = abc(8, 9); br_, bi_ = bb(Cre, Cim, 9)
          cmul("dve", v9(CAre), v9(CAimN), CAre.k(), CAimN.k(), ar_, ai_, br_, bi_, [Are.k(), Aim.k(), Cre.k(), Cim.k()], 1152, neg_im=True, shape=sh9)
          yield
          yield
          cp("pool", Vtv[:, 0], v9(CAre)[:, :, 1:9, :], [CAre.k()], [Vt.k(0)])
          cp("pool", Vtv[:, 1], v9(CAimN)[:, :, 1:9, :], [CAimN.k()], [Vt.k(1)])
          for pr in range(4):
              p_, pk_ = ps_next()
              mm(p_[:, 0:256], v8(BAre)[:, pr, 0:4, :].rearrange("p a b -> p (a b)"), v9(CAre)[:, pr, 0:8, :].rearrange("p a b -> p (a b)"),
                 True, False, [BAre.k(), CAre.k()], [pk_])
              mm(p_[:, 0:256], v8(BAim)[:, pr, 0:4, :].rearrange("p a b -> p (a b)"), v9(CAimN)[:, pr, 0:8, :].rearrange("p a b -> p (a b)"),
                 False, True, [BAim.k(), CAimN.k()], [pk_])
              tt("dve", Ttv[:, pr, :], p_[:, 0:256], tmask.ap(F32), ALU.mult, [pk_, tmask.k()], [Tt.k(pr)])
          for b in (Bre, Bim, Cre, Cim, Bbre, Bbim, BAre, BAim, CAre, CAimN):
              AR.release(b)
          yield
          Esin = tmp(1024, "Esin"); Ecos = tmp(1024, "Ecos"); ang = tmp(1024, "ang")
          e3 = lambda b_: b_.ap(F32).rearrange("p (a c) -> p a c", c=256)
          tt("dve", e3(ang), ph.ap(F32)[:, p0:p0 + 4].unsqueeze(2).broadcast_to([128, 4, 256]),
             cvals.ap(F32).unsqueeze(1).broadcast_to([128, 4, 256]), ALU.mult, [ph.k(), cvals.k()], [ang.k()])
          reduce_angle(Esin.ap(F32), Esin.k(), ang.ap(F32), ang.k(), 1024, 0.0, None)
          yield
          yield
          reduce_angle(Ecos.ap(F32), Ecos.k(), ang.ap(F32), ang.k(), 1024, math.pi / 2, None)
          act(Esin.ap(F32), Esin.ap(F32), AF.Sin, [Esin.k()], [Esin.k()])
          act(Ecos.ap(F32), Ecos.ap(F32), AF.Sin, [Ecos.k()], [Ecos.k()])
          AR.release(ang)
          yield
          yield
          xsT = A(S_LEN * 2, "xsT")
          for tcb in range(4):
              p_, pk_ = ps_next()
              proj_fm(w_xs_v, w_xs_k, cc * 128, 128, tcb, p_, pk_)
              cp("act", xsT.ap()[:, tcb * 512:(tcb + 1) * 512], p_[:, :], [pk_], [xsT.k(tcb)])
          Xs = A(4 * 2 * 256 * 2, "Xs")
          yield
          Xsv = Xs.ap().rearrange("p (a h c) -> p a h c", h=2, c=256)
          xs_all = [xsT.k(t) for t in range(4)]
          for pw in range(4):
              p_, pk_ = ps_next()
              for hf in range(2):
                  for i4 in range(4):
                      i = hf * 4 + i4
                      mm(p_[:, hf * 256:(hf + 1) * 256], SEL(pw, i4), xsT.ap()[:, i:S_LEN:8], i4 == 0, i4 == 3,
                         xs_all + [sel_bf.k()], [pk_])
              cp("dve" if pw % 2 else "act", Xsv[:, pw, :, :].rearrange("p h c -> p (h c)"), p_[:, :], [pk_], [Xs.k(pw)])
          AR.release(xsT)
          yield
          Ybf = A(4 * 2 * 256 * 2, "Ybf")
          Ybv = Ybf.ap().rearrange("p (a h c) -> p a h c", h=2, c=256)
          stg = []
          for pw in range(4):
              pz, pzk = ps_next()
              for pl in range(2):
                  for kt in range(2):
                      mm(pz[:, pl * 256:(pl + 1) * 256], Wtv[:, pw, pl * 2 + kt, :], Xsv[:, pw, kt, :], kt == 0, kt == 1,
                         [Wt.k(pw), Xs.k(pw)], [pzk])
              zre = pz[:, 0:256]; zim = pz[:, 256:512]
              ec = e3(Ecos)[:, pw, :]; es = e3(Esin)[:, pw, :]
              t1 = tmp(256, "l2a"); t2 = tmp(256, "l2b"); gr = tmp(256, "gr"); gi = tmp(256, "gi")
              t3 = tmp(256, "l2c"); t4 = tmp(256, "l2d")
              tt("dve", t1.ap(F32), zre, ec, ALU.mult, [pzk, Ecos.k()], [t1.k()])
              tt("dve", t2.ap(F32), zim, es, ALU.mult, [pzk, Esin.k()], [t2.k()])
              tt("dve", t3.ap(F32), zim, ec, ALU.mult, [pzk, Ecos.k()], [t3.k()])
              tt("dve", t4.ap(F32), zre, es, ALU.mult, [pzk, Esin.k()], [t4.k()])
              tt("dve", gr.ap(F32), t1.ap(F32), t2.ap(F32), ALU.add, [t1.k(), t2.k()], [gr.k()])
              tt("dve", gi.ap(F32), t3.ap(F32), t4.ap(F32), ALU.subtract, [t3.k(), t4.k()], [gi.k()])
              for b in (t1, t2, t3, t4):
                  AR.release(b)
              stg.append(dict(gr=gr, gi=gi, ec=ec, es=es))
          yield
          for pw in range(4):
              d = stg[pw]; pair = p0 + pw
              Gr = tmp(256, "Gr"); Gi = tmp(256, "Gi")
              r8b = r8.ap(F32)[:, pair:pair + 1].broadcast_to([128, 256])
              S.add("dve", lambda h, Gr=Gr, gr=d["gr"], r8b=r8b: h.tensor_tensor_scan(out=Gr.ap(F32), data0=r8b, data1=gr.ap(F32), initial=0.0, op0=ALU.mult, op1=ALU.add),
                    reads=[d["gr"].k(), r8.k()], writes=[Gr.k()], cost=0.65)
              S.add("dve", lambda h, Gi=Gi, gi=d["gi"], r8b=r8b: h.tensor_tensor_scan(out=Gi.ap(F32), data0=r8b, data1=gi.ap(F32), initial=0.0, op0=ALU.mult, op1=ALU.add),
                    reads=[d["gi"].k(), r8.k()], writes=[Gi.k()], cost=0.65)
              d["Gr"] = Gr; d["Gi"] = Gi
              AR.release(d["gr"]); AR.release(d["gi"])
          yield
          for pw in range(4):
              d = stg[pw]
              Gr, Gi, ec, es = d["Gr"], d["Gi"], d["ec"], d["es"]
              t1 = tmp(256, "l3a"); t2 = tmp(256, "l3b"); t3 = tmp(256, "l3c"); t4 = tmp(256, "l3d")
              Hb = A(2 * 256 * 2, "Hb")
              Hbv = Hb.ap().rearrange("p (s c) -> p s c", c=256)
              memset("pool", Hbv[:, :, 0:1], 0.0, [Hb.k()])
              tt("dve", t1.ap(F32), Gr.ap(F32), ec, ALU.mult, [Gr.k(), Ecos.k()], [t1.k()])
              tt("dve", t2.ap(F32), Gi.ap(F32), es, ALU.mult, [Gi.k(), Esin.k()], [t2.k()])
              tt("dve", Hbv[:, 0, 1:256], t1.ap(F32)[:, 0:255], t2.ap(F32)[:, 0:255], ALU.subtract, [t1.k(), t2.k()], [Hb.k()])
              tt("dve", t3.ap(F32), Gr.ap(F32), es, ALU.mult, [Gr.k(), Esin.k()], [t3.k()])
              tt("dve", t4.ap(F32), Gi.ap(F32), ec, ALU.mult, [Gi.k(), Ecos.k()], [t4.k()])
              tt("dve", Hbv[:, 1, 1:256], t3.ap(F32)[:, 0:255], t4.ap(F32)[:, 0:255], ALU.add, [t3.k(), t4.k()], [Hb.k()])
              for b in (t1, t2, t3, t4, Gr, Gi):
                  AR.release(b)
              d["Hb"] = Hb; d["Hbv"] = Hbv
          yield
          yield
          for pw in range(4):
              d = stg[pw]; pair = p0 + pw
              Hb, Hbv = d["Hb"], d["Hbv"]
              py, pyk = ps_next()
              for mt in range(2):
                  o = py[:, mt * 256:(mt + 1) * 256]
                  jl = slice(mt * 4, mt * 4 + 4)
                  if mt == 0:
                      mm(o, Ttv[:, pw, 0:128], Xsv[:, pw, 0, :], True, False, [Tt.k(pw), Xs.k(pw)], [pyk])
                  else:
                      mm(o, Ttv[:, pw, 128:256], Xsv[:, pw, 0, :], True, False, [Tt.k(pw), Xs.k(pw)], [pyk])
                      mm(o, Ttv[:, pw, 0:128], Xsv[:, pw, 1, :], False, False, [Tt.k(pw), Xs.k(pw)], [pyk])
                  mm(o, Vtv[:, 0, pw, jl, :].rearrange("p j b -> p (j b)"), Hbv[:, 0, :], False, False, [Vt.k(0), Hb.k()], [pyk])
                  mm(o, Vtv[:, 1, pw, jl, :].rearrange("p j b -> p (j b)"), Hbv[:, 1, :], False, True, [Vt.k(1), Hb.k()], [pyk])
              AR.release(Hb)
              for mt in range(2):
                  stt("dve", Ybv[:, pw, mt, :], Xsv[:, pw, mt, :], dpair.ap(F32)[:, pair:pair + 1], py[:, mt * 256:(mt + 1) * 256],
                      ALU.mult, ALU.add, [Xs.k(pw), dpair.k(), pyk], [Ybf.k((pw, mt))])
          yield
          for j in range(8):
              p_, pk_ = ps_next()
              for pw in range(4):
                  mm(p_[:, 0:256], SEL(j % 4, pw), Ybv[:, pw, j // 4, :], pw == 0, pw == 3, [sel_bf.k(), Ybf.k((pw, j // 4))], [pk_])
              act(yT[cc].ap()[:, j:S_LEN:8], p_[:, 0:256], AF.Gelu, [pk_], [yT[cc].k()])
          for b in (Tt, Wt, Vt, Esin, Ecos, Xs, Ybf):
              AR.release(b)
          yield
    s5g = s5_gen()

    w_f32 = A(64 * 4, "w_f32"); dma(w_f32.ap(F32), w_fd, [], [w_f32.k()])
    w_f = A(64 * 2, "w_f"); cp("dve", w_f.ap(), w_f32.ap(F32), [w_f32.k()], [w_f.k()])
    AR.release(w_f32)
    w_f_v = w_f.ap().rearrange("p (k n) -> p k n", n=8); w_f_k = [w_f.k()]
    bfb = A(4, "bf"); dma(bfb.ap(F32)[0:8, :], b_f, [], [bfb.k()])
    nbf = A(4, "nbf")
    ts("dve", nbf.ap(F32)[0:8, :], bfb.ap(F32)[0:8, :], -1.0, None, ALU.mult, None, [bfb.k()], [nbf.k()])
    lg = A(S_LEN * 4, "lg"); cumf = A(S_LEN * 4, "cumf")
    for tcb in range(4):
        p_, pk_ = ps_next()
        proj_fm(w_f_v, w_f_k, 0, 8, tcb, p_, pk_)
        act(lg.ap(F32)[0:8, tcb * 512:(tcb + 1) * 512], p_[0:8, :], AF.Exp, [pk_, nbf.k()], [lg.k(tcb)], bias=nbf.ap(F32)[0:8, 0:1], scale=-1.0)
    one8 = A(4, "one8"); memset("pool", one8.ap(F32)[0:8, :], 1.0, [one8.k()])
    lgk = [lg.k(t) for t in range(4)]
    act(lg.ap(F32)[0:8, :], lg.ap(F32)[0:8, :], AF.Ln, lgk + [one8.k()], lgk, bias=one8.ap(F32)[0:8, 0:1])
    S.add("dve", lambda h: h.tensor_tensor_scan(out=cumf.ap(F32)[0:8, :], data0=one8.ap(F32)[0:8, 0:1].broadcast_to([8, S_LEN]),
                                                data1=lg.ap(F32)[0:8, :], initial=0.0, op0=ALU.mult, op1=ALU.subtract),
          reads=lgk + [one8.k()], writes=[cumf.k()], cost=4.5)
    CF = A(3 * S_LEN * 2, "CF"); NCF = A(3 * S_LEN * 2, "NCF")
    CFv = CF.ap().rearrange("p (a n) -> p a n", n=S_LEN); NCFv = NCF.ap().rearrange("p (a n) -> p a n", n=S_LEN)
    r1 = A(S_LEN * 4, "r1")
    cp("dve", CFv[0:8, 0, :], cumf.ap(F32)[0:8, :], [cumf.k()], [CF.k(0)])
    tt("dve", r1.ap(F32)[0:8, :], cumf.ap(F32)[0:8, :], CFv[0:8, 0, :], ALU.subtract, [cumf.k(), CF.k(0)], [r1.k()])
    cp("dve", CFv[0:8, 1, :], r1.ap(F32)[0:8, :], [r1.k()], [CF.k(1)])
    tt("dve", lg.ap(F32)[0:8, :], r1.ap(F32)[0:8, :], CFv[0:8, 1, :], ALU.subtract, [r1.k(), CF.k(1)], lgk)
    cp("dve", CFv[0:8, 2, :], lg.ap(F32)[0:8, :], lgk, [CF.k(2)])
    ts("dve", NCF.ap()[0:8, :], CF.ap()[0:8, :], -1.0, None, ALU.mult, None, [CF.k(0), CF.k(1), CF.k(2)], [NCF.k()])
    dump("cumf", cumf.ap(F32)[0:8, :], [8, S_LEN], cumf.k())
    cf_scr = nc.dram_tensor("cf_scr", [8, 6, S_LEN], BF16, kind="Internal").ap()
    dma(cf_scr[:, 0:3, :], CFv[0:8, :, :], [CF.k(0), CF.k(1), CF.k(2)], ["cf_scr"])
    dma(cf_scr[:, 3:6, :], NCFv[0:8, :, :], [NCF.k()], ["cf_scr"])
    for b in (lg, r1, cumf, w_f, one8, bfb, nbf, CF, NCF):
        AR.release(b)
    oz = [A(S_LEN * 2, "oz%d" % i) for i in range(4)]
    ps_mode[0] = "gen"
    kt_count = [0]
    for _ in range(PRETICK):
        next(s5g, None)
    for hc in range(4):
        (w_q, w_q_v, w_q_k), (w_k, w_k_v, w_k_k), (w_za, w_za_v, w_za_k), (w_v, w_v_v, w_v_k) = hc_w
        Vaug = A(16 * 2 * 128 * 2, "Vaug")
        Vv = Vaug.ap().rearrange("p (t e n) -> p t e n", e=2, n=128)
        memset("pool", Vaug.ap(), 1.0, [Vaug.k()])
        for g4 in range(4):
            p_, pk_ = ps_next()
            for t4 in range(4):
                tti = g4 * 4 + t4
                for kc in range(8):
                    mm(p_[:, t4 * 128:(t4 + 1) * 128], uT[kc].ap()[:, tti * 128:(tti + 1) * 128], w_v_v[:, kc, :], kc == 0, kc == 7,
                       list(w_v_k) + [uT[kc].k(g4)], [pk_])
            pv = p_[:, :].rearrange("p (t n) -> p t n", n=128)
            cp("dve", Vv[:, g4 * 4:(g4 + 1) * 4, 0, 0:64], pv[:, :, 0:64], [pk_], [Vaug.k()])
            cp("dve", Vv[:, g4 * 4:(g4 + 1) * 4, 1, 64:128], pv[:, :, 64:128], [pk_], [Vaug.k()])
        QA = [A(S_LEN * 2, "QA%d" % e) for e in range(2)]
        KA = [A(S_LEN * 2, "KA%d" % e) for e in range(2)]
        for e in range(2):
            h_ = hc * 2 + e
            memset("pool", QA[e].ap()[64:70, :], 1.0, [QA[e].k("x")])
            memset("pool", KA[e].ap()[64:70, :], 1.0, [KA[e].k("x")])
            dma(QA[e].ap()[64:67, :], cf_scr[h_, 0:3, :], ["cf_scr"], [QA[e].k("x")])
            dma(KA[e].ap()[67:70, :], cf_scr[h_, 3:6, :], ["cf_scr"], [KA[e].k("x")])
        for tcb in range(4):
            p_, pk_ = ps_next()
            proj_fm(w_q_v, w_q_k, 0, 128, tcb, p_, pk_)
            for e in range(2):
                act(QA[e].ap()[0:64, tcb * 512:(tcb + 1) * 512], p_[e * 64:(e + 1) * 64, :], AF.Identity, [pk_], [QA[e].k(tcb)], scale=0.125)
            p_, pk_ = ps_next()
            proj_fm(w_k_v, w_k_k, 0, 128, tcb, p_, pk_)
            for e in range(2):
                cp("dve", KA[e].ap()[0:64, tcb * 512:(tcb + 1) * 512], p_[e * 64:(e + 1) * 64, :], [pk_], [KA[e].k(tcb)])
        if hc == 0:
            dump("QA", QA[0].ap(), [128, S_LEN], QA[0].k(3))
            dump("KA", KA[0].ap(), [128, S_LEN], KA[0].k(3))
        if hc + 1 < 4:
            hc_w = load_hc(hc + 1)
        else:
            w_g_, w_g_v, w_g_k = load_w(w_glu, 0, 8, 4, "w_glu")
            w_z, w_z_v, w_z_k = load_w(w_in, 4, 4, 8, "w_zs")
        steps = [(qc, e, kt) for qc in range(4) for e in range(2) for kt in range(4 * qc + 4)]
        qk = {}

        def issue_qk(si):
            qc, e, kt = steps[si]
            q0 = qc * 512
            d_ = kt - 4 * qc
            coff = max(0, d_) * 128
            N = 512 - coff
            p_, pk_ = ps_qk()
            mm(p_[:, 0:N], KA[e].ap()[0:70, kt * 128:(kt + 1) * 128], QA[e].ap()[0:70, q0 + coff:q0 + 512], True, d_ < 0,
               [KA[e].k("x"), KA[e].k(kt // 4), QA[e].k("x"), QA[e].k(qc)], [pk_])
            if d_ >= 0:
                mm(p_[:, 0:128], ident_bf.ap(), cmask_bf.ap(), False, True, [ident_bf.k(), cmask_bf.k()], [pk_])
            qk[si] = (p_, pk_, coff, N)
        issue_qk(0); issue_qk(1); issue_qk(2)
        osbs = {}
        po_dd = {}
        for si, (qc, e, kt) in enumerate(steps):
            q0 = qc * 512
            nkt = 4 * qc + 4
            pacc, pacck = psum[6 + e], ("ps", 6 + e)
            if (qc, 0) not in osbs and e == 0 and kt == 0:
                osbs[qc] = tmp(512, "osb")
            osb = osbs[qc]
            p_, pk_, coff, N = qk.pop(si)
            PT = A(512 * 2, "PT")
            act(PT.ap()[:, 0:N], p_[:, 0:N], AF.Exp, [pk_], [PT.k()])
            if si + 3 < len(steps):
                issue_qk(si + 3)
            mm(pacc[:, coff:512], Vv[:, kt, e, :], PT.ap()[:, 0:N], kt == 0, kt == nkt - 1, [Vaug.k(), PT.k()], [pacck])
            AR.release(PT)
            kt_count[0] += 1
            if kt_count[0] % TICK == 0:
                next(s5g, None)
            if kt == nkt - 1:
                lo, hi = (0, 64) if e == 0 else (64, 128)
                dl, dh = (64, 128) if e == 0 else (0, 64)
                if e == 0:
                    po_dd[qc] = (tmp(512, "po"), tmp(512, "dd"))
                po, dd = po_dd[qc]
                act(po.ap(F32)[lo:hi, :], pacc[lo:hi, :], AF.Identity, [pacck], [po.k(e)])
                cp("dve" if DDV else "act", dd.ap(F32)[lo:hi, :], pacc[dl:dh, :], [pacck], [dd.k(e)])
                if e == 1:
                    S.add("dve", lambda h, dd=dd: h.reciprocal(out=dd.ap(F32), in_=dd.ap(F32)),
                          reads=[dd.k(0), dd.k(1)], writes=[dd.k(0), dd.k(1)], cost=3.4)
                    tt("dve", osb.ap(F32), po.ap(F32), dd.ap(F32), ALU.mult, [po.k(0), po.k(1), dd.k(0), dd.k(1)], [osb.k(0), osb.k(1)])
                    AR.release(po); AR.release(dd)
                if e == 1:
                    pz, pzk = ps_next()
                    proj_fm(w_za_v, w_za_k, 0, 128, qc, pz, pzk)
                    sz = tmp(512, "sza")
                    act(sz.ap(F32), pz[:, :], AF.Tanh, [pzk], [sz.k()], scale=0.5)
                    stt("dve", sz.ap(F32), sz.ap(F32), 1.0, pz[:, :], ALU.add, ALU.mult, [sz.k(), pzk], [sz.k()])
                    stt("dve", oz[hc].ap()[:, q0:q0 + 512], sz.ap(F32), 0.5, osb.ap(F32), ALU.mult, ALU.mult,
                        [osb.k(0), osb.k(1), sz.k()], [oz[hc].k(qc)])
                    AR.release(osb); AR.release(sz)
        for b in QA + KA + [w_q, w_k, w_za, w_v, Vaug]:
            AR.release(b)
    for _ in s5g:
        pass
    ps_mode[0] = "all8"
    for hc in range(4):
        dump("oz%d" % hc, oz[hc].ap(), [128, S_LEN], oz[hc].k(3))
    AR.release(w_xs)
    for cc in range(4):
        dump("yT%d" % cc, yT[cc].ap(), [128, S_LEN], yT[cc].k())

    s5o = [A(S_LEN * 2, "s5o%d" % i) for i in range(4)]
    for fc in range(4):
        for tcb in range(4):
            pa, pak = ps_next(); pb, pbk = ps_next(); pz, pzk = ps_next()
            proj_fm(w_g_v, w_g_k, fc * 128, 128, tcb, pa, pak, kcn=4, src=yT, srckeys=lambda kc, t: yT[kc].k())
            proj_fm(w_g_v, w_g_k, 512 + fc * 128, 128, tcb, pb, pbk, kcn=4, src=yT, srckeys=lambda kc, t: yT[kc].k())
            proj_fm(w_z_v, w_z_k, fc * 128, 128, tcb, pz, pzk)
            sb_ = tmp(512, "sgb"); sz = tmp(512, "sz"); t_ = tmp(512, "glt")
            act(sb_.ap(F32), pb[:, :], AF.Sigmoid, [pbk], [sb_.k()])
            act(sz.ap(F32), pz[:, :], AF.Silu, [pzk], [sz.k()])
            tt("dve", t_.ap(F32), pa[:, :], sb_.ap(F32), ALU.mult, [pak, sb_.k()], [t_.k()])
            tt("pool", s5o[fc].ap()[:, tcb * 512:(tcb + 1) * 512], t_.ap(F32), sz.ap(F32), ALU.mult, [t_.k(), sz.k()], [s5o[fc].k(tcb)])
            for b in (sb_, sz, t_):
                AR.release(b)
    AR.release(w_g_); AR.release(w_z)
    for b in yT:
        AR.release(b)
    for fc in range(4):
        dump("s5o%d" % fc, s5o[fc].ap(), [128, S_LEN], s5o[fc].k(3))


    gate_row = A(1024 * 4, "gate_row")
    brow = A(1024 * 4, "brow")
    dma(brow.ap(F32)[0:1, :], b_ada_row[0:1, 2048:3072], [], [brow.k()])
    for hf in range(2):
        sg = A(8 * 512 * 4, "wada_stg")
        sv = sg.ap(F32).rearrange("p (k n) -> p k n", n=512)
        dma(sg.ap(F32).rearrange("p (j c) -> p j c", j=4),
            w_ada[2048 + hf * 512:2048 + (hf + 1) * 512, :].rearrange("(j p) c -> p j c", p=128), [], [sg.k()])
        s16 = A(8 * 512 * 2, "wada_bf")
        s16v = s16.ap().rearrange("p (k n) -> p k n", n=512)
        cp("dve" if hf % 2 == 0 else "act", s16v.rearrange("p k (j n) -> p k j n", j=4),
           sg.ap(F32).rearrange("p (j k n) -> p k j n", j=4, k=8), [sg.k()], [s16.k()])
        AR.release(sg)
        pg, pgk = ps_next()
        for kc in range(8):
            mm(pg[0:1, :], cact_bf.ap()[:, kc:kc + 1], s16v[:, kc, :], kc == 0, kc == 7, [s16.k(), cact_bf.k()], [pgk])
        AR.release(s16)
        tt("dve", gate_row.ap(F32)[0:1, hf * 512:(hf + 1) * 512], pg[0:1, :], brow.ap(F32)[0:1, hf * 512:(hf + 1) * 512], ALU.add,
           [pgk, brow.k()], [gate_row.k(hf)])
    AR.release(brow)
    ones_row = A(128 * 4, "ones_row")
    memset("pool", ones_row.ap(F32)[0:1, :], 1.0, [ones_row.k()])
    gate_b = A(1024 * 4, "gate_b")
    for hf in range(2):
        p_, pk_ = ps_next()
        mm(p_[:, :], ones_row.ap(F32)[0:1, :], gate_row.ap(F32)[0:1, hf * 512:(hf + 1) * 512], True, True,
           [ones_row.k(), gate_row.k(hf)], [pk_])
        cp("dve", gate_b.ap(F32)[:, hf * 512:(hf + 1) * 512], p_[:, :], [pk_], [gate_b.k()])
    AR.release(gate_row); AR.release(ones_row)
    lng = A(1024 * 4, "lng"); lnb = A(1024 * 4, "lnb")
    dma(lng.ap(F32), ln_g[0:1, :].broadcast_to([128, D]), [], [lng.k()])
    dma(lnb.ap(F32), ln_b[0:1, :].broadcast_to([128, D]), [], [lnb.k()])
    wo = A(8 * 1024 * 2, "w_out")
    wov = wo.ap().rearrange("p (k n) -> p k n", n=1024)
    wok = []
    for jj in range(0, 8, 4):
        sg = A(4 * 8 * 128 * 4, "w_out_stg")
        dma(sg.ap(F32).rearrange("p (j c) -> p j c", j=4),
            w_out[jj * 128:(jj + 4) * 128, :].rearrange("(j p) c -> p j c", p=128), [], [sg.k()])
        for j in range(4):
            c0 = (jj + j) * 128
            tt("dve", wov[:, :, c0:c0 + 128],
               sg.ap(F32).rearrange("p (j k n) -> p j k n", j=4, k=8)[:, j, :, :],
               gate_b.ap(F32)[:, c0:c0 + 128].unsqueeze(1).broadcast_to([128, 8, 128]), ALU.mult,
               [sg.k(), gate_b.k()], [wo.k((jj, j))])
            wok.append(wo.k((jj, j)))
        AR.release(sg)
    mg = [A(S_LEN * 2, "mg%d" % i) for i in range(8)]
    def load_fc(fc):
        return (load_w(w_ps, fc, 1, 4, "w_ps", eng="dve"), load_w(w_pa, fc, 1, 4, "w_pa", eng="dve"),
                load_w(w_in, 24 + fc, 1, 8, "w_gs", eng="dve"), load_w(w_in, 32 + fc, 1, 8, "w_ga", eng="dve"))
    fc_w = load_fc(0)
    for fc in range(8):
        (w1, w1v, w1k), (w2, w2v, w2k), (wgs, wgsv, wgsk), (wga, wgav, wgak) = fc_w
        if fc + 1 < 8 and PREFETCH_FC:
            fc_w = load_fc(fc + 1)
        for tcb in range(4):
            p1, p1k = ps_next(); p2, p2k = ps_next(); p3, p3k = ps_next(); p4, p4k = ps_next()
            proj_fm(w1v, w1k, 0, 128, tcb, p1, p1k, kcn=4, src=s5o, srckeys=lambda kc, t: s5o[kc].k(t))
            proj_fm(w2v, w2k, 0, 128, tcb, p2, p2k, kcn=4, src=oz, srckeys=lambda kc, t: oz[kc].k(t))
            proj_fm(wgsv, wgsk, 0, 128, tcb, p3, p3k)
            proj_fm(wgav, wgak, 0, 128, tcb, p4, p4k)
            s1 = tmp(512, "sg1"); s2 = tmp(512, "sg2")
            act(s1.ap(F32), p3[:, :], AF.Sigmoid, [p3k], [s1.k()])
            act(s2.ap(F32), p4[:, :], AF.Sigmoid, [p4k], [s2.k()])
            tt("dve", s1.ap(F32), p1[:, :], s1.ap(F32), ALU.mult, [p1k, s1.k()], [s1.k()])
            tt("dve", s2.ap(F32), p2[:, :], s2.ap(F32), ALU.mult, [p2k, s2.k()], [s2.k()])
            tt("pool" if tcb % 2 else "dve", mg[fc].ap()[:, tcb * 512:(tcb + 1) * 512], s1.ap(F32), s2.ap(F32), ALU.add,
               [s1.k(), s2.k()], [mg[fc].k(tcb)])
            for b in (s1, s2):
                AR.release(b)
        for b in (w1, w2, wgs, wga):
            AR.release(b)
        if fc + 1 < 8 and not PREFETCH_FC:
            fc_w = load_fc(fc + 1)
    for b in s5o + oz + uT:
        AR.release(b)
    for tcb in range(4):
        xts = []; pres = []
        for t4 in range(4):
            tti = tcb * 4 + t4
            xt = A(1024 * 4, "xt2")
            dma(xt.ap(F32), x[tti * 128:(tti + 1) * 128, :], [], [xt.k()])
            pre = A(1024 * 4, "pre")
            for hf in range(2):
                p_, pk_ = ps_next()
                for kc in range(8):
                    mm(p_[:, :], mg[kc].ap()[:, tti * 128:(tti + 1) * 128], wov[:, kc, hf * 512:(hf + 1) * 512], kc == 0, kc == 7,
                       list(wok) + [mg[kc].k(tcb)], [pk_])
                stt("dve", pre.ap(F32)[:, hf * 512:(hf + 1) * 512], xt.ap(F32)[:, hf * 512:(hf + 1) * 512], ALPHA, p_[:, :],
                    ALU.mult, ALU.add, [xt.k(), pk_], [pre.k(hf)])
            xts.append(xt); pres.append(pre)
        stats = layer_norm_stats_multi([(pre.ap(F32), [pre.k(0), pre.k(1)]) for pre in pres], "b")
        for t4 in range(4):
            xt = xts[t4]; pre = pres[t4]; rstd, nmr = stats[t4]
            act(xt.ap(F32), pre.ap(F32), AF.Identity, [pre.k(0), pre.k(1), rstd.k(), nmr.k()], [xt.k()],
                bias=nmr.ap(F32)[:, 0:1], scale=rstd.ap(F32)[:, 0:1])
        for t4 in range(4):
            xt = xts[t4]
            tt("dve" if (tcb == 3 or t4 % 2) else "pool", xt.ap(F32), xt.ap(F32), lng.ap(F32), ALU.mult, [xt.k(), lng.k()], [xt.k()])
        for t4 in range(4):
            tti = tcb * 4 + t4
            xt = xts[t4]; pre = pres[t4]; rstd, nmr = stats[t4]
            tt("dve", pre.ap(F32), xt.ap(F32), lnb.ap(F32), ALU.add, [xt.k(), lnb.k()], [pre.k(0), pre.k(1)])
            dma(out[tti * 128:(tti + 1) * 128, :], pre.ap(F32), [pre.k(0), pre.k(1)], [])
            for b in (xt, pre, rstd, nmr):
                AR.release(b)

    S.emit(nc)
    st.close()
    return nc, dbg_out, AR.peak


def _host_consts():
    ident = np.eye(128, dtype=np.float32)
    sel = np.zeros((128, 16, 128), np.float32)
    for a in range(4):
        for b in range(4):
            for r in range(32):
                sel[a * 32 + r, a * 4 + b, b * 32 + r] = 1.0
    kk = np.arange(128)
    cmask = np.where(kk[None, :] >= kk[:, None], 0.0, -30000.0).astype(np.float32)
    tmask = np.zeros((128, 256), np.float32)
    for i4 in range(4):
        for j in range(8):
            if j >= i4:
                tmask[i4 * 32:(i4 + 1) * 32, j * 32:(j + 1) * 32] = 1.0
    kv = np.array([0, -1, -2, -3, -4, -5, -6, -7] + list(range(9)) + [7, 6, 5, 4, 3, 2, 1, 0], np.float32)
    kvals = np.tile(kv[None, :], (128, 1))
    cvals = np.tile(np.arange(256, dtype=np.float32)[None, :], (128, 1))
    return dict(ident=ident, sel=sel.reshape(128, 16 * 128), cmask=cmask, tmask=tmask, kvals=kvals, cvals=cvals)


def _pair_layout_vec(v):
    return np.ascontiguousarray(v.reshape(16, 2, 64).transpose(1, 2, 0).reshape(128, 16))


def _blk(m):
    o = np.zeros((2, 64, 16, 2, 16), np.float32)
    mm_ = m.reshape(16, 2, 64, 16)
    for gg in range(2):
        o[gg, :, :, gg, :] = mm_[:, gg].transpose(1, 0, 2)
    return np.ascontiguousarray(o.reshape(128, 16 * 32))


def _make_in_maps(inp):
    f = lambda a: np.ascontiguousarray(np.asarray(a, dtype=np.float32))
    consts = _host_consts()
    shared = dict(consts)
    def blockify(W):
        K, N = W.shape
        return np.ascontiguousarray(W.reshape(K // 128, 128, N // 128, 128).transpose(2, 1, 0, 3).reshape(N, K))
    w_in_full = f(inp["w_in"][0])
    segs = [(0, 512), (512, 1024), (1024, 1536), (1536, 2048), (2048, 2560), (2568, 3080), (3080, 4104), (4104, 5128)]
    shared["w_in"] = np.concatenate([blockify(w_in_full[:, a:b]) for a, b in segs], axis=0)
    shared["w_f"] = np.ascontiguousarray(w_in_full[:, 2560:2568].reshape(8, 128, 8).transpose(1, 0, 2).reshape(128, 64))
    shared["w_ada"] = blockify(f(inp["w_ada"][0]))
    shared["b_adaT"] = f(inp["b_ada"][0].reshape(24, 128).T)
    shared["b_ada_row"] = f(inp["b_ada"][0].reshape(1, 3 * D))
    shared["b_f"] = f(inp["b_f"][0].reshape(8, 1))
    shared["lam_re"] = _pair_layout_vec(f(inp["lam_re"][0]))
    shared["lam_im"] = _pair_layout_vec(f(inp["lam_im"][0]))
    shared["ldt"] = _pair_layout_vec(np.repeat(f(inp["log_dt"][0])[:, None], 64, axis=1))
    shared["Bre"] = _blk(f(inp["ssm_b_re"][0])); shared["Bim"] = _blk(f(inp["ssm_b_im"][0]))
    shared["Cre"] = _blk(f(inp["ssm_c_re"][0]).transpose(0, 2, 1)); shared["Cim"] = _blk(f(inp["ssm_c_im"][0]).transpose(0, 2, 1))
    dvec = f(inp["ssm_d"][0]).reshape(16, 32)
    shared["dpair"] = np.ascontiguousarray(np.tile(dvec.T, (4, 1)))
    shared["w_glu"] = blockify(f(inp["w_glu"][0])); shared["w_ps"] = blockify(f(inp["w_proj_ssm"][0]))
    shared["w_pa"] = blockify(f(inp["w_proj_attn"][0]))
    shared["w_out"] = blockify(f(inp["w_out"][0])); shared["ln_g"] = f(inp["ln_g"][0].reshape(1, D)); shared["ln_b"] = f(inp["ln_b"][0].reshape(1, D))
    maps = []
    xs = f(inp["x"]); cs = f(inp["c"])
    for b in range(8):
        m = dict(shared)
        m["x"] = xs[b]
        m["cT"] = np.ascontiguousarray(cs[b].reshape(8, 128).T)
        maps.append(m)
    return maps


_CACHE = {}


def kernel(**inputs):
    if "nc" not in _CACHE:
        _CACHE["nc"] = build()[0]
    nc = _CACHE["nc"]
    maps = _make_in_maps(inputs)
    res = run_bass_kernel_spmd(nc, maps, core_ids=list(range(8)))
    return np.stack([np.asarray(r["out"], dtype=np.float32) for r in res.results], axis=0)
```

```python
import contextlib
import math
import numpy as np
import concourse.bass as bass
import concourse.mybir as mybir
from concourse.bass_utils import run_bass_kernel_spmd

F32 = mybir.dt.float32
BF16 = mybir.dt.bfloat16
I32 = mybir.dt.int32
ALU = mybir.AluOpType
AF = mybir.ActivationFunctionType

COMPUTE = ("pe", "act", "dve", "pool")
DMAQ = ("sp",)
ENGS = COMPUTE + DMAQ
N_DMA_SEMS = 32
import os
PREFETCH_FC = os.environ.get("PF", "1") == "1"
TICK = int(os.environ.get("TICK", "4"))
UTA = int(os.environ.get("UTA", "2"))
DDV = os.environ.get("DDV", "0") == "1"
PRETICK = int(os.environ.get("PRETICK", "0"))

S_LEN = 2048
D = 1024
NKV = 25
ALPHA = 2.0 ** 0.25
LN_EPS = 1e-5
TWO_PI = 2.0 * math.pi
CW_C1 = 6.28125
CW_C2 = TWO_PI - CW_C1


class Op:
    __slots__ = ("eng", "fn", "preds", "idx", "gid", "signal", "is_dma", "dsem", "dval", "dprev",
                 "cost", "tag", "nbytes", "prio", "nsucc", "succs", "npend", "ready", "fin", "pos")

    def __init__(self, eng, fn, idx, gid, is_dma, cost, tag, nbytes):
        self.eng = eng; self.fn = fn; self.idx = idx; self.gid = gid
        self.preds = set()
        self.signal = False; self.is_dma = is_dma
        self.dsem = None; self.dval = None; self.dprev = None
        self.cost = cost; self.tag = tag; self.nbytes = nbytes
        self.prio = 0.0; self.succs = []; self.npend = 0; self.ready = 0.0; self.fin = 0.0; self.pos = -1


ACT_SETS = {"Exp": 0, "Tanh": 0, "Identity": -1, "Copy": -1, "Gelu": 1, "Sigmoid": 2, "Silu": 3, "Sin": 4, "Sqrt": 5, "Ln": 6}
XLAT = float(os.environ.get("XLAT", "0.5"))
DMA_BW = float(os.environ.get("DMABW", "400e3"))
DMA_LAT = 2.0


class Sched:
    def __init__(self):
        self.ops = {e: [] for e in ENGS}
        self.all = []
        self.last_write = {}
        self.reads = {}
        self.dma_counts = [0] * N_DMA_SEMS
        self.reorder = True

    def add(self, eng, fn, reads=(), writes=(), cost=0.1, tag=None, nbytes=0):
        op = Op(eng, fn, len(self.ops[eng]), len(self.all), eng in DMAQ, cost, tag, nbytes)
        for b in reads:
            if b in self.last_write:
                op.preds.add(self.last_write[b])
        for b in writes:
            if b in self.last_write:
                op.preds.add(self.last_write[b])
            for r in self.reads.get(b, ()):
                op.preds.add(r)
        op.preds.discard(op)
        self.ops[eng].append(op)
        self.all.append(op)
        for b in reads:
            self.reads.setdefault(b, []).append(op)
        for b in writes:
            self.last_write[b] = op
            self.reads[b] = []
        return op

    def schedule(self):
        ops = self.all
        for op in ops:
            op.succs = []
        for op in ops:
            for p in op.preds:
                p.succs.append(op)
        for op in reversed(ops):
            m = 0.0
            for s_ in op.succs:
                if s_.prio > m:
                    m = s_.prio
            op.prio = op.cost + m + (DMA_LAT if op.is_dma else 0.0)
        if not self.reorder:
            return {e: list(self.ops[e]) for e in ENGS}
        for op in ops:
            op.npend = len(op.preds); op.ready = 0.0
        cand = {e: [] for e in ENGS}
        for op in ops:
            if op.npend == 0:
                cand[op.eng].append(op)
        free = {e: 0.0 for e in ENGS}
        last_set = [None]
        dma_free = [0.0]
        final = {e: [] for e in ENGS}
        n_left = len(ops)
        while n_left:
            best = None; bkey = None
            for e in ENGS:
                cl = cand[e]
                if not cl:
                    continue
                fe = free[e]
                for op in cl:
                    st_ = op.ready if op.ready > fe else fe
                    pen = 0.0
                    if e == "act" and op.tag is not None:
                        ts_ = ACT_SETS.get(op.tag, 9)
                        if ts_ >= 0 and last_set[0] is not None and ts_ != last_set[0]:
                            pen = 1.3
                    key = (st_ + pen, -op.prio, op.gid)
                    if bkey is None or key < bkey:
                        bkey = key; best = op
            op = best
            e = op.eng
            st_ = max(op.ready, free[e])
            if e == "act" and op.tag is not None:
                ts_ = ACT_SETS.get(op.tag, 9)
                if ts_ >= 0:
                    if last_set[0] is not None and ts_ != last_set[0]:
                        st_ += 1.3
                    last_set[0] = ts_
            if op.is_dma:
                free[e] = st_ + 0.06
                t0 = max(st_ + DMA_LAT, dma_free[0])
                op.fin = t0 + op.nbytes / DMA_BW
                dma_free[0] = op.fin
            else:
                op.fin = st_ + op.cost
                free[e] = op.fin
            cand[e].remove(op)
            op.pos = len(final[e]); final[e].append(op)
            n_left -= 1
            for s_ in op.succs:
                lat = op.fin + (0.0 if (s_.eng == e and not op.is_dma) else XLAT)
                if lat > s_.ready:
                    s_.ready = lat
                s_.npend -= 1
                if s_.npend == 0:
                    cand[s_.eng].append(s_)
        self.makespan = max(op.fin for op in ops)
        return final

    def emit(self, nc):
        final = self.schedule()
        for e in ENGS:
            for i, op in enumerate(final[e]):
                op.pos = i
        rr = 0
        for op in final["sp"]:
            s_ = rr; rr = (rr + 1) % N_DMA_SEMS
            op.dsem = s_; op.dprev = self.dma_counts[s_]
            self.dma_counts[s_] += 16; op.dval = self.dma_counts[s_]
        for op in self.all:
            for p in op.preds:
                if p.eng == "pe" and op.eng == "pe":
                    continue
                p.signal = True
        sigcount = {}
        for e in COMPUTE:
            c = 0; arr = []
            for op in final[e]:
                if op.signal:
                    c += 1
                arr.append(c)
            sigcount[e] = arr
        with contextlib.ExitStack() as st:
            sems = {e: st.enter_context(nc.semaphore("s_" + e)) for e in COMPUTE}
            dsems = [st.enter_context(nc.semaphore("d%d" % i)) for i in range(N_DMA_SEMS)]
            block = st.enter_context(nc.Block())
            sched = self

            def run(e, h):
                waited = {}
                for op in final[e]:
                    need = {}
                    for p in op.preds:
                        if p.is_dma:
                            wk = ("d", p.dsem)
                            if need.get(wk, -1) < p.dval:
                                need[wk] = p.dval
                        else:
                            if p.eng == "pe" and e == "pe":
                                continue
                            v = sigcount[p.eng][p.pos]
                            if need.get(p.eng, -1) < v:
                                need[p.eng] = v
                    if op.is_dma and op.dprev > 0:
                        wk = ("d", op.dsem)
                        if need.get(wk, -1) < op.dprev:
                            need[wk] = op.dprev
                    for wk, v in need.items():
                        if waited.get(wk, -1) < v:
                            if isinstance(wk, tuple):
                                h.wait_ge(dsems[wk[1]], v)
                            else:
                                h.wait_ge(sems[wk], v)
                            waited[wk] = v
                    ins = op.fn(h)
                    if op.is_dma:
                        ins.then_inc(dsems[op.dsem], 16)
                    elif op.signal:
                        ins.then_inc(sems[e], 1)
                if e == "sp":
                    for s_ in range(N_DMA_SEMS):
                        if sched.dma_counts[s_] > 0 and waited.get(("d", s_), -1) < sched.dma_counts[s_]:
                            h.wait_ge(dsems[s_], sched.dma_counts[s_])

            @block.tensor
            def _(h):
                run("pe", h)

            @block.scalar
            def _(h):
                run("act", h)

            @block.vector
            def _(h):
                run("dve", h)

            @block.gpsimd
            def _(h):
                run("pool", h)

            @block.sync
            def _(h):
                run("sp", h)


class Buf:
    _uid = 0

    def __init__(self, arena, off, nbytes, prior, name):
        Buf._uid += 1
        self.uid = Buf._uid
        self.arena = arena; self.off = off; self.nbytes = nbytes; self.asize = (nbytes + 63) // 64 * 64
        self.prior = prior; self.keys = set(); self.name = name

    def k(self, sub=None):
        key = (self.uid, sub)
        if key not in self.keys:
            self.keys.add(key)
            self.arena.S.reads[key] = list(self.prior)
        return key

    def ap(self, dt=BF16):
        a = self.arena.t[:, self.off // 2:(self.off + self.nbytes) // 2]
        if dt != BF16:
            a = a.bitcast(dt)
        return a


class Arena:
    def __init__(self, S, tensor, nbytes):
        self.S = S; self.t = tensor
        self.free = [[0, nbytes, [], 0]]
        self.peak = 0; self.used = 0; self.clock = 0

    LONG = {"uT", "yT", "s5o", "oz", "mg", "Vaug", "QA", "KA", "w_xs", "w_out", "Tt", "Wt", "Vt", "Esin", "Ecos",
            "Xs", "Ybf", "xsT", "ident", "ident_bf", "sel_bf", "cmask_bf", "tmask", "kvals", "cvals", "lre", "lim",
            "ldt", "dpair", "badaT", "cT", "Are", "Aim", "ph", "r8", "cr", "ci", "gate_b", "lng", "lnb", "modT", "sc1",
            "eps", "cact", "cact_bf", "w_q", "w_k", "w_za", "w_v", "w_glu", "w_zs", "w_ps", "w_pa", "w_gs", "w_ga"}

    def alloc(self, nbytes, name=""):
        req = nbytes
        nbytes = (nbytes + 63) // 64 * 64
        longlived = name.rstrip("0123456789") in self.LONG
        fr = self.free
        best = None; bkey = None
        for i in range(len(fr)):
            tot = 0; mx = 0; j = i
            while True:
                tot += fr[j][1]; mx = max(mx, fr[j][3])
                if tot >= nbytes:
                    if longlived:
                        key = (-(fr[j][0] + fr[j][1]), 0)
                    else:
                        key = (mx, fr[i][0])
                    if bkey is None or key < bkey:
                        bkey = key; best = (i, j)
                    break
                if j + 1 < len(fr) and fr[j][0] + fr[j][1] == fr[j + 1][0]:
                    j += 1
                else:
                    break
        if best is None:
            raise RuntimeError("arena OOM for %s (%d bytes), used %d" % (name, nbytes, self.used))
        i, j = best
        acc = set()
        need = nbytes
        if longlived:
            end = fr[j][0] + fr[j][1]
            off = end - nbytes
            newfree = fr[:i]
            mid = []
            for k in range(j, i - 1, -1):
                seg = fr[k]
                if need <= 0:
                    mid.append(seg); continue
                acc.update(seg[2])
                if seg[1] <= need:
                    need -= seg[1]
                else:
                    mid.append([seg[0], seg[1] - need, seg[2], seg[3]])
                    need = 0
            newfree += list(reversed(mid))
            newfree += fr[j + 1:]
        else:
            off = fr[i][0]
            newfree = fr[:i]
            for k in range(i, j + 1):
                seg = fr[k]
                if need <= 0:
                    newfree.append(seg); continue
                acc.update(seg[2])
                if seg[1] <= need:
                    need -= seg[1]
                else:
                    newfree.append([seg[0] + need, seg[1] - need, seg[2], seg[3]])
                    need = 0
            newfree += fr[j + 1:]
        self.free = newfree
        self.used += nbytes
        self.peak = max(self.peak, self.used)
        return Buf(self, off, req, list(acc), name)

    def release(self, b):
        acc = list(b.prior) if not b.keys else []
        for key in b.keys:
            if key in self.S.last_write:
                acc.append(self.S.last_write[key])
            acc += self.S.reads.get(key, [])
        acc = list(set(acc))
        self.used -= b.asize
        self.clock += 1
        self.free.append([b.off, b.asize, acc, self.clock])
        self.free.sort(key=lambda x: x[0])


def build(debug=()):
    nc = bass.Bass("TRN2", target_bir_lowering=False)
    S = Sched()

    def din(name, shape):
        return nc.dram_tensor(name, list(shape), F32, kind="ExternalInput").ap()

    x = din("x", [S_LEN, D]); cT = din("cT", [128, 8]); w_ada = din("w_ada", [24 * 128, 8 * 128])
    b_adaT = din("b_adaT", [128, 24]); b_ada_row = din("b_ada_row", [1, 3 * D])
    w_in = din("w_in", [40 * 128, 8 * 128]); w_fd = din("w_f", [128, 64]); b_f = din("b_f", [8, 1])
    lam_re = din("lam_re", [128, 16]); lam_im = din("lam_im", [128, 16]); ldt = din("ldt", [128, 16])
    Bre_d = din("Bre", [128, 16 * 32]); Bim_d = din("Bim", [128, 16 * 32])
    Cre_d = din("Cre", [128, 16 * 32]); Cim_d = din("Cim", [128, 16 * 32])
    dpair_d = din("dpair", [128, 16])
    w_glu = din("w_glu", [8 * 128, 4 * 128]); w_ps = din("w_ps", [8 * 128, 4 * 128]); w_pa = din("w_pa", [8 * 128, 4 * 128])
    w_out = din("w_out", [8 * 128, 8 * 128]); ln_g = din("ln_g", [1, D]); ln_b = din("ln_b", [1, D])
    ident_d = din("ident", [128, 128]); sel_d = din("sel", [128, 16 * 128]); cmask_d = din("cmask", [128, 128])
    tmask_d = din("tmask", [128, 256]); kvals_d = din("kvals", [128, NKV]); cvals_d = din("cvals", [128, 256])
    out = nc.dram_tensor("out", [S_LEN, D], F32, kind="ExternalOutput").ap()
    dbg_out = {}

    ARENA_BYTES = 198 * 1024
    st = contextlib.ExitStack()
    arena_t = st.enter_context(nc.sbuf_tensor("arena", [128, ARENA_BYTES // 2], BF16))
    psum = [st.enter_context(nc.psum_tensor("ps%d" % i, [128, 512], F32)) for i in range(8)]
    ti_ts = [st.enter_context(nc.sbuf_tensor("ra_int%d" % i, [128, 1024], I32)) for i in range(2)]
    ti_rr = [0]
    AR = Arena(S, arena_t, ARENA_BYTES)
    ps_rr = [0]

    ps_mode = ["all"]
    qk_rr = [0]

    def ps_next():
        if ps_mode[0] == "all8":
            i = ps_rr[0] % 8; ps_rr[0] = (i + 1) % 8
        elif ps_mode[0] == "all":
            i = ps_rr[0] % 6; ps_rr[0] = (i + 1) % 6
        else:
            i = 3 + ps_rr[0] % 3; ps_rr[0] = (ps_rr[0] + 1) % 3
        return psum[i], ("ps", i)

    def ps_qk():
        i = qk_rr[0]; qk_rr[0] = (i + 1) % 3
        return psum[i], ("ps", i)

    def A(nbytes, name=""):
        return AR.alloc(nbytes, name)

    def fcols(ap):
        n = 1
        for d_ in ap.shape[1:]:
            n *= int(d_)
        return n

    def esize(ap):
        return 2 if ap.dtype == BF16 else 4

    def vcost(eng, ap, mult=1.0):
        c = fcols(ap) * mult
        return (0.12 + c / 900.0) if eng == "dve" else (0.15 + c / 300.0)

    def dma(out_ap, in_ap, reads, writes):
        nb = int(out_ap.shape[0]) * fcols(out_ap) * esize(out_ap)
        S.add("sp", lambda h: h.dma_start(out=out_ap, in_=in_ap), reads=reads, writes=writes, cost=0.06, nbytes=nb)

    def mm(out_ap, lhsT, rhs, start, stop, reads, writes):
        n = max(fcols(rhs), 64)
        c = 0.086 if n <= 128 else (0.26 if n == 256 else n / 2100.0 + 0.012)
        if int(lhsT.shape[0]) == 70:
            c += 0.03
        if rhs.dtype == F32:
            c *= 4.0
        S.add("pe", lambda h: h.matmul(out_ap, lhsT, rhs, start=start, stop=stop), reads=reads, writes=writes, cost=c)

    def act(out_ap, in_ap, func, reads, writes, bias=0.0, scale=1.0):
        S.add("act", lambda h: h.activation(out=out_ap, in_=in_ap, func=func, bias=bias, scale=scale),
              reads=reads, writes=writes, cost=(0.08 + fcols(out_ap) / 1150.0) * (3.0 if func == AF.Gelu else 1.0), tag=func.name)

    def tt(eng, out_ap, a, b, op, reads, writes):
        S.add(eng, lambda h: h.tensor_tensor(out=out_ap, in0=a, in1=b, op=op), reads=reads, writes=writes,
              cost=vcost(eng, out_ap))

    def ts(eng, out_ap, a, s1, s2, op0, op1, reads, writes):
        if s2 is None:
            S.add(eng, lambda h: h.tensor_scalar(out=out_ap, in0=a, scalar1=s1, scalar2=None, op0=op0),
                  reads=reads, writes=writes, cost=vcost(eng, out_ap))
        else:
            S.add(eng, lambda h: h.tensor_scalar(out=out_ap, in0=a, scalar1=s1, scalar2=s2, op0=op0, op1=op1),
                  reads=reads, writes=writes, cost=vcost(eng, out_ap))

    def stt(eng, out_ap, a, s, b, op0, op1, reads, writes):
        S.add(eng, lambda h: h.scalar_tensor_tensor(out=out_ap, in0=a, scalar=s, in1=b, op0=op0, op1=op1),
              reads=reads, writes=writes, cost=vcost(eng, out_ap))

    def cp(eng, out_ap, in_ap, reads, writes):
        if eng == "act":
            S.add(eng, lambda h: h.activation(out=out_ap, in_=in_ap, func=AF.Identity), reads=reads, writes=writes,
                  cost=0.08 + fcols(out_ap) / 1150.0, tag="Identity")
        else:
            S.add(eng, lambda h: h.tensor_copy(out=out_ap, in_=in_ap), reads=reads, writes=writes,
                  cost=vcost(eng, out_ap, 0.6 if eng == "dve" else 1.5))

    def memset(eng, ap, val, writes):
        S.add(eng, lambda h: h.memset(ap, val), writes=writes, cost=vcost(eng, ap, 0.5))

    def dump(name, ap, shape, key):
        if name in debug:
            t = nc.dram_tensor("dbg_" + name, list(shape), ap.dtype, kind="ExternalOutput").ap()
            dbg_out[name] = t
            dma(t, ap, [key], [])

    def load_const(dram, cols, name, to_bf=False):
        b = A(cols * 4, name)
        dma(b.ap(F32), dram, [], [b.k()])
        if not to_bf:
            return b
        bb = A(cols * 2, name + "_bf")
        cp("pool", bb.ap(), b.ap(F32), [b.k()], [bb.k()])
        AR.release(b)
        return bb

    ident = load_const(ident_d, 128, "ident")
    ident_bf = A(256, "ident_bf")
    cp("pool", ident_bf.ap(), ident.ap(F32), [ident.k()], [ident_bf.k()])
    sel_bf = load_const(sel_d, 16 * 128, "sel", True)
    cmask_bf = load_const(cmask_d, 128, "cmask", True)
    tmask = load_const(tmask_d, 256, "tmask")
    kvals = load_const(kvals_d, NKV, "kvals")
    cvals = load_const(cvals_d, 256, "cvals")
    lre = load_const(lam_re, 16, "lre"); lim = load_const(lam_im, 16, "lim"); ldtb = load_const(ldt, 16, "ldt")
    dpair = load_const(dpair_d, 16, "dpair")
    badaT = load_const(b_adaT, 24, "badaT")
    cTb = load_const(cT, 8, "cT")
    sel3 = sel_bf.ap().rearrange("p (a d) -> p a d", d=128)

    def SEL(a, b):
        return sel3[:, a * 4 + b, :]

    cast_rr = [0]
    def load_w(blk2d, j0, nblk, kcn, name, eng=("dve", "act")):
        ncols = nblk * 128
        wb = A(kcn * ncols * 2, name)
        wv = wb.ap().rearrange("p (k n) -> p k n", n=ncols)
        step = max(1, (2048 // kcn) // 128)
        jj = 0; pi = 0
        while jj < nblk:
            nb = min(step, nblk - jj)
            sg = A(nb * kcn * 128 * 4, name + "_stg")
            dma(sg.ap(F32).rearrange("p (j c) -> p j c", j=nb),
                blk2d[(j0 + jj) * 128:(j0 + jj + nb) * 128, :].rearrange("(j p) c -> p j c", p=128), [], [sg.k()])
            e = eng if isinstance(eng, str) else eng[cast_rr[0] % len(eng)]
            cast_rr[0] += 1
            cp(e, wv[:, :, jj * 128:(jj + nb) * 128].rearrange("p k (j n) -> p k j n", j=nb),
               sg.ap(F32).rearrange("p (j k n) -> p k j n", j=nb, k=kcn), [sg.k()], [wb.k(pi)])
            AR.release(sg)
            jj += nb; pi += 1
        return wb, wv, [wb.k(i) for i in range(pi)]

    def tmp(cols, name):
        return A(cols * 4, name)

    dt_t = tmp(16, "dt")
    act(dt_t.ap(F32), ldtb.ap(F32), AF.Exp, [ldtb.k()], [dt_t.k()])
    lrdt = tmp(16, "lrdt"); lidt = tmp(16, "lidt")
    tt("dve", lrdt.ap(F32), lre.ap(F32), dt_t.ap(F32), ALU.mult, [lre.k(), dt_t.k()], [lrdt.k()])
    tt("dve", lidt.ap(F32), lim.ap(F32), dt_t.ap(F32), ALU.mult, [lim.k(), dt_t.k()], [lidt.k()])
    NT = 16 * NKV
    kv3 = kvals.ap(F32).unsqueeze(1).broadcast_to([128, 16, NKV])

    def v3(b, n=NKV):
        return b.ap(F32).rearrange("p (a k) -> p a k", k=n)

    def reduce_angle(dst, dstk, src, srck, cols, shift, view):
        y = tmp(cols, "ra_y"); t1 = tmp(cols, "ra_t1"); tf = tmp(cols, "ra_tf")
        ts("dve", y.ap(F32), src, shift, None, ALU.add, None, [srck], [y.k()])
        ts("dve", t1.ap(F32), y.ap(F32), 1.0 / TWO_PI, None, ALU.mult, None, [y.k()], [t1.k()])
        ti_i = ti_rr[0] % 2; ti_rr[0] += 1
        ti_t = ti_ts[ti_i]
        cp("dve", ti_t[:, 0:cols], t1.ap(F32), [t1.k()], ["ra_int%d" % ti_i])
        cp("dve", tf.ap(F32), ti_t[:, 0:cols], ["ra_int%d" % ti_i], [tf.k()])
        stt("dve", t1.ap(F32), tf.ap(F32), -CW_C1, y.ap(F32), ALU.mult, ALU.add, [tf.k(), y.k()], [t1.k()])
        stt("dve", y.ap(F32), tf.ap(F32), -CW_C2, t1.ap(F32), ALU.mult, ALU.add, [tf.k(), t1.k()], [y.k()])
        ts("dve", dst, y.ap(F32), -math.pi, math.pi, ALU.max, ALU.min, [y.k()], [dstk])
        for b in (y, t1, tf):
            AR.release(b)

    magk = tmp(NT, "magk"); angk = tmp(NT, "angk")
    tt("dve", v3(magk), lrdt.ap(F32).unsqueeze(2).broadcast_to([128, 16, NKV]), kv3, ALU.mult, [lrdt.k(), kvals.k()], [magk.k()])
    act(magk.ap(F32), magk.ap(F32), AF.Exp, [magk.k()], [magk.k()])
    tt("dve", v3(angk), lidt.ap(F32).unsqueeze(2).broadcast_to([128, 16, NKV]), kv3, ALU.mult, [lidt.k(), kvals.k()], [angk.k()])
    sred = tmp(NT, "sred"); cred = tmp(NT, "cred")
    reduce_angle(sred.ap(F32), sred.k(), angk.ap(F32), angk.k(), NT, 0.0, None)
    reduce_angle(cred.ap(F32), cred.k(), angk.ap(F32), angk.k(), NT, math.pi / 2, None)
    Are = tmp(NT, "Are"); Aim = tmp(NT, "Aim")
    act(sred.ap(F32), sred.ap(F32), AF.Sin, [sred.k()], [sred.k()])
    act(cred.ap(F32), cred.ap(F32), AF.Sin, [cred.k()], [cred.k()])
    tt("dve", Are.ap(F32), magk.ap(F32), cred.ap(F32), ALU.mult, [magk.k(), cred.k()], [Are.k()])
    tt("dve", Aim.ap(F32), magk.ap(F32), sred.ap(F32), ALU.mult, [magk.k(), sred.k()], [Aim.k()])
    ph = tmp(16, "ph"); r8 = tmp(16, "r8")
    ang8 = tmp(16, "ang8")
    cp("dve", ang8.ap(F32), v3(angk)[:, :, 16], [angk.k()], [ang8.k()])
    reduce_angle(ph.ap(F32), ph.k(), ang8.ap(F32), ang8.k(), 16, 0.0, None)
    cp("dve", r8.ap(F32), v3(magk)[:, :, 16], [magk.k()], [r8.k()])
    AR.release(ang8)
    am1 = tmp(16, "am1"); abi = tmp(16, "abi"); den = tmp(16, "den"); t_a = tmp(16, "t_a"); t_b = tmp(16, "t_b")
    cr = tmp(16, "cr"); ci = tmp(16, "ci")
    ts("dve", am1.ap(F32), v3(Are)[:, :, 9], -1.0, None, ALU.add, None, [Are.k()], [am1.k()])
    cp("dve", abi.ap(F32), v3(Aim)[:, :, 9], [Aim.k()], [abi.k()])
    tt("dve", den.ap(F32), lre.ap(F32), lre.ap(F32), ALU.mult, [lre.k()], [den.k()])
    tt("dve", t_a.ap(F32), lim.ap(F32), lim.ap(F32), ALU.mult, [lim.k()], [t_a.k()])
    tt("dve", den.ap(F32), den.ap(F32), t_a.ap(F32), ALU.add, [den.k(), t_a.k()], [den.k()])
    S.add("dve", lambda h: h.reciprocal(out=den.ap(F32), in_=den.ap(F32)), reads=[den.k()], writes=[den.k()], cost=0.2)
    tt("dve", t_a.ap(F32), am1.ap(F32), lre.ap(F32), ALU.mult, [am1.k(), lre.k()], [t_a.k()])
    tt("dve", t_b.ap(F32), abi.ap(F32), lim.ap(F32), ALU.mult, [abi.k(), lim.k()], [t_b.k()])
    tt("dve", t_a.ap(F32), t_a.ap(F32), t_b.ap(F32), ALU.add, [t_a.k(), t_b.k()], [t_a.k()])
    tt("dve", cr.ap(F32), t_a.ap(F32), den.ap(F32), ALU.mult, [t_a.k(), den.k()], [cr.k()])
    tt("dve", t_a.ap(F32), abi.ap(F32), lre.ap(F32), ALU.mult, [abi.k(), lre.k()], [t_a.k()])
    tt("dve", t_b.ap(F32), am1.ap(F32), lim.ap(F32), ALU.mult, [am1.k(), lim.k()], [t_b.k()])
    tt("dve", t_a.ap(F32), t_a.ap(F32), t_b.ap(F32), ALU.subtract, [t_a.k(), t_b.k()], [t_a.k()])
    tt("dve", ci.ap(F32), t_a.ap(F32), den.ap(F32), ALU.mult, [t_a.k(), den.k()], [ci.k()])
    for b in (am1, abi, den, t_a, t_b, magk, angk, sred, cred, dt_t, lrdt, lidt):
        AR.release(b)

    def cmul(eng, o_re, o_im, ore_k, oim_k, a_re, a_im, b_re, b_im, rk, cols, neg_im=False, shape=None):
        t1 = tmp(cols, "cm1"); t2 = tmp(cols, "cm2")
        v = (lambda b_: b_.ap(F32)) if shape is None else (lambda b_: b_.ap(F32).rearrange(shape[0], **shape[1]))
        tt(eng, v(t1), a_re, b_re, ALU.mult, rk, [t1.k()])
        tt(eng, v(t2), a_im, b_im, ALU.mult, rk, [t2.k()])
        tt(eng, o_re, v(t1), v(t2), ALU.subtract, [t1.k(), t2.k()], [ore_k])
        tt(eng, v(t1), a_re, b_im, ALU.mult, rk, [t1.k()])
        tt(eng, v(t2), a_im, b_re, ALU.mult, rk, [t2.k()])
        if neg_im:
            stt(eng, o_im, v(t1), -1.0, v(t2), ALU.mult, ALU.subtract, [t1.k(), t2.k()], [oim_k])
        else:
            tt(eng, o_im, v(t1), v(t2), ALU.add, [t1.k(), t2.k()], [oim_k])
        AR.release(t1); AR.release(t2)

    Are3 = v3(Are); Aim3 = v3(Aim)
    def load_hc(hc):
        return (load_w(w_in, 8 + hc, 1, 8, "w_q"),
                load_w(w_in, 12 + hc, 1, 8, "w_k"),
                load_w(w_in, 20 + hc, 1, 8, "w_za"),
                load_w(w_in, 16 + hc, 1, 8, "w_v"))
    hc_w = load_hc(0)
    cact = A(8 * 4, "cact")
    act(cact.ap(F32), cTb.ap(F32), AF.Silu, [cTb.k()], [cact.k()])
    cact_bf = A(8 * 2, "cact_bf")
    cp("dve", cact_bf.ap(), cact.ap(F32), [cact.k()], [cact_bf.k()])
    mrow = A(2048 * 4, "mrow")
    browA = A(2048 * 4, "browA")
    dma(browA.ap(F32)[0:1, :], b_ada_row[0:1, 0:2048], [], [browA.k()])
    for blk in range(4):
        sg = A(8 * 512 * 4, "wada_stg")
        sv = sg.ap(F32).rearrange("p (k n) -> p k n", n=512)
        dma(sg.ap(F32).rearrange("p (j c) -> p j c", j=4),
            w_ada[blk * 512:(blk + 1) * 512, :].rearrange("(j p) c -> p j c", p=128), [], [sg.k()])
        s16 = A(8 * 512 * 2, "wada_bf")
        s16v = s16.ap().rearrange("p (k n) -> p k n", n=512)
        cp("dve" if blk % 2 == 0 else "act", s16v.rearrange("p k (j n) -> p k j n", j=4),
           sg.ap(F32).rearrange("p (j k n) -> p k j n", j=4, k=8), [sg.k()], [s16.k()])
        AR.release(sg)
        pg, pgk = ps_next()
        for kc in range(8):
            mm(pg[0:1, :], cact_bf.ap()[:, kc:kc + 1], s16v[:, kc, :], kc == 0, kc == 7, [s16.k(), cact_bf.k()], [pgk])
        AR.release(s16)
        tt("dve", mrow.ap(F32)[0:1, blk * 512:(blk + 1) * 512], pg[0:1, :], browA.ap(F32)[0:1, blk * 512:(blk + 1) * 512], ALU.add,
           [pgk, browA.k()], [mrow.k(blk)])
    one1 = A(4, "one1")
    memset("pool", one1.ap(F32)[0:1, :], 1.0, [one1.k()])
    modT = A(16 * 4, "modT")
    pm, pmk = ps_next()
    for j in range(16):
        mm(pm[:, j:j + 1], mrow.ap(F32)[0:1, j * 128:(j + 1) * 128], one1.ap(F32)[0:1, 0:1], True, True,
           [mrow.k(j // 4), one1.k()], [pmk])
    cp("dve", modT.ap(F32), pm[:, 0:16], [pmk], [modT.k()])
    sc1 = A(8 * 4, "sc1")
    ts("dve", sc1.ap(F32), modT.ap(F32)[:, 8:16], 1.0, None, ALU.add, None, [modT.k()], [sc1.k()])
    AR.release(mrow); AR.release(browA); AR.release(one1)
    uT = [A(S_LEN * 2, "uT%d" % i) for i in range(8)]
    eps_t = A(4, "eps")
    memset("pool", eps_t.ap(F32), LN_EPS, [eps_t.k()])

    def layer_norm_stats_multi(srcs, tag):
        n = len(srcs)
        stts = [A(2 * 6 * 4, "bnst" + tag) for _ in range(n)]
        mvs = [A(2 * 4, "mv" + tag) for _ in range(n)]
        sds = [A(4, "sd" + tag) for _ in range(n)]
        rstds = [A(4, "rstd" + tag) for _ in range(n)]
        nmrs = [A(4, "nmr" + tag) for _ in range(n)]
        for i, (src_ap, src_key) in enumerate(srcs):
            sv = stts[i].ap(F32).rearrange("p (a b) -> p a b", b=6)
            for hf in range(2):
                S.add("dve", lambda h, hf=hf, sv=sv, src_ap=src_ap: h.bn_stats(out=sv[:, hf, :], in_=src_ap[:, hf * 512:(hf + 1) * 512]),
                      reads=[src_key[hf] if isinstance(src_key, list) else src_key], writes=[stts[i].k(hf)], cost=0.65)
            S.add("dve", lambda h, i=i: h.bn_aggr(out=mvs[i].ap(F32), in_=stts[i].ap(F32)),
                  reads=[stts[i].k(0), stts[i].k(1)], writes=[mvs[i].k()], cost=0.2)
        for i in range(n):
            act(sds[i].ap(F32), mvs[i].ap(F32)[:, 1:2], AF.Sqrt, [mvs[i].k(), eps_t.k()], [sds[i].k()], bias=eps_t.ap(F32)[:, 0:1])
        for i in range(n):
            S.add("dve", lambda h, i=i: h.reciprocal(out=rstds[i].ap(F32), in_=sds[i].ap(F32)), reads=[sds[i].k()], writes=[rstds[i].k()], cost=0.15)
            stt("dve", nmrs[i].ap(F32), mvs[i].ap(F32)[:, 0:1], -1.0, rstds[i].ap(F32), ALU.mult, ALU.mult,
                [mvs[i].k(), rstds[i].k()], [nmrs[i].k()])
        for b in stts + mvs + sds:
            AR.release(b)
        return list(zip(rstds, nmrs))

    for g4 in range(4):
        xn = A(4 * 1024 * 2, "xn")
        xnv = xn.ap().rearrange("p (a n) -> p a n", n=1024)
        xts = []
        for t4 in range(4):
            tti = g4 * 4 + t4
            xt = A(1024 * 4, "xt")
            dma(xt.ap(F32), x[tti * 128:(tti + 1) * 128, :], [], [xt.k()])
            xts.append(xt)
        stats = layer_norm_stats_multi([(xt.ap(F32), xt.k()) for xt in xts], "a")
        for t4 in range(4):
            xt = xts[t4]; rstd, nmr = stats[t4]
            act(xnv[:, t4, :], xt.ap(F32), AF.Identity, [xt.k(), rstd.k(), nmr.k()], [xn.k(t4)],
                bias=nmr.ap(F32)[:, 0:1], scale=rstd.ap(F32)[:, 0:1])
            AR.release(xt); AR.release(rstd); AR.release(nmr)
        for kc in range(8):
            p_, pk_ = ps_next()
            for t4 in range(4):
                mm(p_[:, t4 * 128:(t4 + 1) * 128], xnv[:, t4, kc * 128:(kc + 1) * 128], ident_bf.ap(), True, True,
                   [xn.k(t4), ident_bf.k()], [pk_])
            if UTA and kc % UTA == UTA - 1:
                act(uT[kc].ap()[:, g4 * 512:(g4 + 1) * 512], p_[:, :], AF.Identity, [pk_, sc1.k(), modT.k()], [uT[kc].k(g4)],
                    bias=modT.ap(F32)[:, kc:kc + 1], scale=sc1.ap(F32)[:, kc:kc + 1])
            else:
                ts("dve", uT[kc].ap()[:, g4 * 512:(g4 + 1) * 512], p_[:, :],
                   sc1.ap(F32)[:, kc:kc + 1], modT.ap(F32)[:, kc:kc + 1], ALU.mult, ALU.add,
                   [pk_, sc1.k(), modT.k()], [uT[kc].k(g4)])
        AR.release(xn)
    for kc in range(8):
        dump("uT%d" % kc, uT[kc].ap(), [128, S_LEN], uT[kc].k(3))

    def uT_keys(kc, tcb):
        return uT[kc].k(tcb)

    def proj_fm(wv, wkeys, col0, M, tcb, p_, pk_, kcn=8, src=None, srckeys=None):
        for kc in range(kcn):
            rhs = (uT[kc].ap() if src is None else src[kc].ap())[:, tcb * 512:(tcb + 1) * 512]
            rk = uT[kc].k(tcb) if src is None else srckeys(kc, tcb)
            mm(p_[0:M, :], wv[:, kc, col0:col0 + M], rhs, kc == 0, kc == kcn - 1, list(wkeys) + [rk], [pk_])

    w_xs, w_xs_v, w_xs_k = load_w(w_in, 0, 4, 8, "w_xs", eng="act")
    yT = [A(S_LEN * 2, "yT%d" % i) for i in range(4)]
    e3 = lambda b_: b_.ap(F32).rearrange("p (a c) -> p a c", c=256)

    def s5_gen():
      for cc in range(4):
          p0 = cc * 4
          Bre = tmp(128, "Bre"); Bim = tmp(128, "Bim"); Cre = tmp(128, "Cre"); Cim = tmp(128, "Cim")
          dma(Bre.ap(F32), Bre_d[:, p0 * 32:(p0 + 4) * 32], [], [Bre.k()])
          dma(Bim.ap(F32), Bim_d[:, p0 * 32:(p0 + 4) * 32], [], [Bim.k()])
          dma(Cre.ap(F32), Cre_d[:, p0 * 32:(p0 + 4) * 32], [], [Cre.k()])
          dma(Cim.ap(F32), Cim_d[:, p0 * 32:(p0 + 4) * 32], [], [Cim.k()])
          Bbre = tmp(128, "Bbre"); Bbim = tmp(128, "Bbim")
          sh43 = ("p (a b) -> p a b", dict(b=32))
          b3 = lambda b_: b_.ap(F32).rearrange("p (a b) -> p a b", b=32)
          crb = cr.ap(F32)[:, p0:p0 + 4].unsqueeze(2).broadcast_to([128, 4, 32])
          cib = ci.ap(F32)[:, p0:p0 + 4].unsqueeze(2).broadcast_to([128, 4, 32])
          cmul("dve", b3(Bbre), b3(Bbim), Bbre.k(), Bbim.k(), crb, cib, b3(Bre), b3(Bim),
               [cr.k(), ci.k(), Bre.k(), Bim.k()], 128, shape=sh43)
          yield
          sh8 = ("p (a i b) -> p a i b", dict(i=8, b=32)); sh9 = ("p (a i b) -> p a i b", dict(i=9, b=32))
          v8 = lambda b_: b_.ap(F32).rearrange(sh8[0], **sh8[1])
          v9 = lambda b_: b_.ap(F32).rearrange(sh9[0], **sh9[1])
          def abc(c0, n):
              return (Are3[:, p0:p0 + 4, c0:c0 + n].unsqueeze(3).broadcast_to([128, 4, n, 32]),
                      Aim3[:, p0:p0 + 4, c0:c0 + n].unsqueeze(3).broadcast_to([128, 4, n, 32]))

          def bb(bre, bim, n):
              return (b3(bre).unsqueeze(2).broadcast_to([128, 4, n, 32]), b3(bim).unsqueeze(2).broadcast_to([128, 4, n, 32]))
          Tt = A(4 * 256 * 2, "Tt"); Wt = A(4 * 4 * 128 * 2, "Wt"); Vt = A(2 * 1024 * 2, "Vt")
          Ttv = Tt.ap().rearrange("p (a n) -> p a n", n=256)
          Wtv = Wt.ap().rearrange("p (a b n) -> p a b n", b=4, n=128)
          Vtv = Vt.ap().rearrange("p (s a j b) -> p s a j b", s=2, a=4, j=8)
          WAre = tmp(1024, "WAre"); WAim = tmp(1024, "WAim")
          br_, bi_ = bb(Bbre, Bbim, 8)
          ar_, ai_ = abc(17, 8)
          cmul("dve", v8(WAre), v8(WAim), WAre.k(), WAim.k(), ar_, ai_, br_, bi_, [Are.k(), Aim.k(), Bbre.k(), Bbim.k()], 1024, shape=sh8)
          yield
          yield
          WAb = A(2 * 1024 * 2, "WAb")
          WAbv = WAb.ap().rearrange("p (s a i b) -> p s a i b", s=2, a=4, i=8)
          cp("act", WAbv[:, 0], v8(WAre), [WAre.k()], [WAb.k(0)])
          cp("pool", WAbv[:, 1], v8(WAim), [WAim.k()], [WAb.k(1)])
          AR.release(WAre); AR.release(WAim)
          for pr in range(4):
              p_, pk_ = ps_next()
              for pl in range(2):
                  for kt in range(2):
                      q = pl * 2 + kt
                      mm(p_[:, q * 128:(q + 1) * 128], WAbv[:, pl, pr, kt * 4:(kt + 1) * 4, :].rearrange("p a b -> p (a b)"), ident_bf.ap(),
                         True, True, [WAb.k(pl), ident_bf.k()], [pk_])
              cp("act", Wtv[:, pr, :, :].rearrange("p b n -> p (b n)"), p_[:, :], [pk_], [Wt.k(pr)])
          AR.release(WAb)
          yield
          BAre = tmp(1024, "BAre"); BAim = tmp(1024, "BAim")
          ar_, ai_ = abc(0, 8)
          cmul("dve", v8(BAre), v8(BAim), BAre.k(), BAim.k(), ar_, ai_, br_, bi_, [Are.k(), Aim.k(), Bbre.k(), Bbim.k()], 1024, shape=sh8)
          yield
          yield
          CAre = tmp(1152, "CAre"); CAimN = tmp(1152, "CAimN")
          ar_, ai_ = abc(8, 9); br_, bi_ = bb(Cre, Cim, 9)
          cmul("dve", v9(CAre), v9(CAimN), CAre.k(), CAimN.k(), ar_, ai_, br_, bi_, [Are.k(), Aim.k(), Cre.k(), Cim.k()], 1152, neg_im=True, shape=sh9)
          yield
          yield
          cp("pool", Vtv[:, 0], v9(CAre)[:, :, 1:9, :], [CAre.k()], [Vt.k(0)])
          cp("pool", Vtv[:, 1], v9(CAimN)[:, :, 1:9, :], [CAimN.k()], [Vt.k(1)])
          for pr in range(4):
              p_, pk_ = ps_next()
              mm(p_[:, 0:256], v8(BAre)[:, pr, 0:4, :].rearrange("p a b -> p (a b)"), v9(CAre)[:, pr, 0:8, :].rearrange("p a b -> p (a b)"),
                 True, False, [BAre.k(), CAre.k()], [pk_])
              mm(p_[:, 0:256], v8(BAim)[:, pr, 0:4, :].rearrange("p a b -> p (a b)"), v9(CAimN)[:, pr, 0:8, :].rearrange("p a b -> p (a b)"),
                 False, True, [BAim.k(), CAimN.k()], [pk_])
              tt("dve", Ttv[:, pr, :], p_[:, 0:256], tmask.ap(F32), ALU.mult, [pk_, tmask.k()], [Tt.k(pr)])
          for b in (Bre, Bim, Cre, Cim, Bbre, Bbim, BAre, BAim, CAre, CAimN):
              AR.release(b)
          yield
          Esin = tmp(1024, "Esin"); Ecos = tmp(1024, "Ecos"); ang = tmp(1024, "ang")
          e3 = lambda b_: b_.ap(F32).rearrange("p (a c) -> p a c", c=256)
          tt("dve", e3(ang), ph.ap(F32)[:, p0:p0 + 4].unsqueeze(2).broadcast_to([128, 4, 256]),
             cvals.ap(F32).unsqueeze(1).broadcast_to([128, 4, 256]), ALU.mult, [ph.k(), cvals.k()], [ang.k()])
          reduce_angle(Esin.ap(F32), Esin.k(), ang.ap(F32), ang.k(), 1024, 0.0, None)
          yield
          yield
          reduce_angle(Ecos.ap(F32), Ecos.k(), ang.ap(F32), ang.k(), 1024, math.pi / 2, None)
          act(Esin.ap(F32), Esin.ap(F32), AF.Sin, [Esin.k()], [Esin.k()])
          act(Ecos.ap(F32), Ecos.ap(F32), AF.Sin, [Ecos.k()], [Ecos.k()])
          AR.release(ang)
          yield
          yield
          xsT = A(S_LEN * 2, "xsT")
          for tcb in range(4):
              p_, pk_ = ps_next()
              proj_fm(w_xs_v, w_xs_k, cc * 128, 128, tcb, p_, pk_)
              cp("act", xsT.ap()[:, tcb * 512:(tcb + 1) * 512], p_[:, :], [pk_], [xsT.k(tcb)])
          Xs = A(4 * 2 * 256 * 2, "Xs")
          yield
          Xsv = Xs.ap().rearrange("p (a h c) -> p a h c", h=2, c=256)
          xs_all = [xsT.k(t) for t in range(4)]
          for pw in range(4):
              p_, pk_ = ps_next()
              for hf in range(2):
                  for i4 in range(4):
                      i = hf * 4 + i4
                      mm(p_[:, hf * 256:(hf + 1) * 256], SEL(pw, i4), xsT.ap()[:, i:S_LEN:8], i4 == 0, i4 == 3,
                         xs_all + [sel_bf.k()], [pk_])
              cp("dve" if pw % 2 else "act", Xsv[:, pw, :, :].rearrange("p h c -> p (h c)"), p_[:, :], [pk_], [Xs.k(pw)])
          AR.release(xsT)
          yield
          Ybf = A(4 * 2 * 256 * 2, "Ybf")
          Ybv = Ybf.ap().rearrange("p (a h c) -> p a h c", h=2, c=256)
          stg = []
          for pw in range(4):
              pz, pzk = ps_next()
              for pl in range(2):
                  for kt in range(2):
                      mm(pz[:, pl * 256:(pl + 1) * 256], Wtv[:, pw, pl * 2 + kt, :], Xsv[:, pw, kt, :], kt == 0, kt == 1,
                         [Wt.k(pw), Xs.k(pw)], [pzk])
              zre = pz[:, 0:256]; zim = pz[:, 256:512]
              ec = e3(Ecos)[:, pw, :]; es = e3(Esin)[:, pw, :]
              t1 = tmp(256, "l2a"); t2 = tmp(256, "l2b"); gr = tmp(256, "gr"); gi = tmp(256, "gi")
              t3 = tmp(256, "l2c"); t4 = tmp(256, "l2d")
              tt("dve", t1.ap(F32), zre, ec, ALU.mult, [pzk, Ecos.k()], [t1.k()])
              tt("dve", t2.ap(F32), zim, es, ALU.mult, [pzk, Esin.k()], [t2.k()])
              tt("dve", t3.ap(F32), zim, ec, ALU.mult, [pzk, Ecos.k()], [t3.k()])
              tt("dve", t4.ap(F32), zre, es, ALU.mult, [pzk, Esin.k()], [t4.k()])
              tt("dve", gr.ap(F32), t1.ap(F32), t2.ap(F32), ALU.add, [t1.k(), t2.k()], [gr.k()])
              tt("dve", gi.ap(F32), t3.ap(F32), t4.ap(F32), ALU.subtract, [t3.k(), t4.k()], [gi.k()])
              for b in (t1, t2, t3, t4):
                  AR.release(b)
              stg.append(dict(gr=gr, gi=gi, ec=ec, es=es))
          yield
          for pw in range(4):
              d = stg[pw]; pair = p0 + pw
              Gr = tmp(256, "Gr"); Gi = tmp(256, "Gi")
              r8b = r8.ap(F32)[:, pair:pair + 1].broadcast_to([128, 256])
              S.add("dve", lambda h, Gr=Gr, gr=d["gr"], r8b=r8b: h.tensor_tensor_scan(out=Gr.ap(F32), data0=r8b, data1=gr.ap(F32), initial=0.0, op0=ALU.mult, op1=ALU.add),
                    reads=[d["gr"].k(), r8.k()], writes=[Gr.k()], cost=0.65)
              S.add("dve", lambda h, Gi=Gi, gi=d["gi"], r8b=r8b: h.tensor_tensor_scan(out=Gi.ap(F32), data0=r8b, data1=gi.ap(F32), initial=0.0, op0=ALU.mult, op1=ALU.add),
                    reads=[d["gi"].k(), r8.k()], writes=[Gi.k()], cost=0.65)
              d["Gr"] = Gr; d["Gi"] = Gi
              AR.release(d["gr"]); AR.release(d["gi"])
          yield
          for pw in range(4):
              d = stg[pw]
              Gr, Gi, ec, es = d["Gr"], d["Gi"], d["ec"], d["es"]
              t1 = tmp(256, "l3a"); t2 = tmp(256, "l3b"); t3 = tmp(256, "l3c"); t4 = tmp(256, "l3d")
              Hb = A(2 * 256 * 2, "Hb")
              Hbv = Hb.ap().rearrange("p (s c) -> p s c", c=256)
              memset("pool", Hbv[:, :, 0:1], 0.0, [Hb.k()])
              tt("dve", t1.ap(F32), Gr.ap(F32), ec, ALU.mult, [Gr.k(), Ecos.k()], [t1.k()])
              tt("dve", t2.ap(F32), Gi.ap(F32), es, ALU.mult, [Gi.k(), Esin.k()], [t2.k()])
              tt("dve", Hbv[:, 0, 1:256], t1.ap(F32)[:, 0:255], t2.ap(F32)[:, 0:255], ALU.subtract, [t1.k(), t2.k()], [Hb.k()])
              tt("dve", t3.ap(F32), Gr.ap(F32), es, ALU.mult, [Gr.k(), Esin.k()], [t3.k()])
              tt("dve", t4.ap(F32), Gi.ap(F32), ec, ALU.mult, [Gi.k(), Ecos.k()], [t4.k()])
              tt("dve", Hbv[:, 1, 1:256], t3.ap(F32)[:, 0:255], t4.ap(F32)[:, 0:255], ALU.add, [t3.k(), t4.k()], [Hb.k()])
              for b in (t1, t2, t3, t4, Gr, Gi):
                  AR.release(b)
              d["Hb"] = Hb; d["Hbv"] = Hbv
          yield
          yield
          for pw in range(4):
              d = stg[pw]; pair = p0 + pw
              Hb, Hbv = d["Hb"], d["Hbv"]
              py, pyk = ps_next()
              for mt in range(2):
                  o = py[:, mt * 256:(mt + 1) * 256]
                  jl = slice(mt * 4, mt * 4 + 4)
                  if mt == 0:
                      mm(o, Ttv[:, pw, 0:128], Xsv[:, pw, 0, :], True, False, [Tt.k(pw), Xs.k(pw)], [pyk])
                  else:
                      mm(o, Ttv[:, pw, 128:256], Xsv[:, pw, 0, :], True, False, [Tt.k(pw), Xs.k(pw)], [pyk])
                      mm(o, Ttv[:, pw, 0:128], Xsv[:, pw, 1, :], False, False, [Tt.k(pw), Xs.k(pw)], [pyk])
                  mm(o, Vtv[:, 0, pw, jl, :].rearrange("p j b -> p (j b)"), Hbv[:, 0, :], False, False, [Vt.k(0), Hb.k()], [pyk])
                  mm(o, Vtv[:, 1, pw, jl, :].rearrange("p j b -> p (j b)"), Hbv[:, 1, :], False, True, [Vt.k(1), Hb.k()], [pyk])
              AR.release(Hb)
              for mt in range(2):
                  stt("dve", Ybv[:, pw, mt, :], Xsv[:, pw, mt, :], dpair.ap(F32)[:, pair:pair + 1], py[:, mt * 256:(mt + 1) * 256],
                      ALU.mult, ALU.add, [Xs.k(pw), dpair.k(), pyk], [Ybf.k((pw, mt))])
          yield
          for j in range(8):
              p_, pk_ = ps_next()
              for pw in range(4):
                  mm(p_[:, 0:256], SEL(j % 4, pw), Ybv[:, pw, j // 4, :], pw == 0, pw == 3, [sel_bf.k(), Ybf.k((pw, j // 4))], [pk_])
              act(yT[cc].ap()[:, j:S_LEN:8], p_[:, 0:256], AF.Gelu, [pk_], [yT[cc].k()])
          for b in (Tt, Wt, Vt, Esin, Ecos, Xs, Ybf):
              AR.release(b)
          yield
    s5g = s5_gen()

    w_f32 = A(64 * 4, "w_f32"); dma(w_f32.ap(F32), w_fd, [], [w_f32.k()])
    w_f = A(64 * 2, "w_f"); cp("dve", w_f.ap(), w_f32.ap(F32), [w_f32.k()], [w_f.k()])
    AR.release(w_f32)
    w_f_v = w_f.ap().rearrange("p (k n) -> p k n", n=8); w_f_k = [w_f.k()]
    bfb = A(4, "bf"); dma(bfb.ap(F32)[0:8, :], b_f, [], [bfb.k()])
    nbf = A(4, "nbf")
    ts("dve", nbf.ap(F32)[0:8, :], bfb.ap(F32)[0:8, :], -1.0, None, ALU.mult, None, [bfb.k()], [nbf.k()])
    lg = A(S_LEN * 4, "lg"); cumf = A(S_LEN * 4, "cumf")
    for tcb in range(4):
        p_, pk_ = ps_next()
        proj_fm(w_f_v, w_f_k, 0, 8, tcb, p_, pk_)
        act(lg.ap(F32)[0:8, tcb * 512:(tcb + 1) * 512], p_[0:8, :], AF.Exp, [pk_, nbf.k()], [lg.k(tcb)], bias=nbf.ap(F32)[0:8, 0:1], scale=-1.0)
    one8 = A(4, "one8"); memset("pool", one8.ap(F32)[0:8, :], 1.0, [one8.k()])
    lgk = [lg.k(t) for t in range(4)]
    act(lg.ap(F32)[0:8, :], lg.ap(F32)[0:8, :], AF.Ln, lgk + [one8.k()], lgk, bias=one8.ap(F32)[0:8, 0:1])
    S.add("dve", lambda h: h.tensor_tensor_scan(out=cumf.ap(F32)[0:8, :], data0=one8.ap(F32)[0:8, 0:1].broadcast_to([8, S_LEN]),
                                                data1=lg.ap(F32)[0:8, :], initial=0.0, op0=ALU.mult, op1=ALU.subtract),
          reads=lgk + [one8.k()], writes=[cumf.k()], cost=4.5)
    CF = A(3 * S_LEN * 2, "CF"); NCF = A(3 * S_LEN * 2, "NCF")
    CFv = CF.ap().rearrange("p (a n) -> p a n", n=S_LEN); NCFv = NCF.ap().rearrange("p (a n) -> p a n", n=S_LEN)
    r1 = A(S_LEN * 4, "r1")
    cp("dve", CFv[0:8, 0, :], cumf.ap(F32)[0:8, :], [cumf.k()], [CF.k(0)])
    tt("dve", r1.ap(F32)[0:8, :], cumf.ap(F32)[0:8, :], CFv[0:8, 0, :], ALU.subtract, [cumf.k(), CF.k(0)], [r1.k()])
    cp("dve", CFv[0:8, 1, :], r1.ap(F32)[0:8, :], [r1.k()], [CF.k(1)])
    tt("dve", lg.ap(F32)[0:8, :], r1.ap(F32)[0:8, :], CFv[0:8, 1, :], ALU.subtract, [r1.k(), CF.k(1)], lgk)
    cp("dve", CFv[0:8, 2, :], lg.ap(F32)[0:8, :], lgk, [CF.k(2)])
    ts("dve", NCF.ap()[0:8, :], CF.ap()[0:8, :], -1.0, None, ALU.mult, None, [CF.k(0), CF.k(1), CF.k(2)], [NCF.k()])
    dump("cumf", cumf.ap(F32)[0:8, :], [8, S_LEN], cumf.k())
    cf_scr = nc.dram_tensor("cf_scr", [8, 6, S_LEN], BF16, kind="Internal").ap()
    dma(cf_scr[:, 0:3, :], CFv[0:8, :, :], [CF.k(0), CF.k(1), CF.k(2)], ["cf_scr"])
    dma(cf_scr[:, 3:6, :], NCFv[0:8, :, :], [NCF.k()], ["cf_scr"])
    for b in (lg, r1, cumf, w_f, one8, bfb, nbf, CF, NCF):
        AR.release(b)
    oz = [A(S_LEN * 2, "oz%d" % i) for i in range(4)]
    ps_mode[0] = "gen"
    kt_count = [0]
    for _ in range(PRETICK):
        next(s5g, None)
    for hc in range(4):
        (w_q, w_q_v, w_q_k), (w_k, w_k_v, w_k_k), (w_za, w_za_v, w_za_k), (w_v, w_v_v, w_v_k) = hc_w
        Vaug = A(16 * 2 * 128 * 2, "Vaug")
        Vv = Vaug.ap().rearrange("p (t e n) -> p t e n", e=2, n=128)
        memset("pool", Vaug.ap(), 1.0, [Vaug.k()])
        for g4 in range(4):
            p_, pk_ = ps_next()
            for t4 in range(4):
                tti = g4 * 4 + t4
                for kc in range(8):
                    mm(p_[:, t4 * 128:(t4 + 1) * 128], uT[kc].ap()[:, tti * 128:(tti + 1) * 128], w_v_v[:, kc, :], kc == 0, kc == 7,
                       list(w_v_k) + [uT[kc].k(g4)], [pk_])
            pv = p_[:, :].rearrange("p (t n) -> p t n", n=128)
            cp("dve", Vv[:, g4 * 4:(g4 + 1) * 4, 0, 0:64], pv[:, :, 0:64], [pk_], [Vaug.k()])
            cp("dve", Vv[:, g4 * 4:(g4 + 1) * 4, 1, 64:128], pv[:, :, 64:128], [pk_], [Vaug.k()])
        QA = [A(S_LEN * 2, "QA%d" % e) for e in range(2)]
        KA = [A(S_LEN * 2, "KA%d" % e) for e in range(2)]
        for e in range(2):
            h_ = hc * 2 + e
            memset("pool", QA[e].ap()[64:70, :], 1.0, [QA[e].k("x")])
            memset("pool", KA[e].ap()[64:70, :], 1.0, [KA[e].k("x")])
            dma(QA[e].ap()[64:67, :], cf_scr[h_, 0:3, :], ["cf_scr"], [QA[e].k("x")])
            dma(KA[e].ap()[67:70, :], cf_scr[h_, 3:6, :], ["cf_scr"], [KA[e].k("x")])
        for tcb in range(4):
            p_, pk_ = ps_next()
            proj_fm(w_q_v, w_q_k, 0, 128, tcb, p_, pk_)
            for e in range(2):
                act(QA[e].ap()[0:64, tcb * 512:(tcb + 1) * 512], p_[e * 64:(e + 1) * 64, :], AF.Identity, [pk_], [QA[e].k(tcb)], scale=0.125)
            p_, pk_ = ps_next()
            proj_fm(w_k_v, w_k_k, 0, 128, tcb, p_, pk_)
            for e in range(2):
                cp("dve", KA[e].ap()[0:64, tcb * 512:(tcb + 1) * 512], p_[e * 64:(e + 1) * 64, :], [pk_], [KA[e].k(tcb)])
        if hc == 0:
            dump("QA", QA[0].ap(), [128, S_LEN], QA[0].k(3))
            dump("KA", KA[0].ap(), [128, S_LEN], KA[0].k(3))
        if hc + 1 < 4:
            hc_w = load_hc(hc + 1)
        else:
            w_g_, w_g_v, w_g_k = load_w(w_glu, 0, 8, 4, "w_glu")
            w_z, w_z_v, w_z_k = load_w(w_in, 4, 4, 8, "w_zs")
        steps = [(qc, e, kt) for qc in range(4) for e in range(2) for kt in range(4 * qc + 4)]
        qk = {}

        def issue_qk(si):
            qc, e, kt = steps[si]
            q0 = qc * 512
            d_ = kt - 4 * qc
            coff = max(0, d_) * 128
            N = 512 - coff
            p_, pk_ = ps_qk()
            mm(p_[:, 0:N], KA[e].ap()[0:70, kt * 128:(kt + 1) * 128], QA[e].ap()[0:70, q0 + coff:q0 + 512], True, d_ < 0,
               [KA[e].k("x"), KA[e].k(kt // 4), QA[e].k("x"), QA[e].k(qc)], [pk_])
            if d_ >= 0:
                mm(p_[:, 0:128], ident_bf.ap(), cmask_bf.ap(), False, True, [ident_bf.k(), cmask_bf.k()], [pk_])
            qk[si] = (p_, pk_, coff, N)
        issue_qk(0); issue_qk(1)
        osbs = {}
        po_dd = {}
        for si, (qc, e, kt) in enumerate(steps):
            q0 = qc * 512
            nkt = 4 * qc + 4
            pacc, pacck = psum[6 + e], ("ps", 6 + e)
            if (qc, 0) not in osbs and e == 0 and kt == 0:
                osbs[qc] = tmp(512, "osb")
            osb = osbs[qc]
            p_, pk_, coff, N = qk.pop(si)
            PT = A(512 * 2, "PT")
            act(PT.ap()[:, 0:N], p_[:, 0:N], AF.Exp, [pk_], [PT.k()])
            if si + 2 < len(steps):
                issue_qk(si + 2)
            mm(pacc[:, coff:512], Vv[:, kt, e, :], PT.ap()[:, 0:N], kt == 0, kt == nkt - 1, [Vaug.k(), PT.k()], [pacck])
            AR.release(PT)
            kt_count[0] += 1
            if kt_count[0] % TICK == 0:
                next(s5g, None)
            if kt == nkt - 1:
                lo, hi = (0, 64) if e == 0 else (64, 128)
                dl, dh = (64, 128) if e == 0 else (0, 64)
                if e == 0:
                    po_dd[qc] = (tmp(512, "po"), tmp(512, "dd"))
                po, dd = po_dd[qc]
                act(po.ap(F32)[lo:hi, :], pacc[lo:hi, :], AF.Identity, [pacck], [po.k(e)])
                cp("dve" if DDV else "act", dd.ap(F32)[lo:hi, :], pacc[dl:dh, :], [pacck], [dd.k(e)])
                if e == 1:
                    S.add("dve", lambda h, dd=dd: h.reciprocal(out=dd.ap(F32), in_=dd.ap(F32)),
                          reads=[dd.k(0), dd.k(1)], writes=[dd.k(0), dd.k(1)], cost=3.4)
                    tt("dve", osb.ap(F32), po.ap(F32), dd.ap(F32), ALU.mult, [po.k(0), po.k(1), dd.k(0), dd.k(1)], [osb.k(0), osb.k(1)])
                    AR.release(po); AR.release(dd)
                if e == 1:
                    pz, pzk = ps_next()
                    proj_fm(w_za_v, w_za_k, 0, 128, qc, pz, pzk)
                    sz = tmp(512, "sza")
                    act(sz.ap(F32), pz[:, :], AF.Tanh, [pzk], [sz.k()], scale=0.5)
                    stt("dve", sz.ap(F32), sz.ap(F32), 1.0, pz[:, :], ALU.add, ALU.mult, [sz.k(), pzk], [sz.k()])
                    stt("dve", oz[hc].ap()[:, q0:q0 + 512], sz.ap(F32), 0.5, osb.ap(F32), ALU.mult, ALU.mult,
                        [osb.k(0), osb.k(1), sz.k()], [oz[hc].k(qc)])
                    AR.release(osb); AR.release(sz)
        for b in QA + KA + [w_q, w_k, w_za, w_v, Vaug]:
            AR.release(b)
    for _ in s5g:
        pass
    ps_mode[0] = "all8"
    for hc in range(4):
        dump("oz%d" % hc, oz[hc].ap(), [128, S_LEN], oz[hc].k(3))
    AR.release(w_xs)
    for cc in range(4):
        dump("yT%d" % cc, yT[cc].ap(), [128, S_LEN], yT[cc].k())

    s5o = [A(S_LEN * 2, "s5o%d" % i) for i in range(4)]
    for fc in range(4):
        for tcb in range(4):
            pa, pak = ps_next(); pb, pbk = ps_next(); pz, pzk = ps_next()
            proj_fm(w_g_v, w_g_k, fc * 128, 128, tcb, pa, pak, kcn=4, src=yT, srckeys=lambda kc, t: yT[kc].k())
            proj_fm(w_g_v, w_g_k, 512 + fc * 128, 128, tcb, pb, pbk, kcn=4, src=yT, srckeys=lambda kc, t: yT[kc].k())
            proj_fm(w_z_v, w_z_k, fc * 128, 128, tcb, pz, pzk)
            sb_ = tmp(512, "sgb"); sz = tmp(512, "sz"); t_ = tmp(512, "glt")
            act(sb_.ap(F32), pb[:, :], AF.Sigmoid, [pbk], [sb_.k()])
            act(sz.ap(F32), pz[:, :], AF.Silu, [pzk], [sz.k()])
            tt("dve", t_.ap(F32), pa[:, :], sb_.ap(F32), ALU.mult, [pak, sb_.k()], [t_.k()])
            tt("pool", s5o[fc].ap()[:, tcb * 512:(tcb + 1) * 512], t_.ap(F32), sz.ap(F32), ALU.mult, [t_.k(), sz.k()], [s5o[fc].k(tcb)])
            for b in (sb_, sz, t_):
                AR.release(b)
    AR.release(w_g_); AR.release(w_z)
    for b in yT:
        AR.release(b)
    for fc in range(4):
        dump("s5o%d" % fc, s5o[fc].ap(), [128, S_LEN], s5o[fc].k(3))


    gate_row = A(1024 * 4, "gate_row")
    brow = A(1024 * 4, "brow")
    dma(brow.ap(F32)[0:1, :], b_ada_row[0:1, 2048:3072], [], [brow.k()])
    for hf in range(2):
        sg = A(8 * 512 * 4, "wada_stg")
        sv = sg.ap(F32).rearrange("p (k n) -> p k n", n=512)
        dma(sg.ap(F32).rearrange("p (j c) -> p j c", j=4),
            w_ada[2048 + hf * 512:2048 + (hf + 1) * 512, :].rearrange("(j p) c -> p j c", p=128), [], [sg.k()])
        s16 = A(8 * 512 * 2, "wada_bf")
        s16v = s16.ap().rearrange("p (k n) -> p k n", n=512)
        cp("dve" if hf % 2 == 0 else "act", s16v.rearrange("p k (j n) -> p k j n", j=4),
           sg.ap(F32).rearrange("p (j k n) -> p k j n", j=4, k=8), [sg.k()], [s16.k()])
        AR.release(sg)
        pg, pgk = ps_next()
        for kc in range(8):
            mm(pg[0:1, :], cact_bf.ap()[:, kc:kc + 1], s16v[:, kc, :], kc == 0, kc == 7, [s16.k(), cact_bf.k()], [pgk])
        AR.release(s16)
        tt("dve", gate_row.ap(F32)[0:1, hf * 512:(hf + 1) * 512], pg[0:1, :], brow.ap(F32)[0:1, hf * 512:(hf + 1) * 512], ALU.add,
           [pgk, brow.k()], [gate_row.k(hf)])
    AR.release(brow)
    ones_row = A(128 * 4, "ones_row")
    memset("pool", ones_row.ap(F32)[0:1, :], 1.0, [ones_row.k()])
    gate_b = A(1024 * 4, "gate_b")
    for hf in range(2):
        p_, pk_ = ps_next()
        mm(p_[:, :], ones_row.ap(F32)[0:1, :], gate_row.ap(F32)[0:1, hf * 512:(hf + 1) * 512], True, True,
           [ones_row.k(), gate_row.k(hf)], [pk_])
        cp("dve", gate_b.ap(F32)[:, hf * 512:(hf + 1) * 512], p_[:, :], [pk_], [gate_b.k()])
    AR.release(gate_row); AR.release(ones_row)
    lng = A(1024 * 4, "lng"); lnb = A(1024 * 4, "lnb")
    dma(lng.ap(F32), ln_g[0:1, :].broadcast_to([128, D]), [], [lng.k()])
    dma(lnb.ap(F32), ln_b[0:1, :].broadcast_to([128, D]), [], [lnb.k()])
    wo = A(8 * 1024 * 2, "w_out")
    wov = wo.ap().rearrange("p (k n) -> p k n", n=1024)
    wok = []
    for jj in range(0, 8, 4):
        sg = A(4 * 8 * 128 * 4, "w_out_stg")
        dma(sg.ap(F32).rearrange("p (j c) -> p j c", j=4),
            w_out[jj * 128:(jj + 4) * 128, :].rearrange("(j p) c -> p j c", p=128), [], [sg.k()])
        for j in range(4):
            c0 = (jj + j) * 128
            tt("dve", wov[:, :, c0:c0 + 128],
               sg.ap(F32).rearrange("p (j k n) -> p j k n", j=4, k=8)[:, j, :, :],
               gate_b.ap(F32)[:, c0:c0 + 128].unsqueeze(1).broadcast_to([128, 8, 128]), ALU.mult,
               [sg.k(), gate_b.k()], [wo.k((jj, j))])
            wok.append(wo.k((jj, j)))
        AR.release(sg)
    mg = [A(S_LEN * 2, "mg%d" % i) for i in range(8)]
    def load_fc(fc):
        return (load_w(w_ps, fc, 1, 4, "w_ps", eng="dve"), load_w(w_pa, fc, 1, 4, "w_pa", eng="dve"),
                load_w(w_in, 24 + fc, 1, 8, "w_gs", eng="dve"), load_w(w_in, 32 + fc, 1, 8, "w_ga", eng="dve"))
    fc_w = load_fc(0)
    for fc in range(8):
        (w1, w1v, w1k), (w2, w2v, w2k), (wgs, wgsv, wgsk), (wga, wgav, wgak) = fc_w
        if fc + 1 < 8 and PREFETCH_FC:
            fc_w = load_fc(fc + 1)
        for tcb in range(4):
            p1, p1k = ps_next(); p2, p2k = ps_next(); p3, p3k = ps_next(); p4, p4k = ps_next()
            proj_fm(w1v, w1k, 0, 128, tcb, p1, p1k, kcn=4, src=s5o, srckeys=lambda kc, t: s5o[kc].k(t))
            proj_fm(w2v, w2k, 0, 128, tcb, p2, p2k, kcn=4, src=oz, srckeys=lambda kc, t: oz[kc].k(t))
            proj_fm(wgsv, wgsk, 0, 128, tcb, p3, p3k)
            proj_fm(wgav, wgak, 0, 128, tcb, p4, p4k)
            s1 = tmp(512, "sg1"); s2 = tmp(512, "sg2")
            act(s1.ap(F32), p3[:, :], AF.Sigmoid, [p3k], [s1.k()])
            act(s2.ap(F32), p4[:, :], AF.Sigmoid, [p4k], [s2.k()])
            tt("dve", s1.ap(F32), p1[:, :], s1.ap(F32), ALU.mult, [p1k, s1.k()], [s1.k()])
            tt("dve", s2.ap(F32), p2[:, :], s2.ap(F32), ALU.mult, [p2k, s2.k()], [s2.k()])
            tt("pool" if tcb % 2 else "dve", mg[fc].ap()[:, tcb * 512:(tcb + 1) * 512], s1.ap(F32), s2.ap(F32), ALU.add,
               [s1.k(), s2.k()], [mg[fc].k(tcb)])
            for b in (s1, s2):
                AR.release(b)
        for b in (w1, w2, wgs, wga):
            AR.release(b)
        if fc + 1 < 8 and not PREFETCH_FC:
            fc_w = load_fc(fc + 1)
    for b in s5o + oz + uT:
        AR.release(b)
    for tcb in range(4):
        xts = []; pres = []
        for t4 in range(4):
            tti = tcb * 4 + t4
            xt = A(1024 * 4, "xt2")
            dma(xt.ap(F32), x[tti * 128:(tti + 1) * 128, :], [], [xt.k()])
            pre = A(1024 * 4, "pre")
            for hf in range(2):
                p_, pk_ = ps_next()
                for kc in range(8):
                    mm(p_[:, :], mg[kc].ap()[:, tti * 128:(tti + 1) * 128], wov[:, kc, hf * 512:(hf + 1) * 512], kc == 0, kc == 7,
                       list(wok) + [mg[kc].k(tcb)], [pk_])
                stt("dve", pre.ap(F32)[:, hf * 512:(hf + 1) * 512], xt.ap(F32)[:, hf * 512:(hf + 1) * 512], ALPHA, p_[:, :],
                    ALU.mult, ALU.add, [xt.k(), pk_], [pre.k(hf)])
            xts.append(xt); pres.append(pre)
        stats = layer_norm_stats_multi([(pre.ap(F32), [pre.k(0), pre.k(1)]) for pre in pres], "b")
        for t4 in range(4):
            xt = xts[t4]; pre = pres[t4]; rstd, nmr = stats[t4]
            act(xt.ap(F32), pre.ap(F32), AF.Identity, [pre.k(0), pre.k(1), rstd.k(), nmr.k()], [xt.k()],
                bias=nmr.ap(F32)[:, 0:1], scale=rstd.ap(F32)[:, 0:1])
        for t4 in range(4):
            xt = xts[t4]
            tt("dve" if (tcb == 3 or t4 % 2) else "pool", xt.ap(F32), xt.ap(F32), lng.ap(F32), ALU.mult, [xt.k(), lng.k()], [xt.k()])
        for t4 in range(4):
            tti = tcb * 4 + t4
            xt = xts[t4]; pre = pres[t4]; rstd, nmr = stats[t4]
            tt("dve", pre.ap(F32), xt.ap(F32), lnb.ap(F32), ALU.add, [xt.k(), lnb.k()], [pre.k(0), pre.k(1)])
            dma(out[tti * 128:(tti + 1) * 128, :], pre.ap(F32), [pre.k(0), pre.k(1)], [])
            for b in (xt, pre, rstd, nmr):
                AR.release(b)

    S.emit(nc)
    st.close()
    return nc, dbg_out, AR.peak


def _host_consts():
    ident = np.eye(128, dtype=np.float32)
    sel = np.zeros((128, 16, 128), np.float32)
    for a in range(4):
        for b in range(4):
            for r in range(32):
                sel[a * 32 + r, a * 4 + b, b * 32 + r] = 1.0
    kk = np.arange(128)
    cmask = np.where(kk[None, :] >= kk[:, None], 0.0, -30000.0).astype(np.float32)
    tmask = np.zeros((128, 256), np.float32)
    for i4 in range(4):
        for j in range(8):
            if j >= i4:
                tmask[i4 * 32:(i4 + 1) * 32, j * 32:(j + 1) * 32] = 1.0
    kv = np.array([0, -1, -2, -3, -4, -5, -6, -7] + list(range(9)) + [7, 6, 5, 4, 3, 2, 1, 0], np.float32)
    kvals = np.tile(kv[None, :], (128, 1))
    cvals = np.tile(np.arange(256, dtype=np.float32)[None, :], (128, 1))
    return dict(ident=ident, sel=sel.reshape(128, 16 * 128), cmask=cmask, tmask=tmask, kvals=kvals, cvals=cvals)


def _pair_layout_vec(v):
    return np.ascontiguousarray(v.reshape(16, 2, 64).transpose(1, 2, 0).reshape(128, 16))


def _blk(m):
    o = np.zeros((2, 64, 16, 2, 16), np.float32)
    mm_ = m.reshape(16, 2, 64, 16)
    for gg in range(2):
        o[gg, :, :, gg, :] = mm_[:, gg].transpose(1, 0, 2)
    return np.ascontiguousarray(o.reshape(128, 16 * 32))


def _make_in_maps(inp):
    f = lambda a: np.ascontiguousarray(np.asarray(a, dtype=np.float32))
    consts = _host_consts()
    shared = dict(consts)
    def blockify(W):
        K, N = W.shape
        return np.ascontiguousarray(W.reshape(K // 128, 128, N // 128, 128).transpose(2, 1, 0, 3).reshape(N, K))
    w_in_full = f(inp["w_in"][0])
    segs = [(0, 512), (512, 1024), (1024, 1536), (1536, 2048), (2048, 2560), (2568, 3080), (3080, 4104), (4104, 5128)]
    shared["w_in"] = np.concatenate([blockify(w_in_full[:, a:b]) for a, b in segs], axis=0)
    shared["w_f"] = np.ascontiguousarray(w_in_full[:, 2560:2568].reshape(8, 128, 8).transpose(1, 0, 2).reshape(128, 64))
    shared["w_ada"] = blockify(f(inp["w_ada"][0]))
    shared["b_adaT"] = f(inp["b_ada"][0].reshape(24, 128).T)
    shared["b_ada_row"] = f(inp["b_ada"][0].reshape(1, 3 * D))
    shared["b_f"] = f(inp["b_f"][0].reshape(8, 1))
    shared["lam_re"] = _pair_layout_vec(f(inp["lam_re"][0]))
    shared["lam_im"] = _pair_layout_vec(f(inp["lam_im"][0]))
    shared["ldt"] = _pair_layout_vec(np.repeat(f(inp["log_dt"][0])[:, None], 64, axis=1))
    shared["Bre"] = _blk(f(inp["ssm_b_re"][0])); shared["Bim"] = _blk(f(inp["ssm_b_im"][0]))
    shared["Cre"] = _blk(f(inp["ssm_c_re"][0]).transpose(0, 2, 1)); shared["Cim"] = _blk(f(inp["ssm_c_im"][0]).transpose(0, 2, 1))
    dvec = f(inp["ssm_d"][0]).reshape(16, 32)
    shared["dpair"] = np.ascontiguousarray(np.tile(dvec.T, (4, 1)))
    shared["w_glu"] = blockify(f(inp["w_glu"][0])); shared["w_ps"] = blockify(f(inp["w_proj_ssm"][0]))
    shared["w_pa"] = blockify(f(inp["w_proj_attn"][0]))
    shared["w_out"] = blockify(f(inp["w_out"][0])); shared["ln_g"] = f(inp["ln_g"][0].reshape(1, D)); shared["ln_b"] = f(inp["ln_b"][0].reshape(1, D))
    maps = []
    xs = f(inp["x"]); cs = f(inp["c"])
    for b in range(8):
        m = dict(shared)
        m["x"] = xs[b]
        m["cT"] = np.ascontiguousarray(cs[b].reshape(8, 128).T)
        maps.append(m)
    return maps


_CACHE = {}


def kernel(**inputs):
    if "nc" not in _CACHE:
        _CACHE["nc"] = build()[0]
    nc = _CACHE["nc"]
    maps = _make_in_maps(inputs)
    res = run_bass_kernel_spmd(nc, maps, core_ids=list(range(8)))
    return np.stack([np.asarray(r["out"], dtype=np.float32) for r in res.results], axis=0)
```
